# Optimizing a Trainium2 kernel written in Bass

```python
import jax, jax.numpy as jnp
from jax import lax
import numpy as np

D_MODEL = 1024
BATCH = 16
SEQ = 2048
DEPTH = 1
DEC_BATCH = 2
DEC_SEQ = 16384
PAST_LEN = 128

GRID_W = 64
WIN_ROWS = 8
WIN_COLS = 16
NA_HEADS = 8
NA_HEAD_DIM = D_MODEL // 16
NA_WIDTH = NA_HEADS * NA_HEAD_DIM
F_GROUPS = 8
F_GROUP_DIM = D_MODEL // 16
F_WIDTH = F_GROUPS * F_GROUP_DIM
MIX_WIDTH = NA_WIDTH + F_WIDTH
IN_WIDTH = 4 * NA_WIDTH + 2 * F_WIDTH
Q_BLOCK_COLS = 16
N_COL_BLOCKS = GRID_W // Q_BLOCK_COLS
K_SLAB_COLS = 2 * Q_BLOCK_COLS
RMS_EPS = 1e-6
NEG_INF = -1e30

kernel_name = 'hybrid_na_fourier_encoder'


def rms_norm(x, g):
    xf = x.astype(jnp.float32)
    inv = lax.rsqrt(jnp.mean(xf * xf, axis=-1, keepdims=True) + RMS_EPS)
    return (xf * inv * g.astype(jnp.float32)).astype(x.dtype)


def _column_geometry():
    q_cols = np.arange(GRID_W).reshape(N_COL_BLOCKS, Q_BLOCK_COLS)
    win_start = np.clip(q_cols - WIN_COLS // 2, 0, GRID_W - WIN_COLS)
    slab_start = np.clip(np.arange(N_COL_BLOCKS) * Q_BLOCK_COLS - WIN_COLS // 2,
                         0, GRID_W - K_SLAB_COLS)
    slab_cols = slab_start[:, None] + np.arange(K_SLAB_COLS)[None, :]
    kc = slab_cols[:, None, :]
    valid = (kc >= win_start[:, :, None]) & (kc < win_start[:, :, None] + WIN_COLS)
    col_off = np.clip(kc - q_cols[:, :, None] + WIN_COLS - 1, 0, 2 * WIN_COLS - 2)
    return slab_cols, valid, col_off


def neighbourhood_attention(q, k, v, rpb):
    b, s, _ = q.shape
    rows = s // GRID_W
    kr = min(WIN_ROWS, rows)
    slab_cols, valid, col_off = _column_geometry()
    to_grid = lambda t: t.reshape(b, rows, GRID_W, NA_HEADS, NA_HEAD_DIM)
    qg, kg, vg = to_grid(q), to_grid(k), to_grid(v)
    bias_table = rpb.astype(jnp.float32)
    scale = NA_HEAD_DIM ** -0.5

    def one_row(r):
        rs = jnp.clip(r - kr // 2, 0, rows - kr)
        q_row = lax.dynamic_index_in_dim(qg, r, axis=1, keepdims=False)
        q_blk = q_row.reshape(b, N_COL_BLOCKS, Q_BLOCK_COLS, NA_HEADS, NA_HEAD_DIM)
        k_slab = lax.dynamic_slice_in_dim(kg, rs, kr, axis=1)[:, :, slab_cols]
        v_slab = lax.dynamic_slice_in_dim(vg, rs, kr, axis=1)[:, :, slab_cols]
        logits = jnp.einsum('bjqhd,bijkhd->bhjqik', q_blk, k_slab).astype(jnp.float32) * scale
        row_off = rs + jnp.arange(kr) - r + WIN_ROWS - 1
        bias = bias_table[:, row_off[None, None, :, None], col_off[:, :, None, :]]
        logits = jnp.where(valid[:, :, None, :], logits + bias, NEG_INF)
        shp = logits.shape
        p = jax.nn.softmax(logits.reshape(shp[:4] + (kr * K_SLAB_COLS,)), axis=-1).reshape(shp)
        out = jnp.einsum('bhjqik,bijkhd->bjqhd', p.astype(v.dtype), v_slab)
        return out.reshape(b, GRID_W, NA_WIDTH)

    out = lax.map(one_row, jnp.arange(rows))
    return jnp.swapaxes(out, 0, 1).reshape(b, s, NA_WIDTH)


def fourier_mix(u, w_f, b_f):
    b, s, _ = u.shape
    ug = u.astype(jnp.float32).reshape(b, s, F_GROUPS, F_GROUP_DIM)
    spec = jnp.fft.fft2(ug, axes=(1, 3), norm='ortho')
    mixed = spec.real.reshape(b, s, F_WIDTH).astype(u.dtype)
    return mixed @ w_f + b_f


def encoder_layer(x, w_in, rpb, w_fourier, b_fourier, g_pre, g_na, g_f, w_out, g_post):
    h = rms_norm(x, g_pre)
    proj = h @ w_in
    q, k, v, z_a, u_f, z_f = jnp.split(
        proj, [NA_WIDTH, 2 * NA_WIDTH, 3 * NA_WIDTH, 4 * NA_WIDTH, 4 * NA_WIDTH + F_WIDTH], axis=-1)
    y_a = neighbourhood_attention(q, k, v, rpb)
    y_f = fourier_mix(u_f, w_fourier, b_fourier)
    mixed = jnp.concatenate([rms_norm(y_a, g_na) * jax.nn.silu(z_a),
                             rms_norm(y_f, g_f) * jax.nn.silu(z_f)], axis=-1)
    out = mixed @ w_out
    return x + rms_norm(out, g_post)


def setup_inputs(seed: int = 0) -> dict:
    key = jax.random.key(seed)
    ks = jax.random.split(key, 12)
    f32 = jnp.float32
    nrm = lambda k, shp: jax.random.normal(k, shp, f32)
    return {
        'x_prompt': nrm(ks[0], (BATCH, SEQ, D_MODEL)),
        'x_sample': nrm(ks[1], (DEC_BATCH, DEC_SEQ, D_MODEL)),
        'w_in': nrm(ks[2], (DEPTH, D_MODEL, IN_WIDTH)) * D_MODEL ** -0.5,
        'rpb': nrm(ks[3], (DEPTH, NA_HEADS, 2 * WIN_ROWS - 1, 2 * WIN_COLS - 1)) * 0.5,
        'w_fourier': nrm(ks[4], (DEPTH, F_WIDTH, F_WIDTH)) * F_WIDTH ** -0.5,
        'b_fourier': nrm(ks[5], (DEPTH, F_WIDTH)) * 0.01,
        'g_pre': 1.0 + 0.1 * nrm(ks[6], (DEPTH, D_MODEL)),
        'g_na': 1.0 + 0.1 * nrm(ks[7], (DEPTH, NA_WIDTH)),
        'g_f': 1.0 + 0.1 * nrm(ks[8], (DEPTH, F_WIDTH)),
        'w_out': nrm(ks[9], (DEPTH, MIX_WIDTH, D_MODEL)) * MIX_WIDTH ** -0.5,
        'g_post': 1.0 + 0.1 * nrm(ks[10], (DEPTH, D_MODEL)),
    }


def reference(x_prompt, x_sample, w_in, rpb, w_fourier, b_fourier, g_pre, g_na, g_f, w_out, g_post):
    y_prompt = x_prompt
    y_sample = x_sample
    for l in range(DEPTH):
        y_prompt = encoder_layer(y_prompt, w_in[l], rpb[l], w_fourier[l], b_fourier[l],
                                 g_pre[l], g_na[l], g_f[l], w_out[l], g_post[l])
        y_sample = encoder_layer(y_sample, w_in[l], rpb[l], w_fourier[l], b_fourier[l],
                                 g_pre[l], g_na[l], g_f[l], w_out[l], g_post[l])
    return (y_prompt, y_sample)
```

```python
import contextlib
import numpy as np
import ml_dtypes
import concourse.bass as bass
import concourse.mybir as mybir
from concourse.bass_utils import run_bass_kernel_spmd

F32 = mybir.dt.float32
BF16 = mybir.dt.bfloat16
U8 = mybir.dt.uint8
ALU = mybir.AluOpType
AF = mybir.ActivationFunctionType

D = 1024
EPS = 1e-6
NCORES = 8
SP_LEN = 2048
SS_LEN = 16384
OWN_S = 4096
HALO = 2


class Buf:
    __slots__ = ("name", "w", "r")

    def __init__(self, name=""):
        self.name = name
        self.w = None
        self.r = []


class Sched:
    ENGS = ("pe", "act", "dve", "pool", "sp")

    def __init__(self, nc):
        self.nc = nc
        self.ops = {e: [] for e in self.ENGS}
        self.cnt = {}
        self.seen = {e: {} for e in self.ENGS}
        self.sems = {}

    def sem(self, name):
        if name not in self.sems:
            self.sems[name] = None
            self.cnt[name] = 0
        return name

    def _deps(self, eng, reads, writes):
        deps = {}
        for b in reads:
            if b.w is not None:
                s, v = b.w
                deps[s] = max(deps.get(s, 0), v)
        for b in writes:
            for (s, v) in b.r:
                deps[s] = max(deps.get(s, 0), v)
            if b.w is not None:
                s, v = b.w
                deps[s] = max(deps.get(s, 0), v)
        waits = []
        seen = self.seen[eng]
        for s, v in deps.items():
            if eng == "pe" and s == "pe":
                continue
            if seen.get(s, 0) >= v:
                continue
            seen[s] = v
            waits.append((s, v))
        return waits

    def _commit(self, tok, reads, writes):
        for b in writes:
            b.w = tok
            b.r = []
        for b in reads:
            b.r.append(tok)

    def op(self, eng, fns, reads=(), writes=()):
        if callable(fns):
            fns = [fns]
        waits = self._deps(eng, reads, writes)
        s = self.sem(eng)
        self.cnt[s] += 1
        tok = (s, self.cnt[s])
        self.ops[eng].append((waits, fns, (s, 1)))
        self._commit(tok, reads, writes)
        return tok

    def dma(self, q, fn, semname, reads=(), writes=()):
        if semname == "ld0":
            self._uniq = getattr(self, "_uniq", 0) + 1
            semname = "ld0_%d" % self._uniq
        waits = self._deps(q, reads, writes)
        s = self.sem(semname)
        self.cnt[s] += 16
        tok = (s, self.cnt[s])
        self.ops[q].append((waits, [fn], (s, 16)))
        self._commit(tok, reads, writes)
        return tok

    def coll(self, fn, semname, reads=(), writes=()):
        waits = self._deps("pool", reads, writes)
        s = self.sem(semname)
        self.cnt[s] += 1
        tok = (s, self.cnt[s])
        self.ops["pool"].append((waits, [fn], (s, 1)))
        self._commit(tok, reads, writes)
        return tok

    def barrier(self, skip=()):
        toks = [(s, v) for s, v in self.cnt.items() if v > 0 and not any(s.startswith(p) for p in skip)]
        for e in self.ENGS:
            waits = []
            for (s, v) in toks:
                if s == e:
                    continue
                if self.seen[e].get(s, 0) < v:
                    self.seen[e][s] = v
                    waits.append((s, v))
            if waits:
                self.ops[e].append((waits, [], None))

    def emit(self):
        nc = self.nc
        with contextlib.ExitStack() as st:
            for name in self.sems:
                self.sems[name] = st.enter_context(nc.semaphore("s_" + name))
            block = st.enter_context(nc.Block())
            sems = self.sems

            def run(e, lst):
                for waits, fns, inc in lst:
                    for (s, v) in waits:
                        e.wait_ge(sems[s], v)
                    last = None
                    for f in fns:
                        last = f(e)
                    if inc is not None:
                        last.then_inc(sems[inc[0]], inc[1])

            block.tensor(lambda e: run(e, self.ops["pe"]))
            block.scalar(lambda e: run(e, self.ops["act"]))
            block.vector(lambda e: run(e, self.ops["dve"]))
            block.gpsimd(lambda e: run(e, self.ops["pool"]))
            block.sync(lambda e: run(e, self.ops["sp"]))


class Arena:
    def __init__(self, nc, nbytes):
        self.t = nc.alloc_sbuf_tensor("arena", [128, nbytes], U8)
        self.n = nbytes
        self.off = 0

    def alloc(self, shape, dt):
        nb = 2 if dt == BF16 else 4
        n = int(np.prod(shape[1:])) * nb
        off = (self.off + 63) // 64 * 64
        assert off + n <= self.n, ("SBUF arena overflow", off, n, self.n)
        self.off = off + n
        ap = self.t[:, off:off + n].bitcast(dt)
        if len(shape) == 3:
            ap = ap.rearrange("p (a b) -> p a b", a=shape[1])
        elif len(shape) == 4:
            ap = ap.rearrange("p (a b c) -> p a b c", a=shape[1], b=shape[2])
        return ap


def _valid(qrow, krow, qcol, kcol, rows_total):
    rs = min(max(qrow - 4, 0), rows_total - 8)
    ws = min(max(qcol - 8, 0), 48)
    return (0 <= krow < rows_total) and (rs <= krow < rs + 8) and (ws <= kcol < ws + 16)


def _mask(qtile_global, chunks, rows_total):
    m = np.zeros((128, len(chunks), 128), np.float32)
    kr2 = np.arange(128) // 64
    kc = np.arange(128) % 64
    for ci, c in enumerate(chunks):
        for q in range(128):
            qrow = 2 * qtile_global + q // 64
            qcol = q % 64
            rs = min(max(qrow - 4, 0), rows_total - 8)
            ws = min(max(qcol - 8, 0), 48)
            krow = 2 * (qtile_global + c) + kr2
            ok = (krow >= 0) & (krow < rows_total) & (krow >= rs) & (krow < rs + 8) & (kc >= ws) & (kc < ws + 16)
            m[:, ci, q] = ok
    return m


NORMAL_CH = [-2, -1, 0, 1, 2]
SPECIAL = {
    ("p", 0): [0, 1, 2, 3], ("p", 1): [-1, 0, 1, 2], ("p", 14): [-2, -1, 0, 1], ("p", 15): [-3, -2, -1, 0],
    ("s", 0): [-2, -1, 0, 1, 2, 3], ("s", 1): [-2, -1, 0, 1, 2],
    ("s", 30): [-2, -1, 0, 1, 2], ("s", 31): [-3, -2, -1, 0, 1, 2],
}
SPECIAL_KEYS = list(SPECIAL.keys())
MSLOT = {}
_o = 5
for _k in SPECIAL_KEYS:
    MSLOT[_k] = _o
    _o += len(SPECIAL[_k])
NMASK = _o


def _host_tables(core):
    j = core % 4
    masks = np.zeros((128, NMASK, 128), np.float32)
    masks[:, 0:5, :] = _mask(8, NORMAL_CH, 32)
    for k in SPECIAL_KEYS:
        ch = SPECIAL[k]
        if k[0] == "p":
            m = _mask(k[1], ch, 32)
        else:
            m = _mask(32 * j + k[1], ch, 256)
        masks[:, MSLOT[k]:MSLOT[k] + len(ch), :] = m
    b = np.arange(128)[:, None, None]
    a = np.arange(128)[:, None]
    k1 = np.arange(128)[None, :]
    ang = 2 * np.pi * ((a * k1) % 128) / 128
    cs = np.stack([np.cos(ang), -np.sin(ang)], axis=1) / np.sqrt(SS_LEN)
    cs_s = cs.reshape(128, 2, 2, 64).transpose(0, 2, 1, 3)
    a16 = np.arange(16)[:, None]
    k16 = np.arange(16)[None, :]
    ang = 2 * np.pi * ((a16 * k16) % 16) / 16
    cs_p = np.zeros((128, 2, 16), np.float64)
    cs_p[0:16] = np.stack([np.cos(ang), -np.sin(ang)], axis=1) / np.sqrt(SP_LEN)
    k1s = np.arange(128)[None, :, None]
    k2l = np.arange(32)[None, None, :]
    k = k1s + 128 * (32 * j + k2l)
    th = 2 * np.pi * ((k * b) % SS_LEN) / SS_LEN
    xs = np.stack([np.sin(th), np.cos(th), -np.sin(th)], axis=2)
    k1p = np.arange(16)[None, :, None]
    k2p = np.arange(128)[None, None, :]
    k = k1p + 16 * k2p
    th = 2 * np.pi * ((k * b) % SP_LEN) / SP_LEN
    xp = np.stack([np.sin(th), np.cos(th), -np.sin(th)], axis=2)
    l = np.arange(64)[:, None]
    c = np.arange(64)[None, :]
    th = 2 * np.pi * ((l * c) % 64) / 64
    bd = np.zeros((128, 2, 128), np.float64)
    for g in range(2):
        bd[64 * g:64 * g + 64, 0, 64 * g:64 * g + 64] = np.cos(th) / 8
        bd[64 * g:64 * g + 64, 1, 64 * g:64 * g + 64] = np.sin(th) / 8
    bf = ml_dtypes.bfloat16
    return dict(
        masks=masks.astype(bf), cs_s=np.ascontiguousarray(cs_s).astype(bf), cs_p=cs_p.astype(bf),
        x_s=np.ascontiguousarray(xs).astype(bf), x_p=np.ascontiguousarray(xp).astype(bf),
        bd=bd.astype(bf), ident=np.eye(128, dtype=np.float32).astype(bf),
    )


def _bias_index():
    key = np.arange(128)[:, None, None]
    c7 = np.arange(7)[None, :, None] - 3
    q = np.arange(128)[None, None, :]
    dr = 2 * c7 + key // 64 - q // 64
    ri = np.clip(dr + 7, 0, 14)
    ci = np.clip(key % 64 - q % 64 + 15, 0, 30)
    return np.broadcast_to(ri, (128, 7, 128)), np.broadcast_to(ci, (128, 7, 128))


def build_program(dbg=None):
    dbg = dbg or {}
    nc = bass.Bass("TRN2", target_bir_lowering=False)
    S = Sched(nc)

    def din(name, shape, dt=F32):
        return nc.dram_tensor(name, list(shape), dt, kind="ExternalInput").ap()

    xp = din("xp", [2 * SP_LEN, D])
    xh = din("xh", [(32 + 2 * HALO) * 128, D])
    w_in = din("w_in", [D, 3072])
    w_out = din("w_out", [D, D])
    w_f = din("w_f", [512, 512])
    bfb = din("bfb", [128, 512])
    gpre = din("gpre", [128, 8])
    gcat = din("gcat", [128, 8])
    gpost = din("gpost", [128, D])
    biasB = din("biasB", [128, 7, 8, 128])
    masks_d = din("masks", [128, NMASK, 128], BF16)
    cs_s_d = din("cs_s", [128, 2, 2, 64], BF16)
    cs_p_d = din("cs_p", [128, 2, 16], BF16)
    x_s_d = din("x_s", [128, 128, 3, 32], BF16)
    x_p_d = din("x_p", [128, 16, 3, 128], BF16)
    bd_d = din("bd", [128, 2, 128], BF16)
    ident_d = din("ident", [128, 128], BF16)
    yp = nc.dram_tensor("yp", [2 * SP_LEN, D], F32, kind="ExternalOutput").ap()
    ys = nc.dram_tensor("ys", [OWN_S, D], F32, kind="ExternalOutput").ap()
    skind = "ExternalOutput" if dbg else "Internal"
    u_p = nc.dram_tensor("u_p", [2, 4, SP_LEN, 128], BF16, kind=skind).ap()
    u_loc2 = [nc.dram_tensor("u_loc%d" % c_, [OWN_S, 128], BF16, kind="Internal").ap() for c_ in range(4)]
    u_s2 = [nc.dram_tensor("u_s%d" % c_, [SS_LEN, 128], BF16, kind="Internal").ap() for c_ in range(4)]
    yf_all = nc.dram_tensor("yf_all", [8192, 512], F32, kind=skind).ap()

    AR = Arena(nc, 188 * 1024)
    ident = AR.alloc([128, 128], BF16)
    gpre_t = AR.alloc([128, 8], F32)
    gcat_t = AR.alloc([128, 8], F32)
    rstd_f = AR.alloc([128, 64], F32)
    ssf = AR.alloc([128, 64], F32)
    sqf = AR.alloc([128, 64], F32)
    b_ident, b_g, b_ssf, b_rstdf = Buf(), Buf(), [Buf() for _ in range(64)], Buf()
    S.dma("sp", lambda e: e.dma_start(out=ident, in_=ident_d), "ld0", writes=[b_ident])
    S.dma("sp", lambda e: e.dma_start(out=gpre_t, in_=gpre), "ld0", writes=[b_g])
    S.dma("sp", lambda e: e.dma_start(out=gcat_t, in_=gcat), "ld0", writes=[b_g])
    base_mark = AR.off

    PS = nc.alloc_psum_tensor("ps", [128, 4096], F32)
    def bank(i, n=1):
        return PS[:, i * 512:(i + n) * 512]
    pT = [bank(0).bitcast(BF16).rearrange("p (a b) -> p a b", a=8), bank(6).bitcast(BF16).rearrange("p (a b) -> p a b", a=8)]
    pA = bank(1, 2)
    pS = bank(3, 4)
    pOv = [bank(6), bank(7)]
    b_pT = [Buf(), Buf()]
    b_pA = [Buf(), Buf()]
    b_pSb = [Buf(), Buf()]
    b_pOv = [b_pT[1], Buf()]
    gst = [0]

    wst = AR.alloc([128, 8, 512], F32)
    W_uf = AR.alloc([128, 8, 512], BF16)
    b_wst, b_Wuf = Buf(), Buf()
    S.dma("sp", lambda e: e.dma_start(out=wst, in_=w_in[:, 2048:2560].rearrange("(c p) n -> p c n", p=128)), "ld0", writes=[b_wst])
    for kc in range(8):
        S.op("dve", lambda e, kc=kc: e.tensor_scalar(out=W_uf[:, kc, :], in0=wst[:, kc, :], scalar1=gpre_t[:, kc:kc + 1], scalar2=None, op0=ALU.mult),
             reads=[b_wst, b_g], writes=[b_Wuf])
    NXQ = 3
    xq = [AR.alloc([128, 4, D], F32) for _ in range(NXQ)]
    b_xq = [Buf() for _ in range(NXQ)]
    stA = [AR.alloc([128, 4], F32) for _ in range(NXQ)]
    sqA = [AR.alloc([128, 4], F32) for _ in range(NXQ)]
    rsA = [AR.alloc([128, 4], F32) for _ in range(NXQ)]
    b_stA = [[Buf() for _ in range(4)] for _ in range(NXQ)]
    b_sqA, b_rsA = [Buf() for _ in range(NXQ)], [Buf() for _ in range(NXQ)]
    junk = AR.alloc([128, D], BF16)
    b_junk = Buf()
    hb = [AR.alloc([128, D], BF16) for _ in range(2)]
    b_hb = [Buf(), Buf()]
    hT = [AR.alloc([128, 8, 128], BF16) for _ in range(2)]
    b_hT = [Buf(), Buf()]
    ub = [AR.alloc([128, 4, 128], BF16) for _ in range(2)]
    b_ub = [Buf(), Buf()]
    b_up = [[[] for _ in range(4)] for _ in range(2)]
    b_us = [[Buf()] for _ in range(4)]
    b_uloc = [[] for _ in range(4)]

    quads = []
    for q in range(8):
        r0 = HALO * 128 + q * 512
        quads.append((xh[r0:r0 + 512, :], None, q * 512, b_uloc))
    for sq_ in range(2):
        for q in range(4):
            quads.append((xp[sq_ * SP_LEN + q * 512: sq_ * SP_LEN + (q + 1) * 512, :], u_p[sq_], q * 512, b_up[sq_]))
    if "quads" in dbg:
        quads = [quads[i] for i in dbg["quads"]]
    ntA = 4 * len(quads)

    def load_quad(qi):
        src = quads[qi][0]
        sl = qi % NXQ
        S.dma("sp", lambda e: e.dma_start(out=xq[sl], in_=src.rearrange("(t p) d -> p t d", p=128)), "ldxq%d" % sl, writes=[b_xq[sl]])

    def stats_quad(qi):
        sl = qi % NXQ
        for i in range(4):
            S.op("act", lambda e, i=i: e.activation(out=junk, in_=xq[sl][:, i, :], func=AF.Square, scale=1.0 / 32, accum_out=stA[sl][:, i:i + 1]),
                 reads=[b_xq[sl]], writes=[b_junk, b_stA[sl][i]])
        S.op("act", lambda e: e.activation(out=sqA[sl], in_=stA[sl], func=AF.Sqrt, bias=EPS, scale=1.0), reads=b_stA[sl], writes=[b_sqA[sl]])
        S.op("dve", lambda e: e.reciprocal(out=rsA[sl], in_=sqA[sl]), reads=[b_sqA[sl]], writes=[b_rsA[sl]])

    def a_s1(it):
        qi, i, p = it // 4, it % 4, it % 2
        sl = qi % NXQ
        S.op("dve", lambda e: e.tensor_scalar(out=hb[p], in0=xq[sl][:, i, :], scalar1=rsA[sl][:, i:i + 1], scalar2=None, op0=ALU.mult),
             reads=[b_xq[sl], b_rsA[sl]], writes=[b_hb[p]])

    def a_s2(it):
        p = it % 2
        S.op("pe", [(lambda e, c=c: e.transpose(out=pT[p][:, c, :], in_=hb[p][:, c * 128:(c + 1) * 128], identity=ident)) for c in range(8)],
             reads=[b_hb[p], b_ident], writes=[b_pT[p]])
        S.op("act", lambda e: e.activation(out=hT[p], in_=pT[p], func=AF.Copy), reads=[b_pT[p]], writes=[b_hT[p]])

    def a_s3(it):
        qi, i, p = it // 4, it % 4, it % 2
        _, udst, tok0, b_ud = quads[qi]
        S.op("pe", [(lambda e, c=c: e.matmul(out=pA[:, p * 512:(p + 1) * 512], lhsT=hT[p][:, c, :], rhs=W_uf[:, c, :], start=(c == 0), stop=(c == 7))) for c in range(8)],
             reads=[b_hT[p], b_Wuf], writes=[b_pA[p]])
        S.op("dve", lambda e: e.tensor_copy(out=ub[p].rearrange("p a b -> p (a b)"), in_=pA[:, p * 512:(p + 1) * 512]), reads=[b_pA[p]], writes=[b_ub[p]])
        t0 = tok0 + i * 128
        if udst is None:
            for c_ in range(4):
                nb_ = Buf()
                b_ud[c_].append(nb_)
                S.dma("sp", lambda e, c_=c_: e.dma_start(out=u_loc2[c_][t0:t0 + 128, :], in_=ub[p][:, c_, :]), "stub%d_%d" % (p, c_),
                      reads=[b_ub[p]], writes=[nb_])
        else:
            nb_ = Buf()
            for c_ in range(4):
                b_ud[c_].append(nb_)
            S.dma("sp", lambda e: e.dma_start(out=udst[:, t0:t0 + 128, :].rearrange("c p k -> p c k"), in_=ub[p]), "stub%d" % p,
                  reads=[b_ub[p]], writes=[nb_])

    for qi in range(min(2, len(quads))):
        load_quad(qi)
    for it in range(-2, ntA):
        i1 = it + 2
        if 0 <= i1 < ntA:
            if i1 % 4 == 0:
                qi = i1 // 4
                if qi + 2 < len(quads):
                    load_quad(qi + 2)
                stats_quad(qi)
            a_s1(i1)
        if 0 <= it + 1 < ntA:
            a_s2(it + 1)
        if 0 <= it:
            a_s3(it)
            if it == 31 and "quads" not in dbg:
                for c_ in range(4):
                    S.coll(lambda e, c_=c_: e.collective_compute("AllGather", ALU.bypass, replica_groups=[[0, 1, 2, 3], [4, 5, 6, 7]], ins=[u_loc2[c_]], outs=[u_s2[c_]]),
                           "ccag%d" % c_, reads=b_uloc[c_], writes=b_us[c_])

    S.barrier(skip=("ccag",))
    AR.off = base_mark
    if dbg.get("stop") == "A":
        S.emit()
        return nc

    bf_t = AR.alloc([128, 512], F32)
    bd_t = AR.alloc([128, 2, 128], BF16)
    wfb = AR.alloc([128, 4, 512], BF16)
    Wcs = AR.alloc([128, 4, 2, 512], BF16)
    cs_s = AR.alloc([128, 2, 2, 64], BF16)
    cs_p = AR.alloc([128, 2, 16], BF16)
    Xt = AR.alloc([128, 12288], BF16)
    x_s = Xt.rearrange("p (a b c) -> p a b c", a=128, b=3)
    x_p = Xt[:, 0:6144].rearrange("p (a b c) -> p a b c", a=16, b=3)
    Ut = AR.alloc([128, 128, 128], BF16)
    Tt = AR.alloc([128, 2, 64, 128], BF16)
    ZT = AR.alloc([128, 4, 2, 4096], BF16)
    wfs = ZT.rearrange("p c r t -> p (c r t)")[:, 0:4096].bitcast(F32).rearrange("p (c n) -> p c n", c=4)
    yft = [AR.alloc([128, 512], F32) for _ in range(2)]
    b_U, b_T, b_ZT = Buf(), Buf(), Buf()
    b_yft = [Buf(), Buf()]
    b_yf = [Buf() for _ in range(64)]
    b_tab, b_X, b_wfb, b_Wcs = Buf(), Buf(), Buf(), Buf()
    b_wfs = b_ZT
    for dst, srcd in ((bf_t, bfb), (bd_t, bd_d), (cs_s, cs_s_d), (cs_p, cs_p_d)):
        S.dma("sp", lambda e, dst=dst, srcd=srcd: e.dma_start(out=dst, in_=srcd), "ld0", writes=[b_tab])
    S.dma("sp", lambda e: e.dma_start(out=x_p, in_=x_p_d), "ldX", writes=[b_X])
    S.dma("sp", lambda e: e.dma_start(out=wfs, in_=w_f.rearrange("(c p) n -> p c n", p=128)), "ld0", writes=[b_wfs])
    S.op("dve", lambda e: e.tensor_copy(out=wfb.rearrange("p a b -> p (a b)"), in_=wfs.rearrange("p a b -> p (a b)")), reads=[b_wfs], writes=[b_wfb])
    for cb in range(4):
        for ri in range(2):
            p = (cb * 2 + ri) % 2
            S.op("pe", lambda e, cb=cb, ri=ri, p=p: e.matmul(out=pA[:, p * 512:(p + 1) * 512], lhsT=bd_t[:, ri, :], rhs=wfb[:, cb, :], start=True, stop=True),
                 reads=[b_tab, b_wfb], writes=[b_pA[p]])
            S.op("act", lambda e, cb=cb, ri=ri, p=p: e.activation(out=Wcs[:, cb, ri, :], in_=pA[:, p * 512:(p + 1) * 512], func=AF.Copy),
                 reads=[b_pA[p]], writes=[b_Wcs])

    evac_rr = [0]

    def evac(out, in_, reads, writes):
        evac_rr[0] ^= 1
        if evac_rr[0]:
            S.op("act", lambda e: e.activation(out=out, in_=in_, func=AF.Copy), reads=reads, writes=writes)
        else:
            S.op("dve", lambda e: e.tensor_copy(out=out, in_=in_), reads=reads, writes=writes)

    def fourier(A, halves, nk1h, NK2, udram, b_udram, CS, X, ntok, yf_row0, ssf_col0):
        Tv = Tt if nk1h == 64 else Tt.rearrange("p r k c -> p (r k c)")[:, 0:2 * nk1h * 128].rearrange("p (r k c) -> p r k c", r=2, k=nk1h)
        ZTv = ZT if ntok == 4096 else ZT.rearrange("p c r t -> p (c r t)")[:, 0:8 * ntok].rearrange("p (c r t) -> p c r t", c=4, r=2)
        w1 = 2 * nk1h
        G1 = 512 // w1
        w2 = 2 * NK2
        G2 = 512 // w2
        pb = 0
        for cb in range(4):
            S.dma("sp", lambda e, cb=cb: e.dma_start(out=Ut[0:A], in_=(udram[cb] if udram is not None else u_s2[cb]).rearrange("(a b) k -> a b k", b=128)), "ldU",
                  reads=b_udram[cb], writes=[b_U])
            for hf in range(halves):
                for g in range(128 // G1):
                    pb ^= 1
                    ch0 = g * G1
                    S.op("pe", [(lambda e, i=i, ch0=ch0, hf=hf, pb=pb: e.matmul(out=pA[:, pb * 512 + i * w1: pb * 512 + (i + 1) * w1], lhsT=Ut[0:A, :, ch0 + i],
                                                                            rhs=(CS[0:A, hf].rearrange("p r k -> p (r k)") if halves == 2 else CS[0:A].rearrange("p r k -> p (r k)")),
                                                                            start=True, stop=True)) for i in range(G1)],
                         reads=[b_U, b_tab], writes=[b_pA[pb]])
                    evac(Tv[:, :, :, ch0:ch0 + G1].rearrange("p r k c -> p c r k"),
                         pA[:, pb * 512:(pb + 1) * 512].rearrange("p (c r k) -> p c r k", c=G1, r=2), [b_pA[pb]], [b_T])
                for g in range(nk1h // G2):
                    pb ^= 1
                    fl = []
                    for i in range(G2):
                        k1l = g * G2 + i
                        k1a = hf * nk1h + k1l
                        o = pA[:, pb * 512 + i * w2: pb * 512 + (i + 1) * w2]
                        fl.append(lambda e, o=o, k1l=k1l, k1a=k1a: e.matmul(out=o, lhsT=Tv[:, 1, k1l, :], rhs=X[:, k1a, 0:2, :].rearrange("p r k -> p (r k)"), start=True, stop=False))
                        fl.append(lambda e, o=o, k1l=k1l, k1a=k1a: e.matmul(out=o, lhsT=Tv[:, 0, k1l, :], rhs=X[:, k1a, 1:3, :].rearrange("p r k -> p (r k)"), start=False, stop=True))
                    S.op("pe", fl, reads=[b_T, b_X], writes=[b_pA[pb]])
                    k1lo = hf * nk1h + g * G2
                    evac(ZTv[:, cb].rearrange("p r (k2 k1) -> p k1 r k2", k1=A)[:, k1lo:k1lo + G2],
                         pA[:, pb * 512:(pb + 1) * 512].rearrange("p (k1 r k2) -> p k1 r k2", k1=G2, r=2), [b_pA[pb]], [b_ZT])
        for tt in range(ntok // 128):
            pb ^= 1
            fl = []
            for cb in range(4):
                for ri in range(2):
                    fl.append(lambda e, cb=cb, ri=ri, pb=pb, tt=tt: e.matmul(out=pA[:, pb * 512:(pb + 1) * 512], lhsT=ZTv[:, cb, ri, tt * 128:(tt + 1) * 128], rhs=Wcs[:, cb, ri, :],
                                                                       start=(cb == 0 and ri == 0), stop=(cb == 3 and ri == 1)))
            S.op("pe", fl, reads=[b_ZT, b_Wcs], writes=[b_pA[pb]])
            sl = tt % 2
            S.op("dve", lambda e, pb=pb, sl=sl: e.tensor_tensor(out=yft[sl], in0=pA[:, pb * 512:(pb + 1) * 512], in1=bf_t, op=ALU.add),
                 reads=[b_pA[pb], b_tab], writes=[b_yft[sl]])
            col = ssf_col0 + tt
            S.op("act", lambda e, sl=sl, col=col: e.activation(out=junkF, in_=yft[sl], func=AF.Square, scale=float(1.0 / np.sqrt(512.0)), accum_out=ssf[:, col:col + 1]),
                 reads=[b_yft[sl]], writes=[b_junk, b_ssf[col]])
            r0 = yf_row0 + tt * 128
            S.dma("sp", lambda e, sl=sl, r0=r0: e.dma_start(out=yf_all[r0:r0 + 128, :], in_=yft[sl]), "styf%d" % sl, reads=[b_yft[sl]], writes=[b_yf[r0 // 128]])

    junkF = AR.alloc([128, 512], F32)
    fsel = dbg.get("fourier", [0, 1, 2])
    if 0 in fsel:
        fourier(16, 1, 16, 128, u_p[0], b_up[0], cs_p, x_p, 2048, 0, 0)
    if 1 in fsel:
        fourier(16, 1, 16, 128, u_p[1], b_up[1], cs_p, x_p, 2048, 2048, 16)
    if 2 in fsel:
        S.dma("sp", lambda e: e.dma_start(out=x_s, in_=x_s_d), "ldX", writes=[b_X])
        fourier(128, 2, 64, 32, None, b_us, cs_s, x_s, 4096, 4096, 32)
    S.op("act", lambda e: e.activation(out=sqf, in_=ssf, func=AF.Sqrt, bias=EPS, scale=1.0), reads=b_ssf, writes=[b_rstdf])
    S.op("dve", lambda e: e.reciprocal(out=rstd_f, in_=sqf), reads=[b_rstdf], writes=[b_rstdf])

    S.barrier()
    AR.off = base_mark
    if dbg.get("stop") == "F":
        S.emit()
        return nc

    W_r = AR.alloc([128, 8, 2560], BF16)
    W_o = AR.alloc([128, 8, D], BF16)
    E = AR.alloc([128, 7, 8, 128], BF16)
    EBn = AR.alloc([128, 5, 8, 128], BF16)
    mskn = AR.alloc([128, 5, 128], BF16)
    msk = [AR.alloc([128, 6, 128], BF16) for _ in range(2)]
    gpost_t = AR.alloc([128, D], F32)
    prep_mark = AR.off
    wstB = [AR.alloc([128, 3072], F32) for _ in range(2)]
    b_wstB = [Buf(), Buf()]
    b_Wr, b_Wo = Buf(), Buf()
    for kc in range(8):
        sl = kc % 2
        S.dma("sp", lambda e, kc=kc, sl=sl: e.dma_start(out=wstB[sl], in_=w_in[kc * 128:(kc + 1) * 128, :]), "ldw%d" % sl, writes=[b_wstB[sl]])
        S.op("act", lambda e, kc=kc, sl=sl: e.activation(out=W_r[:, kc, 0:2048], in_=wstB[sl][:, 0:2048], func=AF.Copy, scale=gpre_t[:, kc:kc + 1]),
             reads=[b_wstB[sl], b_g], writes=[b_Wr])
        S.op("dve", lambda e, kc=kc, sl=sl: e.tensor_scalar(out=W_r[:, kc, 2048:2560], in0=wstB[sl][:, 2560:3072], scalar1=gpre_t[:, kc:kc + 1], scalar2=None, op0=ALU.mult),
             reads=[b_wstB[sl], b_g], writes=[b_Wr])
    for kc in range(8):
        sl = kc % 2
        S.dma("sp", lambda e, kc=kc, sl=sl: e.dma_start(out=wstB[sl][:, 0:D], in_=w_out[kc * 128:(kc + 1) * 128, :]), "ldw%d" % sl, writes=[b_wstB[sl]])
        S.op("dve", lambda e, kc=kc, sl=sl: e.tensor_scalar(out=W_o[:, kc, :], in0=wstB[sl][:, 0:D], scalar1=gcat_t[:, kc:kc + 1], scalar2=0.5, op0=ALU.mult, op1=ALU.mult),
             reads=[b_wstB[sl], b_g], writes=[b_Wo])
    b_E, b_EBn, b_mskn, b_msk, b_gpost = Buf(), Buf(), Buf(), [Buf(), Buf()], Buf()
    S.dma("sp", lambda e: e.dma_start(out=mskn, in_=masks_d[:, 0:5, :]), "ld0", writes=[b_mskn])
    S.dma("sp", lambda e: e.dma_start(out=gpost_t, in_=gpost), "ld0", writes=[b_gpost])
    for c in range(7):
        sl = c % 2
        st_ = wstB[sl][:, 0:1024].rearrange("p (h q) -> p h q", h=8)
        S.dma("sp", lambda e, c=c, st_=st_: e.dma_start(out=st_, in_=biasB[:, c]), "ldw%d" % sl, writes=[b_wstB[sl]])
        S.op("act", lambda e, c=c, st_=st_: e.activation(out=E[:, c], in_=st_, func=AF.Exp), reads=[b_wstB[sl]], writes=[b_E])
    for c in range(5):
        S.op("dve", lambda e, c=c: e.tensor_tensor(out=EBn[:, c], in0=E[:, c + 1], in1=mskn[:, c:c + 1, :].to_broadcast([128, 8, 128]), op=ALU.mult),
             reads=[b_E, b_mskn], writes=[b_EBn])

    S.barrier()
    AR.off = prep_mark
    NXR = 3
    xr = [AR.alloc([128, D], F32) for _ in range(NXR)]
    b_xr = [Buf() for _ in range(NXR)]
    xres = [AR.alloc([128, D], F32) for _ in range(2)]
    b_xres = [Buf(), Buf()]
    hbB = AR.alloc([128, D], BF16)
    hTB = AR.alloc([128, 8, 128], BF16)
    qk_tm = AR.alloc([128, D], BF16)
    NQ = 6
    qT = [AR.alloc([128, 4, 128], BF16) for _ in range(NQ)]
    NR = 8
    kT = AR.alloc([128, 4, NR, 128], BF16)
    V1 = AR.alloc([128, NR, 8, 65], BF16)
    NG = 7
    gz = [AR.alloc([128, D], BF16) for _ in range(NG)]
    th = AR.alloc([128, 512], F32)
    Pexp = [AR.alloc([128, 6, 128], BF16) for _ in range(2)]
    Pm = [AR.alloc([128, 6, 128], BF16) for _ in range(2)]
    rden = [AR.alloc([128, 8], F32) for _ in range(2)]
    ya = [AR.alloc([128, 8, 64], F32) for _ in range(2)]
    mixed = AR.alloc([128, D], BF16)
    mT = AR.alloc([128, 8, 128], BF16)
    NO = 2
    o_sb = [AR.alloc([128, D], F32) for _ in range(NO)]
    yfr = [AR.alloc([128, 512], F32) for _ in range(2)]
    res1 = AR.alloc([128, D], F32)
    b_res1 = Buf()
    NST = 4
    stB = [AR.alloc([128, 4], F32) for _ in range(NST)]
    sqB = [AR.alloc([128, 4], F32) for _ in range(NST)]
    rsB = [AR.alloc([128, 4], F32) for _ in range(NST)]
    junkB = AR.alloc([128, D], BF16)
    b_hbB, b_hTB, b_qktm, b_th = Buf(), Buf(), Buf(), Buf()
    b_qT = [Buf() for _ in range(NQ)]
    b_kT = [Buf() for _ in range(NR)]
    b_V1 = [Buf() for _ in range(NR)]
    b_gz = [Buf() for _ in range(NG)]
    b_Pexp, b_Pm = [Buf(), Buf()], [Buf(), Buf()]
    b_rden, b_ya = [Buf(), Buf()], [Buf(), Buf()]
    b_mixed, b_mT = Buf(), Buf()
    b_osb = [Buf() for _ in range(NO)]
    b_yfr = [Buf(), Buf()]
    b_stB = [[Buf() for _ in range(4)] for _ in range(NST)]
    b_sqB = [Buf() for _ in range(NST)]
    b_rsB = [Buf() for _ in range(NST)]
    b_junkB = Buf()
    b_out = Buf()
    pTi = pT[0]
    b_pTi = b_pT[0]
    pTo = pTi
    b_pTo = b_pTi
    pO = bank(7)
    b_pO = Buf()
    gctr = [0]

    S.op("dve", lambda e: e.memset(V1.rearrange("p r h d -> p (r h d)"), 1.0), writes=b_V1)
    for i in range(NST):
        S.op("pool", lambda e, i=i: e.memset(stB[i], 1.0), writes=b_stB[i])

    segs = []
    for sq_ in range(2):
        segs.append(dict(kind="p", n=16, lo=0, hi=15, x=xp[sq_ * SP_LEN:(sq_ + 1) * SP_LEN, :], xoff=0,
                         out=yp[sq_ * SP_LEN:(sq_ + 1) * SP_LEN, :], yf0=sq_ * 16))
    segs.append(dict(kind="s", n=32, lo=-HALO, hi=31 + HALO, x=xh, xoff=HALO, out=ys, yf0=32))
    if "segs" in dbg:
        segs = [segs[i] for i in dbg["segs"]]

    def xtile(seg, t):
        r0 = (t + seg["xoff"]) * 128
        return seg["x"][r0:r0 + 128, :]

    def interleave(gens):
        gens = [g for g in gens if g is not None]
        while gens:
            for g in list(gens):
                try:
                    next(g)
                except StopIteration:
                    gens.remove(g)

    def run_segment(seg):
        n, lo, hi, kind = seg["n"], seg["lo"], seg["hi"], seg["kind"]

        def slot(t):
            return t % NR

        def chunks_of(t):
            key = (kind, t)
            if key in SPECIAL:
                return SPECIAL[key], key
            return NORMAL_CH, None

        def load_x(t):
            sl = t % NXR
            S.dma("sp", lambda e, sl=sl, t=t: e.dma_start(out=xr[sl], in_=xtile(seg, t)), "ldx%d" % sl, writes=[b_xr[sl]])

        def sq_pre(t, stslot, col):
            sl = t % NXR
            S.op("act", lambda e: e.activation(out=junkB, in_=xr[sl], func=AF.Square, scale=1.0 / 32, accum_out=stB[stslot][:, col:col + 1]),
                 reads=[b_xr[sl]], writes=[b_junkB, b_stB[stslot][col]])

        def sqrt_batch(stslot):
            S.op("act", lambda e: e.activation(out=sqB[stslot], in_=stB[stslot], func=AF.Sqrt, bias=EPS, scale=1.0), reads=b_stB[stslot], writes=[b_sqB[stslot]])
            S.op("dve", lambda e: e.reciprocal(out=rsB[stslot], in_=sqB[stslot]), reads=[b_sqB[stslot]], writes=[b_rsB[stslot]])

        def stage_in(t, rs_slot):
            kv_only = (t < 0 or t >= n)
            sl = t % NXR
            ks = slot(t)
            S.op("dve", lambda e: e.tensor_scalar(out=hbB, in0=xr[sl], scalar1=rsB[rs_slot][:, 2:3], scalar2=None, op0=ALU.mult),
                 reads=[b_xr[sl], b_rsB[rs_slot]], writes=[b_hbB])
            yield
            S.op("pe", [(lambda e, c=c: e.transpose(out=pTi[:, c, :], in_=hbB[:, c * 128:(c + 1) * 128], identity=ident)) for c in range(8)],
                 reads=[b_hbB, b_ident], writes=[b_pTi])
            S.op("act", lambda e: e.activation(out=hTB, in_=pTi, func=AF.Copy), reads=[b_pTi], writes=[b_hTB])
            yield
            cgs = [1, 2] if kv_only else [1, 0, 2, 3, 4]
            for cgi, cg in enumerate(cgs):
                p = cgi % 2
                pa = pA[:, p * 512:(p + 1) * 512]
                S.op("pe", [(lambda e, c=c, cg=cg, pa=pa: e.matmul(out=pa, lhsT=hTB[:, c, :], rhs=W_r[:, c, cg * 512:(cg + 1) * 512], start=(c == 0), stop=(c == 7))) for c in range(8)],
                     reads=[b_hTB, b_Wr], writes=[b_pA[p]])
                if cg == 0:
                    S.op("act", lambda e, pa=pa: e.activation(out=qk_tm[:, 0:512], in_=pa, func=AF.Copy), reads=[b_pA[p]], writes=[b_qktm])
                elif cg == 1:
                    S.op("dve", lambda e, pa=pa: e.tensor_copy(out=qk_tm[:, 512:1024], in_=pa), reads=[b_pA[p]], writes=[b_qktm])
                elif cg == 2:
                    S.op("dve", lambda e, pa=pa: e.tensor_copy(out=V1[:, ks, :, 0:64], in_=pa.rearrange("p (h d) -> p h d", h=8)), reads=[b_pA[p]], writes=[b_V1[ks]])
                else:
                    go = (cg - 3) * 512
                    g = gz[t % NG]
                    S.op("act", lambda e, pa=pa: e.activation(out=th, in_=pa, func=AF.Tanh, scale=0.5), reads=[b_pA[p]], writes=[b_th])
                    S.op("dve", lambda e, pa=pa, g=g, go=go: e.scalar_tensor_tensor(out=g[:, go:go + 512], in0=th, scalar=1.0, in1=pa, op0=ALU.add, op1=ALU.mult),
                         reads=[b_th, b_pA[p]], writes=[b_gz[t % NG]])
                yield
                if cg == 0 or (kv_only and cg == 1):
                    c0 = 4 if kv_only else 0
                    S.op("pe", [(lambda e, c=c: e.transpose(out=pTi[:, c, :], in_=qk_tm[:, c * 128:(c + 1) * 128], identity=ident)) for c in range(c0, 8)],
                         reads=[b_qktm, b_ident], writes=[b_pTi])
                    if not kv_only:
                        S.op("act", lambda e: e.activation(out=qT[t % NQ], in_=pTi[:, 0:4, :], func=AF.Copy), reads=[b_pTi], writes=[b_qT[t % NQ]])
                    S.op("act", lambda e: e.activation(out=kT[:, :, ks, :], in_=pTi[:, 4:8, :], func=AF.Copy), reads=[b_pTi], writes=[b_kT[ks]])
                    yield

        def stage_att(t, stslot):
            chs, key = chunks_of(t)
            chs = [c for c in chs if lo <= t + c <= hi][:dbg.get("maxch", 6)]
            nch = len(chs)
            ms = 0
            if key is not None:
                full = SPECIAL[key]
                ms = (gctr[0]) % 2
                gctr[0] += 1
                i0 = full.index(chs[0])
                S.dma("sp", lambda e: e.dma_start(out=msk[ms][:, 0:nch, :], in_=masks_d[:, MSLOT[key] + i0: MSLOT[key] + i0 + nch, :]), "ldm%d" % ms, writes=[b_msk[ms]])
            yb = t % 2

            def qk(h):
                hp, po, sb = h // 2, 64 * (h % 2), h % 2
                S.op("pe", [(lambda e, ci=ci, c=c: e.matmul(out=pS[:, (sb * 8 + ci) * 128:(sb * 8 + ci + 1) * 128], lhsT=kT[po:po + 64, hp, slot(t + c), :],
                                                          rhs=qT[t % NQ][po:po + 64, hp, :], start=True, stop=True)) for ci, c in enumerate(chs)],
                     reads=[b_qT[t % NQ]] + [b_kT[slot(t + c)] for c in chs], writes=[b_pSb[sb]])

            def sm(h):
                sb = h % 2
                pe_, pm_ = Pexp[sb], Pm[sb]
                S.op("act", lambda e: e.activation(out=pe_[:, 0:nch, :].rearrange("p a b -> p (a b)"), in_=pS[:, sb * 1024: sb * 1024 + nch * 128], func=AF.Exp, scale=0.125),
                     reads=[b_pSb[sb]], writes=[b_Pexp[sb]])
                yield
                if key is None:
                    S.op("dve", lambda e: e.tensor_tensor(out=pm_[:, 0:5, :], in0=pe_[:, 0:5, :], in1=EBn[:, :, h, :], op=ALU.mult),
                         reads=[b_Pexp[sb], b_EBn], writes=[b_Pm[sb]])
                else:
                    e0 = chs[0] + 3
                    S.op("dve", lambda e: e.tensor_tensor(out=pm_[:, 0:nch, :], in0=pe_[:, 0:nch, :], in1=E[:, e0:e0 + nch, h, :], op=ALU.mult),
                         reads=[b_Pexp[sb], b_E], writes=[b_Pm[sb]])
                    yield
                    S.op("dve", lambda e: e.tensor_tensor(out=pm_[:, 0:nch, :], in0=pm_[:, 0:nch, :], in1=msk[ms][:, 0:nch, :], op=ALU.mult),
                         reads=[b_Pm[sb], b_msk[ms]], writes=[b_Pm[sb]])
                yield

            def pv(h):
                sb = h % 2
                pm_ = Pm[sb]
                ob = pO[:, (h % 4) * 65:(h % 4) * 65 + 65]
                S.op("pe", [(lambda e, ci=ci, c=c: e.matmul(out=ob, lhsT=pm_[:, ci, :], rhs=V1[:, slot(t + c), h, :], start=(ci == 0), stop=(ci == nch - 1)))
                            for ci, c in enumerate(chs)],
                     reads=[b_Pm[sb]] + [b_V1[slot(t + c)] for c in chs], writes=[b_pO])

            def norm(hb2):
                pv_ = pO[:, 0:260].rearrange("p (h d) -> p h d", d=65)
                S.op("dve", lambda e: e.reciprocal(out=rden[yb][:, hb2 * 4:(hb2 + 1) * 4], in_=pv_[:, :, 64]), reads=[b_pO], writes=[b_rden[yb]])
                yield
                S.op("dve", lambda e: e.tensor_tensor(out=ya[yb][:, hb2 * 4:(hb2 + 1) * 4, :], in0=pv_[:, :, 0:64],
                                                      in1=rden[yb][:, hb2 * 4:(hb2 + 1) * 4].unsqueeze(2).to_broadcast([128, 4, 64]), op=ALU.mult),
                     reads=[b_pO, b_rden[yb]], writes=[b_ya[yb]])
                yield

            qk(0)
            yield
            for h in range(8):
                if h + 1 < 8:
                    qk(h + 1)
                    yield
                yield from sm(h)
                pv(h)
                yield
                if h % 4 == 3:
                    yield from norm(h // 4)
            S.op("act", lambda e: e.activation(out=junkB[:, 0:512], in_=ya[yb].rearrange("p h d -> p (h d)"), func=AF.Square, scale=float(1.0 / np.sqrt(512.0)),
                                               accum_out=stB[stslot][:, 0:1]), reads=[b_ya[yb]], writes=[b_junkB, b_stB[stslot][0]])
            yield

        def prefetch_out(t):
            yfs = t % 2
            r0 = (seg["yf0"] + t) * 128
            S.dma("sp", lambda e: e.dma_start(out=yfr[yfs], in_=yf_all[r0:r0 + 128, :]), "ldyf%d" % yfs, reads=[b_yf[seg["yf0"] + t]], writes=[b_yfr[yfs]])

        def prefetch_final(t):
            xs_ = t % 2
            S.dma("sp", lambda e: e.dma_start(out=xres[xs_], in_=xtile(seg, t)), "ldxr%d" % xs_, writes=[b_xres[xs_]])

        def stage_out(t, rs_slot, stslot):
            g = gz[t % NG]
            yfs = t % 2
            yb = t % 2
            S.op("dve", lambda e: e.scalar_tensor_tensor(out=mixed[:, 0:512], in0=ya[yb].rearrange("p h d -> p (h d)"), scalar=rsB[rs_slot][:, 0:1], in1=g[:, 0:512], op0=ALU.mult, op1=ALU.mult),
                 reads=[b_ya[yb], b_rsB[rs_slot], b_gz[t % NG]], writes=[b_mixed])
            yield
            col = seg["yf0"] + t
            S.op("dve", lambda e: e.scalar_tensor_tensor(out=mixed[:, 512:1024], in0=yfr[yfs], scalar=rstd_f[:, col:col + 1], in1=g[:, 512:1024], op0=ALU.mult, op1=ALU.mult),
                 reads=[b_yfr[yfs], b_rstdf, b_gz[t % NG]], writes=[b_mixed])
            yield
            S.op("pe", [(lambda e, c=c: e.transpose(out=pTo[:, c, :], in_=mixed[:, c * 128:(c + 1) * 128], identity=ident)) for c in range(8)],
                 reads=[b_mixed, b_ident], writes=[b_pTo])
            S.op("act", lambda e: e.activation(out=mT, in_=pTo, func=AF.Copy), reads=[b_pTo], writes=[b_mT])
            yield
            for hf in range(2):
                S.op("pe", [(lambda e, c=c, hf=hf: e.matmul(out=pA[:, hf * 512:(hf + 1) * 512], lhsT=mT[:, c, :], rhs=W_o[:, c, hf * 512:(hf + 1) * 512], start=(c == 0), stop=(c == 7))) for c in range(8)],
                     reads=[b_mT, b_Wo], writes=[b_pA[hf]])
            os_ = t % NO
            S.op("act", lambda e: e.activation(out=junkB, in_=pA, func=AF.Square, scale=1.0 / 32, accum_out=stB[stslot][:, 1:2]),
                 reads=[b_pA[0], b_pA[1]], writes=[b_junkB, b_stB[stslot][1]])
            S.op("act", lambda e: e.activation(out=o_sb[os_], in_=pA, func=AF.Copy), reads=[b_pA[0], b_pA[1]], writes=[b_osb[os_]])
            yield

        def stage_final(t, rs_slot):
            os_ = t % NO
            xs_ = t % 2
            fp = dbg.get("fparts", 7)
            if fp & 2:
                S.op("dve", lambda e: e.scalar_tensor_tensor(out=res1, in0=o_sb[os_], scalar=rsB[rs_slot][:, 1:2], in1=gpost_t, op0=ALU.mult, op1=ALU.mult),
                     reads=[b_osb[os_], b_rsB[rs_slot], b_gpost], writes=[b_res1])
                yield
                S.op("dve", lambda e: e.tensor_tensor(out=xres[xs_], in0=res1, in1=xres[xs_], op=ALU.add), reads=[b_res1, b_xres[xs_]], writes=[b_xres[xs_]])
                yield
            r0 = t * 128
            if fp & 4:
                S.dma("sp", lambda e: e.dma_start(out=seg["out"][r0:r0 + 128, :], in_=xres[xs_]), "sto%d" % xs_, reads=[b_xres[xs_]], writes=[b_out])
            yield

        LAG = 4
        tiles_in = list(range(lo, hi + 1))
        own = lambda t: (t is not None and 0 <= t < n)
        load_x(tiles_in[0])
        load_x(tiles_in[1])
        prev = gst[0] % NST
        gst[0] += 1
        sq_pre(tiles_in[0], prev, 2)
        sqrt_batch(prev)
        nsteps = len(tiles_in) + LAG + 2
        seq = tiles_in + [None] * (LAG + 2)
        for step in range(min(nsteps, dbg.get("max_steps", 10 ** 9))):
            cur = gst[0] % NST
            gst[0] += 1
            tin = seq[step]
            ta = seq[step - LAG] if step - LAG >= 0 else None
            to = seq[step - LAG - 1] if step - LAG - 1 >= 0 else None
            tf = seq[step - LAG - 2] if step - LAG - 2 >= 0 else None
            if tin is not None and tin + 2 <= hi:
                load_x(tin + 2)
            if own(to):
                prefetch_out(to)
            if own(tf) and (dbg.get("fparts", 7) & 1):
                prefetch_final(tf)
            gens = []
            stg = dbg.get("stages", "IAOF")
            if tin is not None and "I" in stg:
                gens.append(stage_in(tin, prev))
            if own(ta) and "A" in stg:
                gens.append(stage_att(ta, cur))
            if own(to) and "O" in stg:
                gens.append(stage_out(to, prev, cur))
            if own(tf) and "F" in stg:
                gens.append(stage_final(tf, prev))
            if dbg.get("serial"):
                for g_ in gens:
                    interleave([g_])
            else:
                interleave(gens)
            if tin is not None and tin + 1 <= hi:
                sq_pre(tin + 1, cur, 2)
            sqrt_batch(cur)
            prev = cur

    for seg_ in segs:
        run_segment(seg_)

    S.barrier()
    S.emit()
    return nc


_CACHE = {}


def kernel(x_prompt, x_sample, w_in, rpb, w_fourier, b_fourier, g_pre, g_na, g_f, w_out, g_post):
    f32 = np.float32
    x_prompt = np.asarray(x_prompt, f32)
    x_sample = np.asarray(x_sample, f32)
    w_in = np.asarray(w_in, f32)[0]
    w_out = np.asarray(w_out, f32)[0]
    w_f = np.asarray(w_fourier, f32)[0]
    rpb = np.asarray(rpb, f32)[0]
    bfb = np.ascontiguousarray(np.broadcast_to(np.asarray(b_fourier, f32)[0][None, :], (128, 512)))
    gpre = np.ascontiguousarray(np.asarray(g_pre, f32)[0].reshape(8, 128).T)
    gcat = np.ascontiguousarray(np.concatenate([np.asarray(g_na, f32)[0], np.asarray(g_f, f32)[0]]).reshape(8, 128).T)
    gpost = np.ascontiguousarray(np.broadcast_to(np.asarray(g_post, f32)[0][None, :], (128, D)))
    ri, ci = _bias_index()
    biasB = np.ascontiguousarray(rpb[:, ri, ci].transpose(1, 2, 0, 3))
    if "nc" not in _CACHE:
        _CACHE["nc"] = build_program(None)
    nc = _CACHE["nc"]
    in_maps = []
    for c in range(NCORES):
        b, j = c // 4, c % 4
        tb = _host_tables(c)
        xh = np.zeros(((32 + 2 * HALO) * 128, D), f32)
        lo_t = 4096 * j - HALO * 128
        hi_t = 4096 * (j + 1) + HALO * 128
        s0, s1 = max(lo_t, 0), min(hi_t, SS_LEN)
        xh[s0 - lo_t:s1 - lo_t] = x_sample[b, s0:s1]
        m = dict(xp=np.ascontiguousarray(x_prompt[2 * c:2 * c + 2].reshape(2 * SP_LEN, D)), xh=xh,
                 w_in=w_in, w_out=w_out, w_f=w_f, bfb=bfb, gpre=gpre, gcat=gcat, gpost=gpost, biasB=biasB)
        m.update(tb)
        in_maps.append(m)
    r = run_bass_kernel_spmd(nc, in_maps, core_ids=list(range(NCORES)))
    y_prompt = np.zeros((16, SP_LEN, D), f32)
    y_sample = np.zeros((2, SS_LEN, D), f32)
    for c in range(NCORES):
        b, j = c // 4, c % 4
        y_prompt[2 * c:2 * c + 2] = np.asarray(r.results[c]["yp"], f32).reshape(2, SP_LEN, D)
        y_sample[b, 4096 * j:4096 * (j + 1)] = np.asarray(r.results[c]["ys"], f32)
    return (y_prompt, y_sample)
```

```python
import contextlib
import numpy as np
import ml_dtypes
import concourse.bass as bass
import concourse.mybir as mybir
from concourse.bass_utils import run_bass_kernel_spmd

F32 = mybir.dt.float32
BF16 = mybir.dt.bfloat16
U8 = mybir.dt.uint8
ALU = mybir.AluOpType
AF = mybir.ActivationFunctionType

D = 1024
EPS = 1e-6
NCORES = 8
SP_LEN = 2048
SS_LEN = 16384
OWN_S = 4096
HALO = 2


class Buf:
    __slots__ = ("name", "w", "r")

    def __init__(self, name=""):
        self.name = name
        self.w = None
        self.r = []


class Sched:
    ENGS = ("pe", "act", "dve", "pool", "sp")

    def __init__(self, nc):
        self.nc = nc
        self.ops = {e: [] for e in self.ENGS}
        self.cnt = {}
        self.seen = {e: {} for e in self.ENGS}
        self.sems = {}

    def sem(self, name):
        if name not in self.sems:
            self.sems[name] = None
            self.cnt[name] = 0
        return name

    def _deps(self, eng, reads, writes):
        deps = {}
        for b in reads:
            if b.w is not None:
                s, v = b.w
                deps[s] = max(deps.get(s, 0), v)
        for b in writes:
            for (s, v) in b.r:
                deps[s] = max(deps.get(s, 0), v)
            if b.w is not None:
                s, v = b.w
                deps[s] = max(deps.get(s, 0), v)
        waits = []
        seen = self.seen[eng]
        for s, v in deps.items():
            if eng == "pe" and s == "pe":
                continue
            if seen.get(s, 0) >= v:
                continue
            seen[s] = v
            waits.append((s, v))
        return waits

    def _commit(self, tok, reads, writes):
        for b in writes:
            b.w = tok
            b.r = []
        for b in reads:
            b.r.append(tok)

    def op(self, eng, fns, reads=(), writes=()):
        if callable(fns):
            fns = [fns]
        waits = self._deps(eng, reads, writes)
        s = self.sem(eng)
        self.cnt[s] += 1
        tok = (s, self.cnt[s])
        self.ops[eng].append((waits, fns, (s, 1)))
        self._commit(tok, reads, writes)
        return tok

    def dma(self, q, fn, semname, reads=(), writes=()):
        if semname == "ld0":
            self._uniq = getattr(self, "_uniq", 0) + 1
            semname = "ld0_%d" % self._uniq
        waits = self._deps(q, reads, writes)
        s = self.sem(semname)
        self.cnt[s] += 16
        tok = (s, self.cnt[s])
        self.ops[q].append((waits, [fn], (s, 16)))
        self._commit(tok, reads, writes)
        return tok

    def coll(self, fn, semname, reads=(), writes=()):
        waits = self._deps("pool", reads, writes)
        s = self.sem(semname)
        self.cnt[s] += 1
        tok = (s, self.cnt[s])
        self.ops["pool"].append((waits, [fn], (s, 1)))
        self._commit(tok, reads, writes)
        return tok

    def barrier(self, skip=()):
        toks = [(s, v) for s, v in self.cnt.items() if v > 0 and not any(s.startswith(p) for p in skip)]
        for e in self.ENGS:
            waits = []
            for (s, v) in toks:
                if s == e:
                    continue
                if self.seen[e].get(s, 0) < v:
                    self.seen[e][s] = v
                    waits.append((s, v))
            if waits:
                self.ops[e].append((waits, [], None))

    def emit(self):
        nc = self.nc
        with contextlib.ExitStack() as st:
            for name in self.sems:
                self.sems[name] = st.enter_context(nc.semaphore("s_" + name))
            block = st.enter_context(nc.Block())
            sems = self.sems

            def run(e, lst):
                for waits, fns, inc in lst:
                    for (s, v) in waits:
                        e.wait_ge(sems[s], v)
                    last = None
                    for f in fns:
                        last = f(e)
                    if inc is not None:
                        last.then_inc(sems[inc[0]], inc[1])

            block.tensor(lambda e: run(e, self.ops["pe"]))
            block.scalar(lambda e: run(e, self.ops["act"]))
            block.vector(lambda e: run(e, self.ops["dve"]))
            block.gpsimd(lambda e: run(e, self.ops["pool"]))
            block.sync(lambda e: run(e, self.ops["sp"]))


class Arena:
    def __init__(self, nc, nbytes):
        self.t = nc.alloc_sbuf_tensor("arena", [128, nbytes], U8)
        self.n = nbytes
        self.off = 0

    def alloc(self, shape, dt):
        nb = 2 if dt == BF16 else 4
        n = int(np.prod(shape[1:])) * nb
        off = (self.off + 63) // 64 * 64
        assert off + n <= self.n, ("SBUF arena overflow", off, n, self.n)
        self.off = off + n
        ap = self.t[:, off:off + n].bitcast(dt)
        if len(shape) == 3:
            ap = ap.rearrange("p (a b) -> p a b", a=shape[1])
        elif len(shape) == 4:
            ap = ap.rearrange("p (a b c) -> p a b c", a=shape[1], b=shape[2])
        return ap


def _valid(qrow, krow, qcol, kcol, rows_total):
    rs = min(max(qrow - 4, 0), rows_total - 8)
    ws = min(max(qcol - 8, 0), 48)
    return (0 <= krow < rows_total) and (rs <= krow < rs + 8) and (ws <= kcol < ws + 16)


def _mask(qtile_global, chunks, rows_total):
    m = np.zeros((128, len(chunks), 128), np.float32)
    kr2 = np.arange(128) // 64
    kc = np.arange(128) % 64
    for ci, c in enumerate(chunks):
        for q in range(128):
            qrow = 2 * qtile_global + q // 64
            qcol = q % 64
            rs = min(max(qrow - 4, 0), rows_total - 8)
            ws = min(max(qcol - 8, 0), 48)
            krow = 2 * (qtile_global + c) + kr2
            ok = (krow >= 0) & (krow < rows_total) & (krow >= rs) & (krow < rs + 8) & (kc >= ws) & (kc < ws + 16)
            m[:, ci, q] = ok
    return m


NORMAL_CH = [-2, -1, 0, 1, 2]
SPECIAL = {
    ("p", 0): [0, 1, 2, 3], ("p", 1): [-1, 0, 1, 2], ("p", 14): [-2, -1, 0, 1], ("p", 15): [-3, -2, -1, 0],
    ("s", 0): [-2, -1, 0, 1, 2, 3], ("s", 1): [-2, -1, 0, 1, 2],
    ("s", 30): [-2, -1, 0, 1, 2], ("s", 31): [-3, -2, -1, 0, 1, 2],
}
SPECIAL_KEYS = list(SPECIAL.keys())
MSLOT = {}
_o = 5
for _k in SPECIAL_KEYS:
    MSLOT[_k] = _o
    _o += len(SPECIAL[_k])
NMASK = _o


def _host_tables(core):
    j = core % 4
    masks = np.zeros((128, NMASK, 128), np.float32)
    masks[:, 0:5, :] = _mask(8, NORMAL_CH, 32)
    for k in SPECIAL_KEYS:
        ch = SPECIAL[k]
        if k[0] == "p":
            m = _mask(k[1], ch, 32)
        else:
            m = _mask(32 * j + k[1], ch, 256)
        masks[:, MSLOT[k]:MSLOT[k] + len(ch), :] = m
    b = np.arange(128)[:, None, None]
    a = np.arange(128)[:, None]
    k1 = np.arange(128)[None, :]
    ang = 2 * np.pi * ((a * k1) % 128) / 128
    cs = np.stack([np.cos(ang), -np.sin(ang)], axis=1) / np.sqrt(SS_LEN)
    cs_s = cs.reshape(128, 2, 2, 64).transpose(0, 2, 1, 3)
    a16 = np.arange(16)[:, None]
    k16 = np.arange(16)[None, :]
    ang = 2 * np.pi * ((a16 * k16) % 16) / 16
    cs_p = np.zeros((128, 2, 16), np.float64)
    cs_p[0:16] = np.stack([np.cos(ang), -np.sin(ang)], axis=1) / np.sqrt(SP_LEN)
    k1s = np.arange(128)[None, :, None]
    k2l = np.arange(32)[None, None, :]
    k = k1s + 128 * (32 * j + k2l)
    th = 2 * np.pi * ((k * b) % SS_LEN) / SS_LEN
    xs = np.stack([np.sin(th), np.cos(th), -np.sin(th)], axis=2)
    k1p = np.arange(16)[None, :, None]
    k2p = np.arange(128)[None, None, :]
    k = k1p + 16 * k2p
    th = 2 * np.pi * ((k * b) % SP_LEN) / SP_LEN
    xp = np.stack([np.sin(th), np.cos(th), -np.sin(th)], axis=2)
    l = np.arange(64)[:, None]
    c = np.arange(64)[None, :]
    th = 2 * np.pi * ((l * c) % 64) / 64
    bd = np.zeros((128, 2, 128), np.float64)
    for g in range(2):
        bd[64 * g:64 * g + 64, 0, 64 * g:64 * g + 64] = np.cos(th) / 8
        bd[64 * g:64 * g + 64, 1, 64 * g:64 * g + 64] = np.sin(th) / 8
    bf = ml_dtypes.bfloat16
    return dict(
        masks=masks.astype(bf), cs_s=np.ascontiguousarray(cs_s).astype(bf), cs_p=cs_p.astype(bf),
        x_s=np.ascontiguousarray(xs).astype(bf), x_p=np.ascontiguousarray(xp).astype(bf),
        bd=bd.astype(bf), ident=np.eye(128, dtype=np.float32).astype(bf),
    )


def _bias_index():
    key = np.arange(128)[:, None, None]
    c7 = np.arange(7)[None, :, None] - 3
    q = np.arange(128)[None, None, :]
    dr = 2 * c7 + key // 64 - q // 64
    ri = np.clip(dr + 7, 0, 14)
    ci = np.clip(key % 64 - q % 64 + 15, 0, 30)
    return np.broadcast_to(ri, (128, 7, 128)), np.broadcast_to(ci, (128, 7, 128))


def build_program(dbg=None):
    dbg = dbg or {}
    nc = bass.Bass("TRN2", target_bir_lowering=False)
    S = Sched(nc)

    def din(name, shape, dt=F32):
        return nc.dram_tensor(name, list(shape), dt, kind="ExternalInput").ap()

    xp = din("xp", [2 * SP_LEN, D])
    xh = din("xh", [(32 + 2 * HALO) * 128, D])
    w_in = din("w_in", [D, 3072])
    w_out = din("w_out", [D, D])
    w_f = din("w_f", [512, 512])
    bfb = din("bfb", [128, 512])
    gpre = din("gpre", [128, 8])
    gcat = din("gcat", [128, 8])
    gpost = din("gpost", [128, D])
    biasB = din("biasB", [128, 7, 8, 128])
    masks_d = din("masks", [128, NMASK, 128], BF16)
    cs_s_d = din("cs_s", [128, 2, 2, 64], BF16)
    cs_p_d = din("cs_p", [128, 2, 16], BF16)
    x_s_d = din("x_s", [128, 128, 3, 32], BF16)
    x_p_d = din("x_p", [128, 16, 3, 128], BF16)
    bd_d = din("bd", [128, 2, 128], BF16)
    ident_d = din("ident", [128, 128], BF16)
    yp = nc.dram_tensor("yp", [2 * SP_LEN, D], F32, kind="ExternalOutput").ap()
    ys = nc.dram_tensor("ys", [OWN_S, D], F32, kind="ExternalOutput").ap()
    skind = "ExternalOutput" if dbg else "Internal"
    u_p = nc.dram_tensor("u_p", [2, 4, SP_LEN, 128], BF16, kind=skind).ap()
    u_loc2 = [nc.dram_tensor("u_loc%d" % c_, [OWN_S, 128], BF16, kind="Internal").ap() for c_ in range(4)]
    u_s2 = [nc.dram_tensor("u_s%d" % c_, [SS_LEN, 128], BF16, kind="Internal").ap() for c_ in range(4)]
    yf_all = nc.dram_tensor("yf_all", [8192, 512], F32, kind=skind).ap()

    AR = Arena(nc, 188 * 1024)
    ident = AR.alloc([128, 128], BF16)
    gpre_t = AR.alloc([128, 8], F32)
    gcat_t = AR.alloc([128, 8], F32)
    rstd_f = AR.alloc([128, 64], F32)
    ssf = AR.alloc([128, 64], F32)
    sqf = AR.alloc([128, 64], F32)
    b_ident, b_g, b_ssf, b_rstdf = Buf(), Buf(), [Buf() for _ in range(64)], Buf()
    S.dma("sp", lambda e: e.dma_start(out=ident, in_=ident_d), "ld0", writes=[b_ident])
    S.dma("sp", lambda e: e.dma_start(out=gpre_t, in_=gpre), "ld0", writes=[b_g])
    S.dma("sp", lambda e: e.dma_start(out=gcat_t, in_=gcat), "ld0", writes=[b_g])
    base_mark = AR.off

    PS = nc.alloc_psum_tensor("ps", [128, 4096], F32)
    def bank(i, n=1):
        return PS[:, i * 512:(i + n) * 512]
    pT = [bank(0).bitcast(BF16).rearrange("p (a b) -> p a b", a=8), bank(6).bitcast(BF16).rearrange("p (a b) -> p a b", a=8)]
    pA = bank(1, 2)
    pS = bank(3, 4)
    pOv = [bank(6), bank(7)]
    b_pT = [Buf(), Buf()]
    b_pA = [Buf(), Buf()]
    b_pSb = [Buf(), Buf()]
    b_pOv = [b_pT[1], Buf()]
    gst = [0]

    wst = AR.alloc([128, 8, 512], F32)
    W_uf = AR.alloc([128, 8, 512], BF16)
    b_wst, b_Wuf = Buf(), Buf()
    S.dma("sp", lambda e: e.dma_start(out=wst, in_=w_in[:, 2048:2560].rearrange("(c p) n -> p c n", p=128)), "ld0", writes=[b_wst])
    for kc in range(8):
        S.op("dve", lambda e, kc=kc: e.tensor_scalar(out=W_uf[:, kc, :], in0=wst[:, kc, :], scalar1=gpre_t[:, kc:kc + 1], scalar2=None, op0=ALU.mult),
             reads=[b_wst, b_g], writes=[b_Wuf])
    NXQ = 3
    xq = [AR.alloc([128, 4, D], F32) for _ in range(NXQ)]
    b_xq = [Buf() for _ in range(NXQ)]
    stA = [AR.alloc([128, 4], F32) for _ in range(NXQ)]
    sqA = [AR.alloc([128, 4], F32) for _ in range(NXQ)]
    rsA = [AR.alloc([128, 4], F32) for _ in range(NXQ)]
    b_stA = [[Buf() for _ in range(4)] for _ in range(NXQ)]
    b_sqA, b_rsA = [Buf() for _ in range(NXQ)], [Buf() for _ in range(NXQ)]
    junk = AR.alloc([128, D], BF16)
    b_junk = Buf()
    hb = [AR.alloc([128, D], BF16) for _ in range(2)]
    b_hb = [Buf(), Buf()]
    hT = [AR.alloc([128, 8, 128], BF16) for _ in range(2)]
    b_hT = [Buf(), Buf()]
    ub = [AR.alloc([128, 4, 128], BF16) for _ in range(2)]
    b_ub = [Buf(), Buf()]
    b_up = [[[] for _ in range(4)] for _ in range(2)]
    b_us = [[Buf()] for _ in range(4)]
    b_uloc = [[] for _ in range(4)]

    quads = []
    for q in range(8):
        r0 = HALO * 128 + q * 512
        quads.append((xh[r0:r0 + 512, :], None, q * 512, b_uloc))
    for sq_ in range(2):
        for q in range(4):
            quads.append((xp[sq_ * SP_LEN + q * 512: sq_ * SP_LEN + (q + 1) * 512, :], u_p[sq_], q * 512, b_up[sq_]))
    if "quads" in dbg:
        quads = [quads[i] for i in dbg["quads"]]
    ntA = 4 * len(quads)

    def load_quad(qi):
        src = quads[qi][0]
        sl = qi % NXQ
        S.dma("sp", lambda e: e.dma_start(out=xq[sl], in_=src.rearrange("(t p) d -> p t d", p=128)), "ldxq%d" % sl, writes=[b_xq[sl]])

    def stats_quad(qi):
        sl = qi % NXQ
        for i in range(4):
            S.op("act", lambda e, i=i: e.activation(out=junk, in_=xq[sl][:, i, :], func=AF.Square, scale=1.0 / 32, accum_out=stA[sl][:, i:i + 1]),
                 reads=[b_xq[sl]], writes=[b_junk, b_stA[sl][i]])
        S.op("act", lambda e: e.activation(out=sqA[sl], in_=stA[sl], func=AF.Sqrt, bias=EPS, scale=1.0), reads=b_stA[sl], writes=[b_sqA[sl]])
        S.op("dve", lambda e: e.reciprocal(out=rsA[sl], in_=sqA[sl]), reads=[b_sqA[sl]], writes=[b_rsA[sl]])

    def a_s1(it):
        qi, i, p = it // 4, it % 4, it % 2
        sl = qi % NXQ
        S.op("dve", lambda e: e.tensor_scalar(out=hb[p], in0=xq[sl][:, i, :], scalar1=rsA[sl][:, i:i + 1], scalar2=None, op0=ALU.mult),
             reads=[b_xq[sl], b_rsA[sl]], writes=[b_hb[p]])

    def a_s2(it):
        p = it % 2
        S.op("pe", [(lambda e, c=c: e.transpose(out=pT[p][:, c, :], in_=hb[p][:, c * 128:(c + 1) * 128], identity=ident)) for c in range(8)],
             reads=[b_hb[p], b_ident], writes=[b_pT[p]])
        S.op("act", lambda e: e.activation(out=hT[p], in_=pT[p], func=AF.Copy), reads=[b_pT[p]], writes=[b_hT[p]])

    def a_s3(it):
        qi, i, p = it // 4, it % 4, it % 2
        _, udst, tok0, b_ud = quads[qi]
        S.op("pe", [(lambda e, c=c: e.matmul(out=pA[:, p * 512:(p + 1) * 512], lhsT=hT[p][:, c, :], rhs=W_uf[:, c, :], start=(c == 0), stop=(c == 7))) for c in range(8)],
             reads=[b_hT[p], b_Wuf], writes=[b_pA[p]])
        S.op("dve", lambda e: e.tensor_copy(out=ub[p].rearrange("p a b -> p (a b)"), in_=pA[:, p * 512:(p + 1) * 512]), reads=[b_pA[p]], writes=[b_ub[p]])
        t0 = tok0 + i * 128
        if udst is None:
            for c_ in range(4):
                nb_ = Buf()
                b_ud[c_].append(nb_)
                S.dma("sp", lambda e, c_=c_: e.dma_start(out=u_loc2[c_][t0:t0 + 128, :], in_=ub[p][:, c_, :]), "stub%d_%d" % (p, c_),
                      reads=[b_ub[p]], writes=[nb_])
        else:
            nb_ = Buf()
            for c_ in range(4):
                b_ud[c_].append(nb_)
            S.dma("sp", lambda e: e.dma_start(out=udst[:, t0:t0 + 128, :].rearrange("c p k -> p c k"), in_=ub[p]), "stub%d" % p,
                  reads=[b_ub[p]], writes=[nb_])

    for qi in range(min(2, len(quads))):
        load_quad(qi)
    for it in range(-2, ntA):
        i1 = it + 2
        if 0 <= i1 < ntA:
            if i1 % 4 == 0:
                qi = i1 // 4
                if qi + 2 < len(quads):
                    load_quad(qi + 2)
                stats_quad(qi)
            a_s1(i1)
        if 0 <= it + 1 < ntA:
            a_s2(it + 1)
        if 0 <= it:
            a_s3(it)
            if it == 31 and "quads" not in dbg:
                for c_ in range(4):
                    S.coll(lambda e, c_=c_: e.collective_compute("AllGather", ALU.bypass, replica_groups=[[0, 1, 2, 3], [4, 5, 6, 7]], ins=[u_loc2[c_]], outs=[u_s2[c_]]),
                           "ccag%d" % c_, reads=b_uloc[c_], writes=b_us[c_])

    S.barrier()
    AR.off = base_mark
    if dbg.get("stop") == "A":
        S.emit()
        return nc

    bf_t = AR.alloc([128, 512], F32)
    bd_t = AR.alloc([128, 2, 128], BF16)
    wfb = AR.alloc([128, 4, 512], BF16)
    Wcs = AR.alloc([128, 4, 2, 512], BF16)
    cs_s = AR.alloc([128, 2, 2, 64], BF16)
    cs_p = AR.alloc([128, 2, 16], BF16)
    Xt = AR.alloc([128, 12288], BF16)
    x_s = Xt.rearrange("p (a b c) -> p a b c", a=128, b=3)
    x_p = Xt[:, 0:6144].rearrange("p (a b c) -> p a b c", a=16, b=3)
    Ut = AR.alloc([128, 128, 128], BF16)
    Tt = AR.alloc([128, 128, 2, 64], BF16)
    ZT = AR.alloc([128, 4, 2, 4096], BF16)
    wfs = ZT.rearrange("p c r t -> p (c r t)")[:, 0:4096].bitcast(F32).rearrange("p (c n) -> p c n", c=4)
    yft = [AR.alloc([128, 512], F32) for _ in range(2)]
    b_U, b_T, b_ZT = Buf(), Buf(), Buf()
    b_yft = [Buf(), Buf()]
    b_yf = [Buf() for _ in range(64)]
    b_tab, b_X, b_wfb, b_Wcs = Buf(), Buf(), Buf(), Buf()
    b_wfs = b_ZT
    for dst, srcd in ((bf_t, bfb), (bd_t, bd_d), (cs_s, cs_s_d), (cs_p, cs_p_d)):
        S.dma("sp", lambda e, dst=dst, srcd=srcd: e.dma_start(out=dst, in_=srcd), "ld0", writes=[b_tab])
    S.dma("sp", lambda e: e.dma_start(out=x_p, in_=x_p_d), "ldX", writes=[b_X])
    S.dma("sp", lambda e: e.dma_start(out=wfs, in_=w_f.rearrange("(c p) n -> p c n", p=128)), "ld0", writes=[b_wfs])
    S.op("dve", lambda e: e.tensor_copy(out=wfb.rearrange("p a b -> p (a b)"), in_=wfs.rearrange("p a b -> p (a b)")), reads=[b_wfs], writes=[b_wfb])
    for cb in range(4):
        for ri in range(2):
            p = (cb * 2 + ri) % 2
            S.op("pe", lambda e, cb=cb, ri=ri, p=p: e.matmul(out=pA[:, p * 512:(p + 1) * 512], lhsT=bd_t[:, ri, :], rhs=wfb[:, cb, :], start=True, stop=True),
                 reads=[b_tab, b_wfb], writes=[b_pA[p]])
            S.op("act", lambda e, cb=cb, ri=ri, p=p: e.activation(out=Wcs[:, cb, ri, :], in_=pA[:, p * 512:(p + 1) * 512], func=AF.Copy),
                 reads=[b_pA[p]], writes=[b_Wcs])

    evac_rr = [0]

    def evac(out, in_, reads, writes):
        evac_rr[0] ^= 1
        if evac_rr[0]:
            S.op("act", lambda e: e.activation(out=out, in_=in_, func=AF.Copy), reads=reads, writes=writes)
        else:
            S.op("dve", lambda e: e.tensor_copy(out=out, in_=in_), reads=reads, writes=writes)

    def fourier(A, halves, nk1h, NK2, udram, b_udram, CS, X, ntok, yf_row0, ssf_col0):
        Tv = Tt if nk1h == 64 else Tt.rearrange("p c r k -> p (c r k)")[:, 0:2 * nk1h * 128].rearrange("p (c r k) -> p c r k", c=128, r=2)
        S.barrier(skip=("ccag",))
        b_Tg = [Buf() for _ in range(128 // (512 // (2 * nk1h)))]
        b_ZTg = {}
        ZTv = ZT if ntok == 4096 else ZT.rearrange("p c r t -> p (c r t)")[:, 0:8 * ntok].rearrange("p (c r t) -> p c r t", c=4, r=2)
        w1 = 2 * nk1h
        G1 = 512 // w1
        w2 = 2 * NK2
        G2 = 512 // w2
        pb = 0
        for cb in range(4):
            S.dma("sp", lambda e, cb=cb: e.dma_start(out=Ut[0:A], in_=(udram[cb] if udram is not None else u_s2[cb]).rearrange("(a b) k -> a b k", b=128)), "ldU",
                  reads=b_udram[cb], writes=[b_U])
            for hf in range(halves):
                for g in range(128 // G1):
                    pb ^= 1
                    ch0 = g * G1
                    S.op("pe", [(lambda e, i=i, ch0=ch0, hf=hf, pb=pb: e.matmul(out=pA[:, pb * 512 + i * w1: pb * 512 + (i + 1) * w1], lhsT=Ut[0:A, :, ch0 + i],
                                                                            rhs=(CS[0:A, hf].rearrange("p r k -> p (r k)") if halves == 2 else CS[0:A].rearrange("p r k -> p (r k)")),
                                                                            start=True, stop=True)) for i in range(G1)],
                         reads=[b_U, b_tab], writes=[b_pA[pb]])
                    evac(Tv[:, ch0:ch0 + G1, :, :],
                         pA[:, pb * 512:(pb + 1) * 512].rearrange("p (c r k) -> p c r k", c=G1, r=2), [b_pA[pb]], [b_Tg[g]])
                for g in range(nk1h // G2):
                    pb ^= 1
                    fl = []
                    for i in range(G2):
                        k1l = g * G2 + i
                        k1a = hf * nk1h + k1l
                        o = pA[:, pb * 512 + i * w2: pb * 512 + (i + 1) * w2]
                        fl.append(lambda e, o=o, k1l=k1l, k1a=k1a: e.matmul(out=o, lhsT=Tv[:, :, 1, k1l], rhs=X[:, k1a, 0:2, :].rearrange("p r k -> p (r k)"), start=True, stop=False))
                        fl.append(lambda e, o=o, k1l=k1l, k1a=k1a: e.matmul(out=o, lhsT=Tv[:, :, 0, k1l], rhs=X[:, k1a, 1:3, :].rearrange("p r k -> p (r k)"), start=False, stop=True))
                    S.op("pe", fl, reads=b_Tg + [b_X], writes=[b_pA[pb]])
                    k1lo = hf * nk1h + g * G2
                    evac(ZTv[:, cb].rearrange("p r (k2 k1) -> p k1 r k2", k1=A)[:, k1lo:k1lo + G2],
                         pA[:, pb * 512:(pb + 1) * 512].rearrange("p (k1 r k2) -> p k1 r k2", k1=G2, r=2), [b_pA[pb]], [b_ZTg.setdefault((cb, hf, g), Buf())])
        for tt in range(ntok // 128):
            pb ^= 1
            fl = []
            for cb in range(4):
                for ri in range(2):
                    fl.append(lambda e, cb=cb, ri=ri, pb=pb, tt=tt: e.matmul(out=pA[:, pb * 512:(pb + 1) * 512], lhsT=ZTv[:, cb, ri, tt * 128:(tt + 1) * 128], rhs=Wcs[:, cb, ri, :],
                                                                       start=(cb == 0 and ri == 0), stop=(cb == 3 and ri == 1)))
            S.op("pe", fl, reads=[b_ZT, b_Wcs] + list(b_ZTg.values()), writes=[b_pA[pb]])
            sl = tt % 2
            S.op("dve", lambda e, pb=pb, sl=sl: e.tensor_tensor(out=yft[sl], in0=pA[:, pb * 512:(pb + 1) * 512], in1=bf_t, op=ALU.add),
                 reads=[b_pA[pb], b_tab], writes=[b_yft[sl]])
            col = ssf_col0 + tt
            S.op("act", lambda e, sl=sl, col=col: e.activation(out=junkF, in_=yft[sl], func=AF.Square, scale=float(1.0 / np.sqrt(512.0)), accum_out=ssf[:, col:col + 1]),
                 reads=[b_yft[sl]], writes=[b_junk, b_ssf[col]])
            r0 = yf_row0 + tt * 128
            S.dma("sp", lambda e, sl=sl, r0=r0: e.dma_start(out=yf_all[r0:r0 + 128, :], in_=yft[sl]), "styf%d" % sl, reads=[b_yft[sl]], writes=[b_yf[r0 // 128]])

    junkF = AR.alloc([128, 512], F32)
    fsel = dbg.get("fourier", [0, 1, 2])
    if 0 in fsel:
        fourier(16, 1, 16, 128, u_p[0], b_up[0], cs_p, x_p, 2048, 0, 0)
    if 1 in fsel:
        fourier(16, 1, 16, 128, u_p[1], b_up[1], cs_p, x_p, 2048, 2048, 16)
    if 2 in fsel:
        S.dma("sp", lambda e: e.dma_start(out=x_s, in_=x_s_d), "ldX", writes=[b_X])
        fourier(128, 2, 64, 32, None, b_us, cs_s, x_s, 4096, 4096, 32)
    S.op("act", lambda e: e.activation(out=sqf, in_=ssf, func=AF.Sqrt, bias=EPS, scale=1.0), reads=b_ssf, writes=[b_rstdf])
    S.op("dve", lambda e: e.reciprocal(out=rstd_f, in_=sqf), reads=[b_rstdf], writes=[b_rstdf])

    S.barrier()
    AR.off = base_mark
    if dbg.get("stop") == "F":
        S.emit()
        return nc

    W_r = AR.alloc([128, 8, 2560], BF16)
    W_o = AR.alloc([128, 8, D], BF16)
    E = AR.alloc([128, 7, 8, 128], BF16)
    EBn = AR.alloc([128, 5, 8, 128], BF16)
    mskn = AR.alloc([128, 5, 128], BF16)
    msk = [AR.alloc([128, 6, 128], BF16) for _ in range(2)]
    gpost_t = AR.alloc([128, D], F32)
    prep_mark = AR.off
    wstB = [AR.alloc([128, 3072], F32) for _ in range(2)]
    b_wstB = [Buf(), Buf()]
    b_Wr, b_Wo = Buf(), Buf()
    for kc in range(8):
        sl = kc % 2
        S.dma("sp", lambda e, kc=kc, sl=sl: e.dma_start(out=wstB[sl], in_=w_in[kc * 128:(kc + 1) * 128, :]), "ldw%d" % sl, writes=[b_wstB[sl]])
        S.op("act", lambda e, kc=kc, sl=sl: e.activation(out=W_r[:, kc, 0:2048], in_=wstB[sl][:, 0:2048], func=AF.Copy, scale=gpre_t[:, kc:kc + 1]),
             reads=[b_wstB[sl], b_g], writes=[b_Wr])
        S.op("dve", lambda e, kc=kc, sl=sl: e.tensor_scalar(out=W_r[:, kc, 2048:2560], in0=wstB[sl][:, 2560:3072], scalar1=gpre_t[:, kc:kc + 1], scalar2=None, op0=ALU.mult),
             reads=[b_wstB[sl], b_g], writes=[b_Wr])
    for kc in range(8):
        sl = kc % 2
        S.dma("sp", lambda e, kc=kc, sl=sl: e.dma_start(out=wstB[sl][:, 0:D], in_=w_out[kc * 128:(kc + 1) * 128, :]), "ldw%d" % sl, writes=[b_wstB[sl]])
        S.op("dve", lambda e, kc=kc, sl=sl: e.tensor_scalar(out=W_o[:, kc, :], in0=wstB[sl][:, 0:D], scalar1=gcat_t[:, kc:kc + 1], scalar2=0.5, op0=ALU.mult, op1=ALU.mult),
             reads=[b_wstB[sl], b_g], writes=[b_Wo])
    b_E, b_EBn, b_mskn, b_msk, b_gpost = Buf(), Buf(), Buf(), [Buf(), Buf()], Buf()
    S.dma("sp", lambda e: e.dma_start(out=mskn, in_=masks_d[:, 0:5, :]), "ld0", writes=[b_mskn])
    S.dma("sp", lambda e: e.dma_start(out=gpost_t, in_=gpost), "ld0", writes=[b_gpost])
    for c in range(7):
        sl = c % 2
        st_ = wstB[sl][:, 0:1024].rearrange("p (h q) -> p h q", h=8)
        S.dma("sp", lambda e, c=c, st_=st_: e.dma_start(out=st_, in_=biasB[:, c]), "ldw%d" % sl, writes=[b_wstB[sl]])
        S.op("act", lambda e, c=c, st_=st_: e.activation(out=E[:, c], in_=st_, func=AF.Exp), reads=[b_wstB[sl]], writes=[b_E])
    for c in range(5):
        S.op("dve", lambda e, c=c: e.tensor_tensor(out=EBn[:, c], in0=E[:, c + 1], in1=mskn[:, c:c + 1, :].to_broadcast([128, 8, 128]), op=ALU.mult),
             reads=[b_E, b_mskn], writes=[b_EBn])

    S.barrier()
    AR.off = prep_mark
    NXR = 3
    xr = [AR.alloc([128, D], F32) for _ in range(NXR)]
    b_xr = [Buf() for _ in range(NXR)]
    xres = [AR.alloc([128, D], F32) for _ in range(2)]
    b_xres = [Buf(), Buf()]
    hbB = AR.alloc([128, D], BF16)
    hTB = AR.alloc([128, 8, 128], BF16)
    qk_tm = AR.alloc([128, D], BF16)
    NQ = 6
    qT = [AR.alloc([128, 4, 128], BF16) for _ in range(NQ)]
    NR = 8
    kT = AR.alloc([128, 4, NR, 128], BF16)
    V1 = AR.alloc([128, NR, 8, 65], BF16)
    NG = 7
    gz = [AR.alloc([128, D], BF16) for _ in range(NG)]
    th = AR.alloc([128, 512], F32)
    Pexp = [AR.alloc([128, 6, 128], BF16) for _ in range(2)]
    Pm = [AR.alloc([128, 6, 128], BF16) for _ in range(2)]
    rden = [AR.alloc([128, 8], F32) for _ in range(2)]
    ya = [AR.alloc([128, 8, 64], F32) for _ in range(2)]
    mixed = AR.alloc([128, D], BF16)
    mT = AR.alloc([128, 8, 128], BF16)
    NO = 2
    o_sb = [AR.alloc([128, D], F32) for _ in range(NO)]
    yfr = [AR.alloc([128, 512], F32) for _ in range(2)]
    res1 = AR.alloc([128, D], F32)
    b_res1 = Buf()
    NST = 4
    stB = [AR.alloc([128, 4], F32) for _ in range(NST)]
    sqB = [AR.alloc([128, 4], F32) for _ in range(NST)]
    rsB = [AR.alloc([128, 4], F32) for _ in range(NST)]
    junkB = AR.alloc([128, D], BF16)
    b_hbB, b_hTB, b_qktm, b_th = Buf(), Buf(), Buf(), Buf()
    b_qT = [Buf() for _ in range(NQ)]
    b_kT = [Buf() for _ in range(NR)]
    b_V1 = [Buf() for _ in range(NR)]
    b_gz = [Buf() for _ in range(NG)]
    b_Pexp, b_Pm = [Buf(), Buf()], [Buf(), Buf()]
    b_rden, b_ya = [Buf(), Buf()], [Buf(), Buf()]
    b_mixed, b_mT = Buf(), Buf()
    b_osb = [Buf() for _ in range(NO)]
    b_yfr = [Buf(), Buf()]
    b_stB = [[Buf() for _ in range(4)] for _ in range(NST)]
    b_sqB = [Buf() for _ in range(NST)]
    b_rsB = [Buf() for _ in range(NST)]
    b_junkB = Buf()
    b_out = Buf()
    pTi = pT[0]
    b_pTi = b_pT[0]
    pTo = pTi
    b_pTo = b_pTi
    pO = bank(7)
    b_pO = Buf()
    gctr = [0]

    S.op("dve", lambda e: e.memset(V1.rearrange("p r h d -> p (r h d)"), 1.0), writes=b_V1)
    for i in range(NST):
        S.op("pool", lambda e, i=i: e.memset(stB[i], 1.0), writes=b_stB[i])

    segs = []
    for sq_ in range(2):
        segs.append(dict(kind="p", n=16, lo=0, hi=15, x=xp[sq_ * SP_LEN:(sq_ + 1) * SP_LEN, :], xoff=0,
                         out=yp[sq_ * SP_LEN:(sq_ + 1) * SP_LEN, :], yf0=sq_ * 16))
    segs.append(dict(kind="s", n=32, lo=-HALO, hi=31 + HALO, x=xh, xoff=HALO, out=ys, yf0=32))
    if "segs" in dbg:
        segs = [segs[i] for i in dbg["segs"]]

    def xtile(seg, t):
        r0 = (t + seg["xoff"]) * 128
        return seg["x"][r0:r0 + 128, :]

    def interleave(gens):
        gens = [g for g in gens if g is not None]
        while gens:
            for g in list(gens):
                try:
                    next(g)
                except StopIteration:
                    gens.remove(g)

    def run_segment(seg):
        n, lo, hi, kind = seg["n"], seg["lo"], seg["hi"], seg["kind"]

        def slot(t):
            return t % NR

        def chunks_of(t):
            key = (kind, t)
            if key in SPECIAL:
                return SPECIAL[key], key
            return NORMAL_CH, None

        def load_x(t):
            sl = t % NXR
            S.dma("sp", lambda e, sl=sl, t=t: e.dma_start(out=xr[sl], in_=xtile(seg, t)), "ldx%d" % sl, writes=[b_xr[sl]])

        def sq_pre(t, stslot, col):
            sl = t % NXR
            S.op("act", lambda e: e.activation(out=junkB, in_=xr[sl], func=AF.Square, scale=1.0 / 32, accum_out=stB[stslot][:, col:col + 1]),
                 reads=[b_xr[sl]], writes=[b_junkB, b_stB[stslot][col]])

        def sqrt_batch(stslot):
            S.op("act", lambda e: e.activation(out=sqB[stslot], in_=stB[stslot], func=AF.Sqrt, bias=EPS, scale=1.0), reads=b_stB[stslot], writes=[b_sqB[stslot]])
            S.op("dve", lambda e: e.reciprocal(out=rsB[stslot], in_=sqB[stslot]), reads=[b_sqB[stslot]], writes=[b_rsB[stslot]])

        def stage_in(t, rs_slot):
            kv_only = (t < 0 or t >= n)
            sl = t % NXR
            ks = slot(t)
            S.op("dve", lambda e: e.tensor_scalar(out=hbB, in0=xr[sl], scalar1=rsB[rs_slot][:, 2:3], scalar2=None, op0=ALU.mult),
                 reads=[b_xr[sl], b_rsB[rs_slot]], writes=[b_hbB])
            yield
            S.op("pe", [(lambda e, c=c: e.transpose(out=pTi[:, c, :], in_=hbB[:, c * 128:(c + 1) * 128], identity=ident)) for c in range(8)],
                 reads=[b_hbB, b_ident], writes=[b_pTi])
            S.op("act", lambda e: e.activation(out=hTB, in_=pTi, func=AF.Copy), reads=[b_pTi], writes=[b_hTB])
            yield
            cgs = [1, 2] if kv_only else [1, 0, 2, 3, 4]
            for cgi, cg in enumerate(cgs):
                p = cgi % 2
                pa = pA[:, p * 512:(p + 1) * 512]
                S.op("pe", [(lambda e, c=c, cg=cg, pa=pa: e.matmul(out=pa, lhsT=hTB[:, c, :], rhs=W_r[:, c, cg * 512:(cg + 1) * 512], start=(c == 0), stop=(c == 7))) for c in range(8)],
                     reads=[b_hTB, b_Wr], writes=[b_pA[p]])
                if cg == 0:
                    S.op("act", lambda e, pa=pa: e.activation(out=qk_tm[:, 0:512], in_=pa, func=AF.Copy), reads=[b_pA[p]], writes=[b_qktm])
                elif cg == 1:
                    S.op("dve", lambda e, pa=pa: e.tensor_copy(out=qk_tm[:, 512:1024], in_=pa), reads=[b_pA[p]], writes=[b_qktm])
                elif cg == 2:
                    S.op("dve", lambda e, pa=pa: e.tensor_copy(out=V1[:, ks, :, 0:64], in_=pa.rearrange("p (h d) -> p h d", h=8)), reads=[b_pA[p]], writes=[b_V1[ks]])
                else:
                    go = (cg - 3) * 512
                    g = gz[t % NG]
                    S.op("act", lambda e, pa=pa: e.activation(out=th, in_=pa, func=AF.Tanh, scale=0.5), reads=[b_pA[p]], writes=[b_th])
                    S.op("dve", lambda e, pa=pa, g=g, go=go: e.scalar_tensor_tensor(out=g[:, go:go + 512], in0=th, scalar=1.0, in1=pa, op0=ALU.add, op1=ALU.mult),
                         reads=[b_th, b_pA[p]], writes=[b_gz[t % NG]])
                yield
                if cg == 0 or (kv_only and cg == 1):
                    c0 = 4 if kv_only else 0
                    S.op("pe", [(lambda e, c=c: e.transpose(out=pTi[:, c, :], in_=qk_tm[:, c * 128:(c + 1) * 128], identity=ident)) for c in range(c0, 8)],
                         reads=[b_qktm, b_ident], writes=[b_pTi])
                    if not kv_only:
                        S.op("act", lambda e: e.activation(out=qT[t % NQ], in_=pTi[:, 0:4, :], func=AF.Copy), reads=[b_pTi], writes=[b_qT[t % NQ]])
                    S.op("act", lambda e: e.activation(out=kT[:, :, ks, :], in_=pTi[:, 4:8, :], func=AF.Copy), reads=[b_pTi], writes=[b_kT[ks]])
                    yield

        def stage_att(t, stslot):
            chs, key = chunks_of(t)
            chs = [c for c in chs if lo <= t + c <= hi][:dbg.get("maxch", 6)]
            nch = len(chs)
            ms = 0
            if key is not None:
                full = SPECIAL[key]
                ms = (gctr[0]) % 2
                gctr[0] += 1
                i0 = full.index(chs[0])
                S.dma("sp", lambda e: e.dma_start(out=msk[ms][:, 0:nch, :], in_=masks_d[:, MSLOT[key] + i0: MSLOT[key] + i0 + nch, :]), "ldm%d" % ms, writes=[b_msk[ms]])
            yb = t % 2

            def qk(h):
                hp, po, sb = h // 2, 64 * (h % 2), h % 2
                S.op("pe", [(lambda e, ci=ci, c=c: e.matmul(out=pS[:, (sb * 8 + ci) * 128:(sb * 8 + ci + 1) * 128], lhsT=kT[po:po + 64, hp, slot(t + c), :],
                                                          rhs=qT[t % NQ][po:po + 64, hp, :], start=True, stop=True)) for ci, c in enumerate(chs)],
                     reads=[b_qT[t % NQ]] + [b_kT[slot(t + c)] for c in chs], writes=[b_pSb[sb]])

            def sm(h):
                sb = h % 2
                pe_, pm_ = Pexp[sb], Pm[sb]
                S.op("act", lambda e: e.activation(out=pe_[:, 0:nch, :].rearrange("p a b -> p (a b)"), in_=pS[:, sb * 1024: sb * 1024 + nch * 128], func=AF.Exp, scale=0.125),
                     reads=[b_pSb[sb]], writes=[b_Pexp[sb]])
                yield
                if key is None:
                    S.op("dve", lambda e: e.tensor_tensor(out=pm_[:, 0:5, :], in0=pe_[:, 0:5, :], in1=EBn[:, :, h, :], op=ALU.mult),
                         reads=[b_Pexp[sb], b_EBn], writes=[b_Pm[sb]])
                else:
                    e0 = chs[0] + 3
                    S.op("dve", lambda e: e.tensor_tensor(out=pm_[:, 0:nch, :], in0=pe_[:, 0:nch, :], in1=E[:, e0:e0 + nch, h, :], op=ALU.mult),
                         reads=[b_Pexp[sb], b_E], writes=[b_Pm[sb]])
                    yield
                    S.op("dve", lambda e: e.tensor_tensor(out=pm_[:, 0:nch, :], in0=pm_[:, 0:nch, :], in1=msk[ms][:, 0:nch, :], op=ALU.mult),
                         reads=[b_Pm[sb], b_msk[ms]], writes=[b_Pm[sb]])
                yield

            def pv(h):
                sb = h % 2
                pm_ = Pm[sb]
                ob = pO[:, (h % 4) * 65:(h % 4) * 65 + 65]
                S.op("pe", [(lambda e, ci=ci, c=c: e.matmul(out=ob, lhsT=pm_[:, ci, :], rhs=V1[:, slot(t + c), h, :], start=(ci == 0), stop=(ci == nch - 1)))
                            for ci, c in enumerate(chs)],
                     reads=[b_Pm[sb]] + [b_V1[slot(t + c)] for c in chs], writes=[b_pO])

            def norm(hb2):
                pv_ = pO[:, 0:260].rearrange("p (h d) -> p h d", d=65)
                S.op("dve", lambda e: e.reciprocal(out=rden[yb][:, hb2 * 4:(hb2 + 1) * 4], in_=pv_[:, :, 64]), reads=[b_pO], writes=[b_rden[yb]])
                yield
                S.op("dve", lambda e: e.tensor_tensor(out=ya[yb][:, hb2 * 4:(hb2 + 1) * 4, :], in0=pv_[:, :, 0:64],
                                                      in1=rden[yb][:, hb2 * 4:(hb2 + 1) * 4].unsqueeze(2).to_broadcast([128, 4, 64]), op=ALU.mult),
                     reads=[b_pO, b_rden[yb]], writes=[b_ya[yb]])
                yield

            qk(0)
            yield
            for h in range(8):
                if h + 1 < 8:
                    qk(h + 1)
                    yield
                yield from sm(h)
                pv(h)
                yield
                if h % 4 == 3:
                    yield from norm(h // 4)
            S.op("act", lambda e: e.activation(out=junkB[:, 0:512], in_=ya[yb].rearrange("p h d -> p (h d)"), func=AF.Square, scale=float(1.0 / np.sqrt(512.0)),
                                               accum_out=stB[stslot][:, 0:1]), reads=[b_ya[yb]], writes=[b_junkB, b_stB[stslot][0]])
            yield

        def prefetch_out(t):
            yfs = t % 2
            r0 = (seg["yf0"] + t) * 128
            S.dma("sp", lambda e: e.dma_start(out=yfr[yfs], in_=yf_all[r0:r0 + 128, :]), "ldyf%d" % yfs, reads=[b_yf[seg["yf0"] + t]], writes=[b_yfr[yfs]])

        def prefetch_final(t):
            xs_ = t % 2
            S.dma("sp", lambda e: e.dma_start(out=xres[xs_], in_=xtile(seg, t)), "ldxr%d" % xs_, writes=[b_xres[xs_]])

        def stage_out(t, rs_slot, stslot):
            g = gz[t % NG]
            yfs = t % 2
            yb = t % 2
            S.op("dve", lambda e: e.scalar_tensor_tensor(out=mixed[:, 0:512], in0=ya[yb].rearrange("p h d -> p (h d)"), scalar=rsB[rs_slot][:, 0:1], in1=g[:, 0:512], op0=ALU.mult, op1=ALU.mult),
                 reads=[b_ya[yb], b_rsB[rs_slot], b_gz[t % NG]], writes=[b_mixed])
            yield
            col = seg["yf0"] + t
            S.op("dve", lambda e: e.scalar_tensor_tensor(out=mixed[:, 512:1024], in0=yfr[yfs], scalar=rstd_f[:, col:col + 1], in1=g[:, 512:1024], op0=ALU.mult, op1=ALU.mult),
                 reads=[b_yfr[yfs], b_rstdf, b_gz[t % NG]], writes=[b_mixed])
            yield
            S.op("pe", [(lambda e, c=c: e.transpose(out=pTo[:, c, :], in_=mixed[:, c * 128:(c + 1) * 128], identity=ident)) for c in range(8)],
                 reads=[b_mixed, b_ident], writes=[b_pTo])
            S.op("act", lambda e: e.activation(out=mT, in_=pTo, func=AF.Copy), reads=[b_pTo], writes=[b_mT])
            yield
            for hf in range(2):
                S.op("pe", [(lambda e, c=c, hf=hf: e.matmul(out=pA[:, hf * 512:(hf + 1) * 512], lhsT=mT[:, c, :], rhs=W_o[:, c, hf * 512:(hf + 1) * 512], start=(c == 0), stop=(c == 7))) for c in range(8)],
                     reads=[b_mT, b_Wo], writes=[b_pA[hf]])
            os_ = t % NO
            S.op("act", lambda e: e.activation(out=junkB, in_=pA, func=AF.Square, scale=1.0 / 32, accum_out=stB[stslot][:, 1:2]),
                 reads=[b_pA[0], b_pA[1]], writes=[b_junkB, b_stB[stslot][1]])
            S.op("act", lambda e: e.activation(out=o_sb[os_], in_=pA, func=AF.Copy), reads=[b_pA[0], b_pA[1]], writes=[b_osb[os_]])
            yield

        def stage_final(t, rs_slot):
            os_ = t % NO
            xs_ = t % 2
            fp = dbg.get("fparts", 7)
            if fp & 2:
                S.op("dve", lambda e: e.scalar_tensor_tensor(out=res1, in0=o_sb[os_], scalar=rsB[rs_slot][:, 1:2], in1=gpost_t, op0=ALU.mult, op1=ALU.mult),
                     reads=[b_osb[os_], b_rsB[rs_slot], b_gpost], writes=[b_res1])
                yield
                S.op("dve", lambda e: e.tensor_tensor(out=xres[xs_], in0=res1, in1=xres[xs_], op=ALU.add), reads=[b_res1, b_xres[xs_]], writes=[b_xres[xs_]])
                yield
            r0 = t * 128
            if fp & 4:
                S.dma("sp", lambda e: e.dma_start(out=seg["out"][r0:r0 + 128, :], in_=xres[xs_]), "sto%d" % xs_, reads=[b_xres[xs_]], writes=[b_out])
            yield

        LAG = 4
        tiles_in = list(range(lo, hi + 1))
        own = lambda t: (t is not None and 0 <= t < n)
        load_x(tiles_in[0])
        load_x(tiles_in[1])
        prev = gst[0] % NST
        gst[0] += 1
        sq_pre(tiles_in[0], prev, 2)
        sqrt_batch(prev)
        nsteps = len(tiles_in) + LAG + 2
        seq = tiles_in + [None] * (LAG + 2)
        for step in range(min(nsteps, dbg.get("max_steps", 10 ** 9))):
            cur = gst[0] % NST
            gst[0] += 1
            tin = seq[step]
            ta = seq[step - LAG] if step - LAG >= 0 else None
            to = seq[step - LAG - 1] if step - LAG - 1 >= 0 else None
            tf = seq[step - LAG - 2] if step - LAG - 2 >= 0 else None
            if tin is not None and tin + 2 <= hi:
                load_x(tin + 2)
            if own(to):
                prefetch_out(to)
            if own(tf) and (dbg.get("fparts", 7) & 1):
                prefetch_final(tf)
            gens = []
            stg = dbg.get("stages", "IAOF")
            if tin is not None and "I" in stg:
                gens.append(stage_in(tin, prev))
            if own(ta) and "A" in stg:
                gens.append(stage_att(ta, cur))
            if own(to) and "O" in stg:
                gens.append(stage_out(to, prev, cur))
            if own(tf) and "F" in stg:
                gens.append(stage_final(tf, prev))
            if dbg.get("serial"):
                for g_ in gens:
                    interleave([g_])
            else:
                interleave(gens)
            if tin is not None and tin + 1 <= hi:
                sq_pre(tin + 1, cur, 2)
            sqrt_batch(cur)
            prev = cur

    for seg_ in segs:
        run_segment(seg_)

    S.barrier()
    S.emit()
    return nc


_CACHE = {}


def kernel(x_prompt, x_sample, w_in, rpb, w_fourier, b_fourier, g_pre, g_na, g_f, w_out, g_post):
    f32 = np.float32
    x_prompt = np.asarray(x_prompt, f32)
    x_sample = np.asarray(x_sample, f32)
    w_in = np.asarray(w_in, f32)[0]
    w_out = np.asarray(w_out, f32)[0]
    w_f = np.asarray(w_fourier, f32)[0]
    rpb = np.asarray(rpb, f32)[0]
    bfb = np.ascontiguousarray(np.broadcast_to(np.asarray(b_fourier, f32)[0][None, :], (128, 512)))
    gpre = np.ascontiguousarray(np.asarray(g_pre, f32)[0].reshape(8, 128).T)
    gcat = np.ascontiguousarray(np.concatenate([np.asarray(g_na, f32)[0], np.asarray(g_f, f32)[0]]).reshape(8, 128).T)
    gpost = np.ascontiguousarray(np.broadcast_to(np.asarray(g_post, f32)[0][None, :], (128, D)))
    ri, ci = _bias_index()
    biasB = np.ascontiguousarray(rpb[:, ri, ci].transpose(1, 2, 0, 3))
    if "nc" not in _CACHE:
        _CACHE["nc"] = build_program(None)
    nc = _CACHE["nc"]
    in_maps = []
    for c in range(NCORES):
        b, j = c // 4, c % 4
        tb = _host_tables(c)
        xh = np.zeros(((32 + 2 * HALO) * 128, D), f32)
        lo_t = 4096 * j - HALO * 128
        hi_t = 4096 * (j + 1) + HALO * 128
        s0, s1 = max(lo_t, 0), min(hi_t, SS_LEN)
        xh[s0 - lo_t:s1 - lo_t] = x_sample[b, s0:s1]
        m = dict(xp=np.ascontiguousarray(x_prompt[2 * c:2 * c + 2].reshape(2 * SP_LEN, D)), xh=xh,
                 w_in=w_in, w_out=w_out, w_f=w_f, bfb=bfb, gpre=gpre, gcat=gcat, gpost=gpost, biasB=biasB)
        m.update(tb)
        in_maps.append(m)
    r = run_bass_kernel_spmd(nc, in_maps, core_ids=list(range(NCORES)))
    y_prompt = np.zeros((16, SP_LEN, D), f32)
    y_sample = np.zeros((2, SS_LEN, D), f32)
    for c in range(NCORES):
        b, j = c // 4, c % 4
        y_prompt[2 * c:2 * c + 2] = np.asarray(r.results[c]["yp"], f32).reshape(2, SP_LEN, D)
        y_sample[b, 4096 * j:4096 * (j + 1)] = np.asarray(r.results[c]["ys"], f32)
    return (y_prompt, y_sample)
```

```python
import contextlib
import numpy as np
import ml_dtypes
import concourse.bass as bass
import concourse.mybir as mybir
from concourse.bass_utils import run_bass_kernel_spmd

F32 = mybir.dt.float32
BF16 = mybir.dt.bfloat16
U8 = mybir.dt.uint8
ALU = mybir.AluOpType
AF = mybir.ActivationFunctionType

D = 1024
EPS = 1e-6
NCORES = 8
SP_LEN = 2048
SS_LEN = 16384
OWN_S = 4096
HALO = 2


class Buf:
    __slots__ = ("name", "w", "r")

    def __init__(self, name=""):
        self.name = name
        self.w = None
        self.r = []


class Sched:
    ENGS = ("pe", "act", "dve", "pool", "sp")

    def __init__(self, nc):
        self.nc = nc
        self.ops = {e: [] for e in self.ENGS}
        self.cnt = {}
        self.seen = {e: {} for e in self.ENGS}
        self.sems = {}

    def sem(self, name):
        if name not in self.sems:
            self.sems[name] = None
            self.cnt[name] = 0
        return name

    def _deps(self, eng, reads, writes):
        deps = {}
        for b in reads:
            if b.w is not None:
                s, v = b.w
                deps[s] = max(deps.get(s, 0), v)
        for b in writes:
            for (s, v) in b.r:
                deps[s] = max(deps.get(s, 0), v)
            if b.w is not None:
                s, v = b.w
                deps[s] = max(deps.get(s, 0), v)
        waits = []
        seen = self.seen[eng]
        for s, v in deps.items():
            if eng == "pe" and s == "pe":
                continue
            if seen.get(s, 0) >= v:
                continue
            seen[s] = v
            waits.append((s, v))
        return waits

    def _commit(self, tok, reads, writes):
        for b in writes:
            b.w = tok
            b.r = []
        for b in reads:
            b.r.append(tok)

    def op(self, eng, fns, reads=(), writes=()):
        if callable(fns):
            fns = [fns]
        waits = self._deps(eng, reads, writes)
        s = self.sem(eng)
        self.cnt[s] += 1
        tok = (s, self.cnt[s])
        self.ops[eng].append((waits, fns, (s, 1)))
        self._commit(tok, reads, writes)
        return tok

    def dma(self, q, fn, semname, reads=(), writes=()):
        if semname == "ld0":
            self._uniq = getattr(self, "_uniq", 0) + 1
            semname = "ld0_%d" % self._uniq
        waits = self._deps(q, reads, writes)
        s = self.sem(semname)
        self.cnt[s] += 16
        tok = (s, self.cnt[s])
        self.ops[q].append((waits, [fn], (s, 16)))
        self._commit(tok, reads, writes)
        return tok

    def coll(self, fn, semname, reads=(), writes=()):
        waits = self._deps("pool", reads, writes)
        s = self.sem(semname)
        self.cnt[s] += 1
        tok = (s, self.cnt[s])
        self.ops["pool"].append((waits, [fn], (s, 1)))
        self._commit(tok, reads, writes)
        return tok

    def barrier(self, skip=()):
        toks = [(s, v) for s, v in self.cnt.items() if v > 0 and not any(s.startswith(p) for p in skip)]
        for e in self.ENGS:
            waits = []
            for (s, v) in toks:
                if s == e:
                    continue
                if self.seen[e].get(s, 0) < v:
                    self.seen[e][s] = v
                    waits.append((s, v))
            if waits:
                self.ops[e].append((waits, [], None))

    def emit(self):
        nc = self.nc
        with contextlib.ExitStack() as st:
            for name in self.sems:
                self.sems[name] = st.enter_context(nc.semaphore("s_" + name))
            block = st.enter_context(nc.Block())
            sems = self.sems

            def run(e, lst):
                for waits, fns, inc in lst:
                    for (s, v) in waits:
                        e.wait_ge(sems[s], v)
                    last = None
                    for f in fns:
                        last = f(e)
                    if inc is not None:
                        last.then_inc(sems[inc[0]], inc[1])

            block.tensor(lambda e: run(e, self.ops["pe"]))
            block.scalar(lambda e: run(e, self.ops["act"]))
            block.vector(lambda e: run(e, self.ops["dve"]))
            block.gpsimd(lambda e: run(e, self.ops["pool"]))
            block.sync(lambda e: run(e, self.ops["sp"]))


class Arena:
    def __init__(self, nc, nbytes):
        self.t = nc.alloc_sbuf_tensor("arena", [128, nbytes], U8)
        self.n = nbytes
        self.off = 0

    def alloc(self, shape, dt):
        nb = 2 if dt == BF16 else 4
        n = int(np.prod(shape[1:])) * nb
        off = (self.off + 63) // 64 * 64
        assert off + n <= self.n, ("SBUF arena overflow", off, n, self.n)
        self.off = off + n
        ap = self.t[:, off:off + n].bitcast(dt)
        if len(shape) == 3:
            ap = ap.rearrange("p (a b) -> p a b", a=shape[1])
        elif len(shape) == 4:
            ap = ap.rearrange("p (a b c) -> p a b c", a=shape[1], b=shape[2])
        return ap


def _valid(qrow, krow, qcol, kcol, rows_total):
    rs = min(max(qrow - 4, 0), rows_total - 8)
    ws = min(max(qcol - 8, 0), 48)
    return (0 <= krow < rows_total) and (rs <= krow < rs + 8) and (ws <= kcol < ws + 16)


def _mask(qtile_global, chunks, rows_total):
    m = np.zeros((128, len(chunks), 128), np.float32)
    kr2 = np.arange(128) // 64
    kc = np.arange(128) % 64
    for ci, c in enumerate(chunks):
        for q in range(128):
            qrow = 2 * qtile_global + q // 64
            qcol = q % 64
            rs = min(max(qrow - 4, 0), rows_total - 8)
            ws = min(max(qcol - 8, 0), 48)
            krow = 2 * (qtile_global + c) + kr2
            ok = (krow >= 0) & (krow < rows_total) & (krow >= rs) & (krow < rs + 8) & (kc >= ws) & (kc < ws + 16)
            m[:, ci, q] = ok
    return m


NORMAL_CH = [-2, -1, 0, 1, 2]
SPECIAL = {
    ("p", 0): [0, 1, 2, 3], ("p", 1): [-1, 0, 1, 2], ("p", 14): [-2, -1, 0, 1], ("p", 15): [-3, -2, -1, 0],
    ("s", 0): [-2, -1, 0, 1, 2, 3], ("s", 1): [-2, -1, 0, 1, 2],
    ("s", 30): [-2, -1, 0, 1, 2], ("s", 31): [-3, -2, -1, 0, 1, 2],
}
SPECIAL_KEYS = list(SPECIAL.keys())
MSLOT = {}
_o = 5
for _k in SPECIAL_KEYS:
    MSLOT[_k] = _o
    _o += len(SPECIAL[_k])
NMASK = _o


def _host_tables(core):
    j = core % 4
    masks = np.zeros((128, NMASK, 128), np.float32)
    masks[:, 0:5, :] = _mask(8, NORMAL_CH, 32)
    for k in SPECIAL_KEYS:
        ch = SPECIAL[k]
        if k[0] == "p":
            m = _mask(k[1], ch, 32)
        else:
            m = _mask(32 * j + k[1], ch, 256)
        masks[:, MSLOT[k]:MSLOT[k] + len(ch), :] = m
    b = np.arange(128)[:, None, None]
    a = np.arange(128)[:, None]
    k1 = np.arange(128)[None, :]
    ang = 2 * np.pi * ((a * k1) % 128) / 128
    cs = np.stack([np.cos(ang), -np.sin(ang)], axis=1) / np.sqrt(SS_LEN)
    cs_s = cs.reshape(128, 2, 2, 64).transpose(0, 2, 1, 3)
    a16 = np.arange(16)[:, None]
    k16 = np.arange(16)[None, :]
    ang = 2 * np.pi * ((a16 * k16) % 16) / 16
    cs_p = np.zeros((128, 2, 16), np.float64)
    cs_p[0:16] = np.stack([np.cos(ang), -np.sin(ang)], axis=1) / np.sqrt(SP_LEN)
    k1s = np.arange(128)[None, :, None]
    k2l = np.arange(32)[None, None, :]
    k = k1s + 128 * (32 * j + k2l)
    th = 2 * np.pi * ((k * b) % SS_LEN) / SS_LEN
    xs = np.stack([np.sin(th), np.cos(th), -np.sin(th)], axis=2)
    k1p = np.arange(16)[None, :, None]
    k2p = np.arange(128)[None, None, :]
    k = k1p + 16 * k2p
    th = 2 * np.pi * ((k * b) % SP_LEN) / SP_LEN
    xp = np.stack([np.sin(th), np.cos(th), -np.sin(th)], axis=2)
    l = np.arange(64)[:, None]
    c = np.arange(64)[None, :]
    th = 2 * np.pi * ((l * c) % 64) / 64
    bd = np.zeros((128, 2, 128), np.float64)
    for g in range(2):
        bd[64 * g:64 * g + 64, 0, 64 * g:64 * g + 64] = np.cos(th) / 8
        bd[64 * g:64 * g + 64, 1, 64 * g:64 * g + 64] = np.sin(th) / 8
    bf = ml_dtypes.bfloat16
    return dict(
        masks=masks.astype(bf), cs_s=np.ascontiguousarray(cs_s).astype(bf), cs_p=cs_p.astype(bf),
        x_s=np.ascontiguousarray(xs).astype(bf), x_p=np.ascontiguousarray(xp).astype(bf),
        bd=bd.astype(bf), ident=np.eye(128, dtype=np.float32).astype(bf),
    )


def _bias_index():
    key = np.arange(128)[:, None, None]
    c7 = np.arange(7)[None, :, None] - 3
    q = np.arange(128)[None, None, :]
    dr = 2 * c7 + key // 64 - q // 64
    ri = np.clip(dr + 7, 0, 14)
    ci = np.clip(key % 64 - q % 64 + 15, 0, 30)
    return np.broadcast_to(ri, (128, 7, 128)), np.broadcast_to(ci, (128, 7, 128))


def build_program(dbg=None):
    dbg = dbg or {}
    nc = bass.Bass("TRN2", target_bir_lowering=False)
    S = Sched(nc)

    def din(name, shape, dt=F32):
        return nc.dram_tensor(name, list(shape), dt, kind="ExternalInput").ap()

    xp = din("xp", [2 * SP_LEN, D])
    xh = din("xh", [(32 + 2 * HALO) * 128, D])
    w_in = din("w_in", [D, 3072])
    w_out = din("w_out", [D, D])
    w_f = din("w_f", [512, 512])
    bfb = din("bfb", [128, 512])
    gpre = din("gpre", [128, 8])
    gcat = din("gcat", [128, 8])
    gpost = din("gpost", [128, D])
    biasB = din("biasB", [128, 7, 8, 128])
    masks_d = din("masks", [128, NMASK, 128], BF16)
    cs_s_d = din("cs_s", [128, 2, 2, 64], BF16)
    cs_p_d = din("cs_p", [128, 2, 16], BF16)
    x_s_d = din("x_s", [128, 128, 3, 32], BF16)
    x_p_d = din("x_p", [128, 16, 3, 128], BF16)
    bd_d = din("bd", [128, 2, 128], BF16)
    ident_d = din("ident", [128, 128], BF16)
    yp = nc.dram_tensor("yp", [2 * SP_LEN, D], F32, kind="ExternalOutput").ap()
    ys = nc.dram_tensor("ys", [OWN_S, D], F32, kind="ExternalOutput").ap()
    skind = "ExternalOutput" if dbg else "Internal"
    u_p = nc.dram_tensor("u_p", [2, 4, SP_LEN, 128], BF16, kind=skind).ap()
    u_loc2 = [nc.dram_tensor("u_loc%d" % c_, [OWN_S, 128], BF16, kind="Internal").ap() for c_ in range(4)]
    u_s2 = [nc.dram_tensor("u_s%d" % c_, [SS_LEN, 128], BF16, kind="Internal").ap() for c_ in range(4)]
    yf_all = nc.dram_tensor("yf_all", [8192, 512], F32, kind=skind).ap()

    AR = Arena(nc, 188 * 1024)
    ident = AR.alloc([128, 128], BF16)
    gpre_t = AR.alloc([128, 8], F32)
    gcat_t = AR.alloc([128, 8], F32)
    rstd_f = AR.alloc([128, 64], F32)
    ssf = AR.alloc([128, 64], F32)
    sqf = AR.alloc([128, 64], F32)
    b_ident, b_g, b_ssf, b_rstdf = Buf(), Buf(), [Buf() for _ in range(64)], Buf()
    S.dma("sp", lambda e: e.dma_start(out=ident, in_=ident_d), "ld0", writes=[b_ident])
    S.dma("sp", lambda e: e.dma_start(out=gpre_t, in_=gpre), "ld0", writes=[b_g])
    S.dma("sp", lambda e: e.dma_start(out=gcat_t, in_=gcat), "ld0", writes=[b_g])
    base_mark = AR.off

    PS = nc.alloc_psum_tensor("ps", [128, 4096], F32)
    def bank(i, n=1):
        return PS[:, i * 512:(i + n) * 512]
    pT = [bank(0).bitcast(BF16).rearrange("p (a b) -> p a b", a=8), bank(6).bitcast(BF16).rearrange("p (a b) -> p a b", a=8)]
    pA = bank(1, 2)
    pS = bank(3, 4)
    pOv = [bank(6), bank(7)]
    b_pT = [Buf(), Buf()]
    b_pA = [Buf(), Buf()]
    b_pSb = [Buf(), Buf()]
    b_pOv = [b_pT[1], Buf()]
    gst = [0]

    wst = AR.alloc([128, 8, 512], F32)
    W_uf = AR.alloc([128, 8, 512], BF16)
    b_wst, b_Wuf = Buf(), Buf()
    S.dma("sp", lambda e: e.dma_start(out=wst, in_=w_in[:, 2048:2560].rearrange("(c p) n -> p c n", p=128)), "ld0", writes=[b_wst])
    for kc in range(8):
        S.op("dve", lambda e, kc=kc: e.tensor_scalar(out=W_uf[:, kc, :], in0=wst[:, kc, :], scalar1=gpre_t[:, kc:kc + 1], scalar2=None, op0=ALU.mult),
             reads=[b_wst, b_g], writes=[b_Wuf])
    NXQ = 3
    xq = [AR.alloc([128, 4, D], F32) for _ in range(NXQ)]
    b_xq = [Buf() for _ in range(NXQ)]
    stA = [AR.alloc([128, 4], F32) for _ in range(NXQ)]
    sqA = [AR.alloc([128, 4], F32) for _ in range(NXQ)]
    rsA = [AR.alloc([128, 4], F32) for _ in range(NXQ)]
    b_stA = [[Buf() for _ in range(4)] for _ in range(NXQ)]
    b_sqA, b_rsA = [Buf() for _ in range(NXQ)], [Buf() for _ in range(NXQ)]
    junk = AR.alloc([128, D], BF16)
    b_junk = Buf()
    hb = [AR.alloc([128, D], BF16) for _ in range(2)]
    b_hb = [Buf(), Buf()]
    hT = [AR.alloc([128, 8, 128], BF16) for _ in range(2)]
    b_hT = [Buf(), Buf()]
    ub = [AR.alloc([128, 4, 128], BF16) for _ in range(2)]
    b_ub = [Buf(), Buf()]
    b_up = [[[] for _ in range(4)] for _ in range(2)]
    b_us = [[Buf()] for _ in range(4)]
    b_uloc = [[] for _ in range(4)]

    quads = []
    for q in range(8):
        r0 = HALO * 128 + q * 512
        quads.append((xh[r0:r0 + 512, :], None, q * 512, b_uloc))
    for sq_ in range(2):
        for q in range(4):
            quads.append((xp[sq_ * SP_LEN + q * 512: sq_ * SP_LEN + (q + 1) * 512, :], u_p[sq_], q * 512, b_up[sq_]))
    if "quads" in dbg:
        quads = [quads[i] for i in dbg["quads"]]
    ntA = 4 * len(quads)

    def load_quad(qi):
        src = quads[qi][0]
        sl = qi % NXQ
        S.dma("sp", lambda e: e.dma_start(out=xq[sl], in_=src.rearrange("(t p) d -> p t d", p=128)), "ldxq%d" % sl, writes=[b_xq[sl]])

    def stats_quad(qi):
        sl = qi % NXQ
        for i in range(4):
            S.op("act", lambda e, i=i: e.activation(out=junk, in_=xq[sl][:, i, :], func=AF.Square, scale=1.0 / 32, accum_out=stA[sl][:, i:i + 1]),
                 reads=[b_xq[sl]], writes=[b_junk, b_stA[sl][i]])
        S.op("act", lambda e: e.activation(out=sqA[sl], in_=stA[sl], func=AF.Sqrt, bias=EPS, scale=1.0), reads=b_stA[sl], writes=[b_sqA[sl]])
        S.op("dve", lambda e: e.reciprocal(out=rsA[sl], in_=sqA[sl]), reads=[b_sqA[sl]], writes=[b_rsA[sl]])

    def a_s1(it):
        qi, i, p = it // 4, it % 4, it % 2
        sl = qi % NXQ
        S.op("dve", lambda e: e.tensor_scalar(out=hb[p], in0=xq[sl][:, i, :], scalar1=rsA[sl][:, i:i + 1], scalar2=None, op0=ALU.mult),
             reads=[b_xq[sl], b_rsA[sl]], writes=[b_hb[p]])

    def a_s2(it):
        p = it % 2
        S.op("pe", [(lambda e, c=c: e.transpose(out=pT[p][:, c, :], in_=hb[p][:, c * 128:(c + 1) * 128], identity=ident)) for c in range(8)],
             reads=[b_hb[p], b_ident], writes=[b_pT[p]])
        S.op("act", lambda e: e.activation(out=hT[p], in_=pT[p], func=AF.Copy), reads=[b_pT[p]], writes=[b_hT[p]])

    def a_s3(it):
        qi, i, p = it // 4, it % 4, it % 2
        _, udst, tok0, b_ud = quads[qi]
        S.op("pe", [(lambda e, c=c: e.matmul(out=pA[:, p * 512:(p + 1) * 512], lhsT=hT[p][:, c, :], rhs=W_uf[:, c, :], start=(c == 0), stop=(c == 7))) for c in range(8)],
             reads=[b_hT[p], b_Wuf], writes=[b_pA[p]])
        S.op("dve", lambda e: e.tensor_copy(out=ub[p].rearrange("p a b -> p (a b)"), in_=pA[:, p * 512:(p + 1) * 512]), reads=[b_pA[p]], writes=[b_ub[p]])
        t0 = tok0 + i * 128
        if udst is None:
            for c_ in range(4):
                nb_ = Buf()
                b_ud[c_].append(nb_)
                S.dma("sp", lambda e, c_=c_: e.dma_start(out=u_loc2[c_][t0:t0 + 128, :], in_=ub[p][:, c_, :]), "stub%d_%d" % (p, c_),
                      reads=[b_ub[p]], writes=[nb_])
        else:
            nb_ = Buf()
            for c_ in range(4):
                b_ud[c_].append(nb_)
            S.dma("sp", lambda e: e.dma_start(out=udst[:, t0:t0 + 128, :].rearrange("c p k -> p c k"), in_=ub[p]), "stub%d" % p,
                  reads=[b_ub[p]], writes=[nb_])

    for qi in range(min(2, len(quads))):
        load_quad(qi)
    for it in range(-2, ntA):
        i1 = it + 2
        if 0 <= i1 < ntA:
            if i1 % 4 == 0:
                qi = i1 // 4
                if qi + 2 < len(quads):
                    load_quad(qi + 2)
                stats_quad(qi)
            a_s1(i1)
        if 0 <= it + 1 < ntA:
            a_s2(it + 1)
        if 0 <= it:
            a_s3(it)
            if it == 31 and "quads" not in dbg:
                for c_ in range(4):
                    S.coll(lambda e, c_=c_: e.collective_compute("AllGather", ALU.bypass, replica_groups=[[0, 1, 2, 3], [4, 5, 6, 7]], ins=[u_loc2[c_]], outs=[u_s2[c_]]),
                           "ccag%d" % c_, reads=b_uloc[c_], writes=b_us[c_])

    S.barrier()
    AR.off = base_mark
    if dbg.get("stop") == "A":
        S.emit()
        return nc

    bf_t = AR.alloc([128, 512], F32)
    bd_t = AR.alloc([128, 2, 128], BF16)
    wfb = AR.alloc([128, 4, 512], BF16)
    Wcs = AR.alloc([128, 4, 2, 512], BF16)
    cs_s = AR.alloc([128, 2, 2, 64], BF16)
    cs_p = AR.alloc([128, 2, 16], BF16)
    Xt = AR.alloc([128, 12288], BF16)
    x_s = Xt.rearrange("p (a b c) -> p a b c", a=128, b=3)
    x_p = Xt[:, 0:6144].rearrange("p (a b c) -> p a b c", a=16, b=3)
    Ut = AR.alloc([128, 128, 128], BF16)
    Tt = AR.alloc([128, 128, 2, 64], BF16)
    ZT = AR.alloc([128, 4, 2, 4096], BF16)
    wfs = ZT.rearrange("p c r t -> p (c r t)")[:, 0:4096].bitcast(F32).rearrange("p (c n) -> p c n", c=4)
    yft = [AR.alloc([128, 512], F32) for _ in range(2)]
    b_U, b_T, b_ZT = Buf(), Buf(), Buf()
    b_yft = [Buf(), Buf()]
    b_yf = [Buf() for _ in range(64)]
    b_tab, b_X, b_wfb, b_Wcs = Buf(), Buf(), Buf(), Buf()
    b_wfs = b_ZT
    for dst, srcd in ((bf_t, bfb), (bd_t, bd_d), (cs_s, cs_s_d), (cs_p, cs_p_d)):
        S.dma("sp", lambda e, dst=dst, srcd=srcd: e.dma_start(out=dst, in_=srcd), "ld0", writes=[b_tab])
    S.dma("sp", lambda e: e.dma_start(out=x_p, in_=x_p_d), "ldX", writes=[b_X])
    S.dma("sp", lambda e: e.dma_start(out=wfs, in_=w_f.rearrange("(c p) n -> p c n", p=128)), "ld0", writes=[b_wfs])
    S.op("dve", lambda e: e.tensor_copy(out=wfb.rearrange("p a b -> p (a b)"), in_=wfs.rearrange("p a b -> p (a b)")), reads=[b_wfs], writes=[b_wfb])
    for cb in range(4):
        for ri in range(2):
            p = (cb * 2 + ri) % 2
            S.op("pe", lambda e, cb=cb, ri=ri, p=p: e.matmul(out=pA[:, p * 512:(p + 1) * 512], lhsT=bd_t[:, ri, :], rhs=wfb[:, cb, :], start=True, stop=True),
                 reads=[b_tab, b_wfb], writes=[b_pA[p]])
            S.op("act", lambda e, cb=cb, ri=ri, p=p: e.activation(out=Wcs[:, cb, ri, :], in_=pA[:, p * 512:(p + 1) * 512], func=AF.Copy),
                 reads=[b_pA[p]], writes=[b_Wcs])

    evac_rr = [0]

    def evac(out, in_, reads, writes):
        evac_rr[0] ^= 1
        if evac_rr[0]:
            S.op("act", lambda e: e.activation(out=out, in_=in_, func=AF.Copy), reads=reads, writes=writes)
        else:
            S.op("dve", lambda e: e.tensor_copy(out=out, in_=in_), reads=reads, writes=writes)

    def fourier(A, halves, nk1h, NK2, udram, b_udram, CS, X, ntok, yf_row0, ssf_col0):
        Tv = Tt if nk1h == 64 else Tt.rearrange("p c r k -> p (c r k)")[:, 0:2 * nk1h * 128].rearrange("p (c r k) -> p c r k", c=128, r=2)
        S.barrier(skip=("ccag",))
        b_Tg = [Buf() for _ in range(128 // (512 // (2 * nk1h)))]
        b_ZTg = {}
        ZTv = ZT if ntok == 4096 else ZT.rearrange("p c r t -> p (c r t)")[:, 0:8 * ntok].rearrange("p (c r t) -> p c r t", c=4, r=2)
        w1 = 2 * nk1h
        G1 = 512 // w1
        w2 = 2 * NK2
        G2 = 512 // w2
        pb = 0
        for cb in range(4):
            S.dma("sp", lambda e, cb=cb: e.dma_start(out=Ut[0:A], in_=(udram[cb] if udram is not None else u_s2[cb]).rearrange("(a b) k -> a b k", b=128)), "ldU",
                  reads=b_udram[cb], writes=[b_U])
            for hf in range(halves):
                for g in range(128 // G1):
                    pb ^= 1
                    ch0 = g * G1
                    S.op("pe", [(lambda e, i=i, ch0=ch0, hf=hf, pb=pb: e.matmul(out=pA[:, pb * 512 + i * w1: pb * 512 + (i + 1) * w1], lhsT=Ut[0:A, :, ch0 + i],
                                                                            rhs=(CS[0:A, hf].rearrange("p r k -> p (r k)") if halves == 2 else CS[0:A].rearrange("p r k -> p (r k)")),
                                                                            start=True, stop=True)) for i in range(G1)],
                         reads=[b_U, b_tab], writes=[b_pA[pb]])
                    evac(Tv[:, ch0:ch0 + G1, :, :],
                         pA[:, pb * 512:(pb + 1) * 512].rearrange("p (c r k) -> p c r k", c=G1, r=2), [b_pA[pb]], [b_Tg[g]])
                for g in range(nk1h // G2):
                    pb ^= 1
                    fl = []
                    for i in range(G2):
                        k1l = g * G2 + i
                        k1a = hf * nk1h + k1l
                        o = pA[:, pb * 512 + i * w2: pb * 512 + (i + 1) * w2]
                        fl.append(lambda e, o=o, k1l=k1l, k1a=k1a: e.matmul(out=o, lhsT=Tv[:, :, 1, k1l], rhs=X[:, k1a, 0:2, :].rearrange("p r k -> p (r k)"), start=True, stop=False))
                        fl.append(lambda e, o=o, k1l=k1l, k1a=k1a: e.matmul(out=o, lhsT=Tv[:, :, 0, k1l], rhs=X[:, k1a, 1:3, :].rearrange("p r k -> p (r k)"), start=False, stop=True))
                    S.op("pe", fl, reads=b_Tg + [b_X], writes=[b_pA[pb]])
                    k1lo = hf * nk1h + g * G2
                    evac((ZTv[:, cb].rearrange("p r (k1 k2) -> p k1 r k2", k1=A) if A == 128 else ZTv[:, cb].rearrange("p r (k2 k1) -> p k1 r k2", k1=A))[:, k1lo:k1lo + G2],
                         pA[:, pb * 512:(pb + 1) * 512].rearrange("p (k1 r k2) -> p k1 r k2", k1=G2, r=2), [b_pA[pb]], [b_ZTg.setdefault((cb, hf, g), Buf())])
        for tt in range(ntok // 128):
            pb ^= 1
            fl = []
            for cb in range(4):
                for ri in range(2):
                    fl.append(lambda e, cb=cb, ri=ri, pb=pb, tt=tt: e.matmul(out=pA[:, pb * 512:(pb + 1) * 512], lhsT=(ZTv[:, cb, ri].rearrange("p (k1 k2) -> p k1 k2", k1=A)[:, :, tt] if A == 128 else ZTv[:, cb, ri, tt * 128:(tt + 1) * 128]), rhs=Wcs[:, cb, ri, :],
                                                                       start=(cb == 0 and ri == 0), stop=(cb == 3 and ri == 1)))
            S.op("pe", fl, reads=[b_ZT, b_Wcs] + list(b_ZTg.values()), writes=[b_pA[pb]])
            sl = tt % 2
            S.op("dve", lambda e, pb=pb, sl=sl: e.tensor_tensor(out=yft[sl], in0=pA[:, pb * 512:(pb + 1) * 512], in1=bf_t, op=ALU.add),
                 reads=[b_pA[pb], b_tab], writes=[b_yft[sl]])
            col = ssf_col0 + tt
            S.op("act", lambda e, sl=sl, col=col: e.activation(out=junkF, in_=yft[sl], func=AF.Square, scale=float(1.0 / np.sqrt(512.0)), accum_out=ssf[:, col:col + 1]),
                 reads=[b_yft[sl]], writes=[b_junk, b_ssf[col]])
            r0 = yf_row0 + tt * 128
            S.dma("sp", lambda e, sl=sl, r0=r0: e.dma_start(out=yf_all[r0:r0 + 128, :], in_=yft[sl]), "styf%d" % sl, reads=[b_yft[sl]], writes=[b_yf[r0 // 128]])

    junkF = AR.alloc([128, 512], F32)
    fsel = dbg.get("fourier", [0, 1, 2])
    if 0 in fsel:
        fourier(16, 1, 16, 128, u_p[0], b_up[0], cs_p, x_p, 2048, 0, 0)
    if 1 in fsel:
        fourier(16, 1, 16, 128, u_p[1], b_up[1], cs_p, x_p, 2048, 2048, 16)
    if 2 in fsel:
        S.dma("sp", lambda e: e.dma_start(out=x_s, in_=x_s_d), "ldX", writes=[b_X])
        fourier(128, 2, 64, 32, None, b_us, cs_s, x_s, 4096, 4096, 32)
    S.op("act", lambda e: e.activation(out=sqf, in_=ssf, func=AF.Sqrt, bias=EPS, scale=1.0), reads=b_ssf, writes=[b_rstdf])
    S.op("dve", lambda e: e.reciprocal(out=rstd_f, in_=sqf), reads=[b_rstdf], writes=[b_rstdf])

    S.barrier()
    AR.off = base_mark
    if dbg.get("stop") == "F":
        S.emit()
        return nc

    W_r = AR.alloc([128, 8, 2560], BF16)
    W_o = AR.alloc([128, 8, D], BF16)
    E = AR.alloc([128, 7, 8, 128], BF16)
    EBn = AR.alloc([128, 5, 8, 128], BF16)
    mskn = AR.alloc([128, 5, 128], BF16)
    msk = [AR.alloc([128, 6, 128], BF16) for _ in range(2)]
    gpost_t = AR.alloc([128, D], F32)
    prep_mark = AR.off
    wstB = [AR.alloc([128, 3072], F32) for _ in range(2)]
    b_wstB = [Buf(), Buf()]
    b_Wr, b_Wo = Buf(), Buf()
    for kc in range(8):
        sl = kc % 2
        S.dma("sp", lambda e, kc=kc, sl=sl: e.dma_start(out=wstB[sl], in_=w_in[kc * 128:(kc + 1) * 128, :]), "ldw%d" % sl, writes=[b_wstB[sl]])
        S.op("act", lambda e, kc=kc, sl=sl: e.activation(out=W_r[:, kc, 0:2048], in_=wstB[sl][:, 0:2048], func=AF.Copy, scale=gpre_t[:, kc:kc + 1]),
             reads=[b_wstB[sl], b_g], writes=[b_Wr])
        S.op("dve", lambda e, kc=kc, sl=sl: e.tensor_scalar(out=W_r[:, kc, 2048:2560], in0=wstB[sl][:, 2560:3072], scalar1=gpre_t[:, kc:kc + 1], scalar2=None, op0=ALU.mult),
             reads=[b_wstB[sl], b_g], writes=[b_Wr])
    for kc in range(8):
        sl = kc % 2
        S.dma("sp", lambda e, kc=kc, sl=sl: e.dma_start(out=wstB[sl][:, 0:D], in_=w_out[kc * 128:(kc + 1) * 128, :]), "ldw%d" % sl, writes=[b_wstB[sl]])
        S.op("dve", lambda e, kc=kc, sl=sl: e.tensor_scalar(out=W_o[:, kc, :], in0=wstB[sl][:, 0:D], scalar1=gcat_t[:, kc:kc + 1], scalar2=0.5, op0=ALU.mult, op1=ALU.mult),
             reads=[b_wstB[sl], b_g], writes=[b_Wo])
    b_E, b_EBn, b_mskn, b_msk, b_gpost = Buf(), Buf(), Buf(), [Buf(), Buf()], Buf()
    S.dma("sp", lambda e: e.dma_start(out=mskn, in_=masks_d[:, 0:5, :]), "ld0", writes=[b_mskn])
    S.dma("sp", lambda e: e.dma_start(out=gpost_t, in_=gpost), "ld0", writes=[b_gpost])
    for c in range(7):
        sl = c % 2
        st_ = wstB[sl][:, 0:1024].rearrange("p (h q) -> p h q", h=8)
        S.dma("sp", lambda e, c=c, st_=st_: e.dma_start(out=st_, in_=biasB[:, c]), "ldw%d" % sl, writes=[b_wstB[sl]])
        S.op("act", lambda e, c=c, st_=st_: e.activation(out=E[:, c], in_=st_, func=AF.Exp), reads=[b_wstB[sl]], writes=[b_E])
    for c in range(5):
        S.op("dve", lambda e, c=c: e.tensor_tensor(out=EBn[:, c], in0=E[:, c + 1], in1=mskn[:, c:c + 1, :].to_broadcast([128, 8, 128]), op=ALU.mult),
             reads=[b_E, b_mskn], writes=[b_EBn])

    S.barrier()
    AR.off = prep_mark
    NXR = 3
    xr = [AR.alloc([128, D], F32) for _ in range(NXR)]
    b_xr = [Buf() for _ in range(NXR)]
    xres = [AR.alloc([128, D], F32) for _ in range(2)]
    b_xres = [Buf(), Buf()]
    hbB = AR.alloc([128, D], BF16)
    hTB = AR.alloc([128, 8, 128], BF16)
    qk_tm = AR.alloc([128, D], BF16)
    NQ = 6
    qT = [AR.alloc([128, 4, 128], BF16) for _ in range(NQ)]
    NR = 8
    kT = AR.alloc([128, 4, NR, 128], BF16)
    V1 = AR.alloc([128, NR, 8, 65], BF16)
    NG = 7
    gz = [AR.alloc([128, D], BF16) for _ in range(NG)]
    th = AR.alloc([128, 512], F32)
    Pexp = [AR.alloc([128, 6, 128], BF16) for _ in range(2)]
    Pm = [AR.alloc([128, 6, 128], BF16) for _ in range(2)]
    rden = [AR.alloc([128, 8], F32) for _ in range(2)]
    ya = [AR.alloc([128, 8, 64], F32) for _ in range(2)]
    mixed = AR.alloc([128, D], BF16)
    mT = AR.alloc([128, 8, 128], BF16)
    NO = 2
    o_sb = [AR.alloc([128, D], F32) for _ in range(NO)]
    yfr = [AR.alloc([128, 512], F32) for _ in range(2)]
    res1 = AR.alloc([128, D], F32)
    b_res1 = Buf()
    NST = 4
    stB = [AR.alloc([128, 4], F32) for _ in range(NST)]
    sqB = [AR.alloc([128, 4], F32) for _ in range(NST)]
    rsB = [AR.alloc([128, 4], F32) for _ in range(NST)]
    junkB = AR.alloc([128, D], BF16)
    b_hbB, b_hTB, b_qktm, b_th = Buf(), Buf(), Buf(), Buf()
    b_qT = [Buf() for _ in range(NQ)]
    b_kT = [Buf() for _ in range(NR)]
    b_V1 = [Buf() for _ in range(NR)]
    b_gz = [Buf() for _ in range(NG)]
    b_Pexp, b_Pm = [Buf(), Buf()], [Buf(), Buf()]
    b_rden, b_ya = [Buf(), Buf()], [Buf(), Buf()]
    b_mixed, b_mT = Buf(), Buf()
    b_osb = [Buf() for _ in range(NO)]
    b_yfr = [Buf(), Buf()]
    b_stB = [[Buf() for _ in range(4)] for _ in range(NST)]
    b_sqB = [Buf() for _ in range(NST)]
    b_rsB = [Buf() for _ in range(NST)]
    b_junkB = Buf()
    b_out = Buf()
    pTi = pT[0]
    b_pTi = b_pT[0]
    pTo = pTi
    b_pTo = b_pTi
    pO = bank(7)
    b_pO = Buf()
    gctr = [0]

    S.op("dve", lambda e: e.memset(V1.rearrange("p r h d -> p (r h d)"), 1.0), writes=b_V1)
    for i in range(NST):
        S.op("pool", lambda e, i=i: e.memset(stB[i], 1.0), writes=b_stB[i])

    segs = []
    for sq_ in range(2):
        segs.append(dict(kind="p", n=16, lo=0, hi=15, x=xp[sq_ * SP_LEN:(sq_ + 1) * SP_LEN, :], xoff=0,
                         out=yp[sq_ * SP_LEN:(sq_ + 1) * SP_LEN, :], yf0=sq_ * 16))
    segs.append(dict(kind="s", n=32, lo=-HALO, hi=31 + HALO, x=xh, xoff=HALO, out=ys, yf0=32))
    if "segs" in dbg:
        segs = [segs[i] for i in dbg["segs"]]

    def xtile(seg, t):
        r0 = (t + seg["xoff"]) * 128
        return seg["x"][r0:r0 + 128, :]

    def interleave(gens):
        gens = [g for g in gens if g is not None]
        while gens:
            for g in list(gens):
                try:
                    next(g)
                except StopIteration:
                    gens.remove(g)

    def run_segment(seg):
        n, lo, hi, kind = seg["n"], seg["lo"], seg["hi"], seg["kind"]

        def slot(t):
            return t % NR

        def chunks_of(t):
            key = (kind, t)
            if key in SPECIAL:
                return SPECIAL[key], key
            return NORMAL_CH, None

        def load_x(t):
            sl = t % NXR
            S.dma("sp", lambda e, sl=sl, t=t: e.dma_start(out=xr[sl], in_=xtile(seg, t)), "ldx%d" % sl, writes=[b_xr[sl]])

        def sq_pre(t, stslot, col):
            sl = t % NXR
            S.op("act", lambda e: e.activation(out=junkB, in_=xr[sl], func=AF.Square, scale=1.0 / 32, accum_out=stB[stslot][:, col:col + 1]),
                 reads=[b_xr[sl]], writes=[b_junkB, b_stB[stslot][col]])

        def sqrt_batch(stslot):
            S.op("act", lambda e: e.activation(out=sqB[stslot], in_=stB[stslot], func=AF.Sqrt, bias=EPS, scale=1.0), reads=b_stB[stslot], writes=[b_sqB[stslot]])
            S.op("dve", lambda e: e.reciprocal(out=rsB[stslot], in_=sqB[stslot]), reads=[b_sqB[stslot]], writes=[b_rsB[stslot]])

        def stage_in(t, rs_slot):
            kv_only = (t < 0 or t >= n)
            sl = t % NXR
            ks = slot(t)
            S.op("dve", lambda e: e.tensor_scalar(out=hbB, in0=xr[sl], scalar1=rsB[rs_slot][:, 2:3], scalar2=None, op0=ALU.mult),
                 reads=[b_xr[sl], b_rsB[rs_slot]], writes=[b_hbB])
            yield
            S.op("pe", [(lambda e, c=c: e.transpose(out=pTi[:, c, :], in_=hbB[:, c * 128:(c + 1) * 128], identity=ident)) for c in range(8)],
                 reads=[b_hbB, b_ident], writes=[b_pTi])
            S.op("act", lambda e: e.activation(out=hTB, in_=pTi, func=AF.Copy), reads=[b_pTi], writes=[b_hTB])
            yield
            cgs = [1, 2] if kv_only else [1, 0, 2, 3, 4]
            for cgi, cg in enumerate(cgs):
                p = cgi % 2
                pa = pA[:, p * 512:(p + 1) * 512]
                S.op("pe", [(lambda e, c=c, cg=cg, pa=pa: e.matmul(out=pa, lhsT=hTB[:, c, :], rhs=W_r[:, c, cg * 512:(cg + 1) * 512], start=(c == 0), stop=(c == 7))) for c in range(8)],
                     reads=[b_hTB, b_Wr], writes=[b_pA[p]])
                if cg == 0:
                    S.op("act", lambda e, pa=pa: e.activation(out=qk_tm[:, 0:512], in_=pa, func=AF.Copy), reads=[b_pA[p]], writes=[b_qktm])
                elif cg == 1:
                    S.op("dve", lambda e, pa=pa: e.tensor_copy(out=qk_tm[:, 512:1024], in_=pa), reads=[b_pA[p]], writes=[b_qktm])
                elif cg == 2:
                    S.op("dve", lambda e, pa=pa: e.tensor_copy(out=V1[:, ks, :, 0:64], in_=pa.rearrange("p (h d) -> p h d", h=8)), reads=[b_pA[p]], writes=[b_V1[ks]])
                else:
                    go = (cg - 3) * 512
                    g = gz[t % NG]
                    S.op("act", lambda e, pa=pa: e.activation(out=th, in_=pa, func=AF.Tanh, scale=0.5), reads=[b_pA[p]], writes=[b_th])
                    S.op("dve", lambda e, pa=pa, g=g, go=go: e.scalar_tensor_tensor(out=g[:, go:go + 512], in0=th, scalar=1.0, in1=pa, op0=ALU.add, op1=ALU.mult),
                         reads=[b_th, b_pA[p]], writes=[b_gz[t % NG]])
                yield
                if cg == 0 or (kv_only and cg == 1):
                    c0 = 4 if kv_only else 0
                    S.op("pe", [(lambda e, c=c: e.transpose(out=pTi[:, c, :], in_=qk_tm[:, c * 128:(c + 1) * 128], identity=ident)) for c in range(c0, 8)],
                         reads=[b_qktm, b_ident], writes=[b_pTi])
                    if not kv_only:
                        S.op("act", lambda e: e.activation(out=qT[t % NQ], in_=pTi[:, 0:4, :], func=AF.Copy), reads=[b_pTi], writes=[b_qT[t % NQ]])
                    S.op("act", lambda e: e.activation(out=kT[:, :, ks, :], in_=pTi[:, 4:8, :], func=AF.Copy), reads=[b_pTi], writes=[b_kT[ks]])
                    yield

        def stage_att(t, stslot):
            chs, key = chunks_of(t)
            chs = [c for c in chs if lo <= t + c <= hi][:dbg.get("maxch", 6)]
            nch = len(chs)
            ms = 0
            if key is not None:
                full = SPECIAL[key]
                ms = (gctr[0]) % 2
                gctr[0] += 1
                i0 = full.index(chs[0])
                S.dma("sp", lambda e: e.dma_start(out=msk[ms][:, 0:nch, :], in_=masks_d[:, MSLOT[key] + i0: MSLOT[key] + i0 + nch, :]), "ldm%d" % ms, writes=[b_msk[ms]])
            yb = t % 2

            def qk(h):
                hp, po, sb = h // 2, 64 * (h % 2), h % 2
                S.op("pe", [(lambda e, ci=ci, c=c: e.matmul(out=pS[:, (sb * 8 + ci) * 128:(sb * 8 + ci + 1) * 128], lhsT=kT[po:po + 64, hp, slot(t + c), :],
                                                          rhs=qT[t % NQ][po:po + 64, hp, :], start=True, stop=True)) for ci, c in enumerate(chs)],
                     reads=[b_qT[t % NQ]] + [b_kT[slot(t + c)] for c in chs], writes=[b_pSb[sb]])

            def sm(h):
                sb = h % 2
                pe_, pm_ = Pexp[sb], Pm[sb]
                S.op("act", lambda e: e.activation(out=pe_[:, 0:nch, :].rearrange("p a b -> p (a b)"), in_=pS[:, sb * 1024: sb * 1024 + nch * 128], func=AF.Exp, scale=0.125),
                     reads=[b_pSb[sb]], writes=[b_Pexp[sb]])
                yield
                if key is None:
                    S.op("dve", lambda e: e.tensor_tensor(out=pm_[:, 0:5, :], in0=pe_[:, 0:5, :], in1=EBn[:, :, h, :], op=ALU.mult),
                         reads=[b_Pexp[sb], b_EBn], writes=[b_Pm[sb]])
                else:
                    e0 = chs[0] + 3
                    S.op("dve", lambda e: e.tensor_tensor(out=pm_[:, 0:nch, :], in0=pe_[:, 0:nch, :], in1=E[:, e0:e0 + nch, h, :], op=ALU.mult),
                         reads=[b_Pexp[sb], b_E], writes=[b_Pm[sb]])
                    yield
                    S.op("dve", lambda e: e.tensor_tensor(out=pm_[:, 0:nch, :], in0=pm_[:, 0:nch, :], in1=msk[ms][:, 0:nch, :], op=ALU.mult),
                         reads=[b_Pm[sb], b_msk[ms]], writes=[b_Pm[sb]])
                yield

            def pv(h):
                sb = h % 2
                pm_ = Pm[sb]
                ob = pO[:, (h % 4) * 65:(h % 4) * 65 + 65]
                S.op("pe", [(lambda e, ci=ci, c=c: e.matmul(out=ob, lhsT=pm_[:, ci, :], rhs=V1[:, slot(t + c), h, :], start=(ci == 0), stop=(ci == nch - 1)))
                            for ci, c in enumerate(chs)],
                     reads=[b_Pm[sb]] + [b_V1[slot(t + c)] for c in chs], writes=[b_pO])

            def norm(hb2):
                pv_ = pO[:, 0:260].rearrange("p (h d) -> p h d", d=65)
                S.op("dve", lambda e: e.reciprocal(out=rden[yb][:, hb2 * 4:(hb2 + 1) * 4], in_=pv_[:, :, 64]), reads=[b_pO], writes=[b_rden[yb]])
                yield
                S.op("dve", lambda e: e.tensor_tensor(out=ya[yb][:, hb2 * 4:(hb2 + 1) * 4, :], in0=pv_[:, :, 0:64],
                                                      in1=rden[yb][:, hb2 * 4:(hb2 + 1) * 4].unsqueeze(2).to_broadcast([128, 4, 64]), op=ALU.mult),
                     reads=[b_pO, b_rden[yb]], writes=[b_ya[yb]])
                yield

            qk(0)
            yield
            for h in range(8):
                if h + 1 < 8:
                    qk(h + 1)
                    yield
                yield from sm(h)
                pv(h)
                yield
                if h % 4 == 3:
                    yield from norm(h // 4)
            S.op("act", lambda e: e.activation(out=junkB[:, 0:512], in_=ya[yb].rearrange("p h d -> p (h d)"), func=AF.Square, scale=float(1.0 / np.sqrt(512.0)),
                                               accum_out=stB[stslot][:, 0:1]), reads=[b_ya[yb]], writes=[b_junkB, b_stB[stslot][0]])
            yield

        def prefetch_out(t):
            yfs = t % 2
            r0 = (seg["yf0"] + t) * 128
            S.dma("sp", lambda e: e.dma_start(out=yfr[yfs], in_=yf_all[r0:r0 + 128, :]), "ldyf%d" % yfs, reads=[b_yf[seg["yf0"] + t]], writes=[b_yfr[yfs]])

        def prefetch_final(t):
            xs_ = t % 2
            S.dma("sp", lambda e: e.dma_start(out=xres[xs_], in_=xtile(seg, t)), "ldxr%d" % xs_, writes=[b_xres[xs_]])

        def stage_out(t, rs_slot, stslot):
            g = gz[t % NG]
            yfs = t % 2
            yb = t % 2
            S.op("dve", lambda e: e.scalar_tensor_tensor(out=mixed[:, 0:512], in0=ya[yb].rearrange("p h d -> p (h d)"), scalar=rsB[rs_slot][:, 0:1], in1=g[:, 0:512], op0=ALU.mult, op1=ALU.mult),
                 reads=[b_ya[yb], b_rsB[rs_slot], b_gz[t % NG]], writes=[b_mixed])
            yield
            col = seg["yf0"] + t
            S.op("dve", lambda e: e.scalar_tensor_tensor(out=mixed[:, 512:1024], in0=yfr[yfs], scalar=rstd_f[:, col:col + 1], in1=g[:, 512:1024], op0=ALU.mult, op1=ALU.mult),
                 reads=[b_yfr[yfs], b_rstdf, b_gz[t % NG]], writes=[b_mixed])
            yield
            S.op("pe", [(lambda e, c=c: e.transpose(out=pTo[:, c, :], in_=mixed[:, c * 128:(c + 1) * 128], identity=ident)) for c in range(8)],
                 reads=[b_mixed, b_ident], writes=[b_pTo])
            S.op("act", lambda e: e.activation(out=mT, in_=pTo, func=AF.Copy), reads=[b_pTo], writes=[b_mT])
            yield
            for hf in range(2):
                S.op("pe", [(lambda e, c=c, hf=hf: e.matmul(out=pA[:, hf * 512:(hf + 1) * 512], lhsT=mT[:, c, :], rhs=W_o[:, c, hf * 512:(hf + 1) * 512], start=(c == 0), stop=(c == 7))) for c in range(8)],
                     reads=[b_mT, b_Wo], writes=[b_pA[hf]])
            os_ = t % NO
            S.op("act", lambda e: e.activation(out=junkB, in_=pA, func=AF.Square, scale=1.0 / 32, accum_out=stB[stslot][:, 1:2]),
                 reads=[b_pA[0], b_pA[1]], writes=[b_junkB, b_stB[stslot][1]])
            S.op("act", lambda e: e.activation(out=o_sb[os_], in_=pA, func=AF.Copy), reads=[b_pA[0], b_pA[1]], writes=[b_osb[os_]])
            yield

        def stage_final(t, rs_slot):
            os_ = t % NO
            xs_ = t % 2
            fp = dbg.get("fparts", 7)
            if fp & 2:
                S.op("dve", lambda e: e.scalar_tensor_tensor(out=res1, in0=o_sb[os_], scalar=rsB[rs_slot][:, 1:2], in1=gpost_t, op0=ALU.mult, op1=ALU.mult),
                     reads=[b_osb[os_], b_rsB[rs_slot], b_gpost], writes=[b_res1])
                yield
                S.op("dve", lambda e: e.tensor_tensor(out=xres[xs_], in0=res1, in1=xres[xs_], op=ALU.add), reads=[b_res1, b_xres[xs_]], writes=[b_xres[xs_]])
                yield
            r0 = t * 128
            if fp & 4:
                S.dma("sp", lambda e: e.dma_start(out=seg["out"][r0:r0 + 128, :], in_=xres[xs_]), "sto%d" % xs_, reads=[b_xres[xs_]], writes=[b_out])
            yield

        LAG = 4
        tiles_in = list(range(lo, hi + 1))
        own = lambda t: (t is not None and 0 <= t < n)
        load_x(tiles_in[0])
        load_x(tiles_in[1])
        prev = gst[0] % NST
        gst[0] += 1
        sq_pre(tiles_in[0], prev, 2)
        sqrt_batch(prev)
        nsteps = len(tiles_in) + LAG + 2
        seq = tiles_in + [None] * (LAG + 2)
        for step in range(min(nsteps, dbg.get("max_steps", 10 ** 9))):
            cur = gst[0] % NST
            gst[0] += 1
            tin = seq[step]
            ta = seq[step - LAG] if step - LAG >= 0 else None
            to = seq[step - LAG - 1] if step - LAG - 1 >= 0 else None
            tf = seq[step - LAG - 2] if step - LAG - 2 >= 0 else None
            if tin is not None and tin + 2 <= hi:
                load_x(tin + 2)
            if own(to):
                prefetch_out(to)
            if own(tf) and (dbg.get("fparts", 7) & 1):
                prefetch_final(tf)
            gens = []
            stg = dbg.get("stages", "IAOF")
            if tin is not None and "I" in stg:
                gens.append(stage_in(tin, prev))
            if own(ta) and "A" in stg:
                gens.append(stage_att(ta, cur))
            if own(to) and "O" in stg:
                gens.append(stage_out(to, prev, cur))
            if own(tf) and "F" in stg:
                gens.append(stage_final(tf, prev))
            if dbg.get("serial"):
                for g_ in gens:
                    interleave([g_])
            else:
                interleave(gens)
            if tin is not None and tin + 1 <= hi:
                sq_pre(tin + 1, cur, 2)
            sqrt_batch(cur)
            prev = cur

    for seg_ in segs:
        run_segment(seg_)

    S.barrier()
    S.emit()
    return nc


_CACHE = {}


def kernel(x_prompt, x_sample, w_in, rpb, w_fourier, b_fourier, g_pre, g_na, g_f, w_out, g_post):
    f32 = np.float32
    x_prompt = np.asarray(x_prompt, f32)
    x_sample = np.asarray(x_sample, f32)
    w_in = np.asarray(w_in, f32)[0]
    w_out = np.asarray(w_out, f32)[0]
    w_f = np.asarray(w_fourier, f32)[0]
    rpb = np.asarray(rpb, f32)[0]
    bfb = np.ascontiguousarray(np.broadcast_to(np.asarray(b_fourier, f32)[0][None, :], (128, 512)))
    gpre = np.ascontiguousarray(np.asarray(g_pre, f32)[0].reshape(8, 128).T)
    gcat = np.ascontiguousarray(np.concatenate([np.asarray(g_na, f32)[0], np.asarray(g_f, f32)[0]]).reshape(8, 128).T)
    gpost = np.ascontiguousarray(np.broadcast_to(np.asarray(g_post, f32)[0][None, :], (128, D)))
    ri, ci = _bias_index()
    biasB = np.ascontiguousarray(rpb[:, ri, ci].transpose(1, 2, 0, 3))
    if "nc" not in _CACHE:
        _CACHE["nc"] = build_program(None)
    nc = _CACHE["nc"]
    in_maps = []
    for c in range(NCORES):
        b, j = c // 4, c % 4
        tb = _host_tables(c)
        xh = np.zeros(((32 + 2 * HALO) * 128, D), f32)
        lo_t = 4096 * j - HALO * 128
        hi_t = 4096 * (j + 1) + HALO * 128
        s0, s1 = max(lo_t, 0), min(hi_t, SS_LEN)
        xh[s0 - lo_t:s1 - lo_t] = x_sample[b, s0:s1]
        m = dict(xp=np.ascontiguousarray(x_prompt[2 * c:2 * c + 2].reshape(2 * SP_LEN, D)), xh=xh,
                 w_in=w_in, w_out=w_out, w_f=w_f, bfb=bfb, gpre=gpre, gcat=gcat, gpost=gpost, biasB=biasB)
        m.update(tb)
        in_maps.append(m)
    r = run_bass_kernel_spmd(nc, in_maps, core_ids=list(range(NCORES)))
    y_prompt = np.zeros((16, SP_LEN, D), f32)
    y_sample = np.zeros((2, SS_LEN, D), f32)
    for c in range(NCORES):
        b, j = c // 4, c % 4
        y_prompt[2 * c:2 * c + 2] = np.asarray(r.results[c]["yp"], f32).reshape(2, SP_LEN, D)
        y_sample[b, 4096 * j:4096 * (j + 1)] = np.asarray(r.results[c]["ys"], f32)
    return (y_prompt, y_sample)
```

```python
import contextlib
import numpy as np
import ml_dtypes
import concourse.bass as bass
import concourse.mybir as mybir
from concourse.bass_utils import run_bass_kernel_spmd

F32 = mybir.dt.float32
BF16 = mybir.dt.bfloat16
U8 = mybir.dt.uint8
ALU = mybir.AluOpType
AF = mybir.ActivationFunctionType

D = 1024
EPS = 1e-6
NCORES = 8
SP_LEN = 2048
SS_LEN = 16384
OWN_S = 4096
HALO = 2


class Buf:
    __slots__ = ("name", "w", "r")

    def __init__(self, name=""):
        self.name = name
        self.w = None
        self.r = []


class Sched:
    ENGS = ("pe", "act", "dve", "pool", "sp")

    def __init__(self, nc):
        self.nc = nc
        self.ops = {e: [] for e in self.ENGS}
        self.cnt = {}
        self.seen = {e: {} for e in self.ENGS}
        self.sems = {}

    def sem(self, name):
        if name not in self.sems:
            self.sems[name] = None
            self.cnt[name] = 0
        return name

    def _deps(self, eng, reads, writes):
        deps = {}
        for b in reads:
            if b.w is not None:
                s, v = b.w
                deps[s] = max(deps.get(s, 0), v)
        for b in writes:
            for (s, v) in b.r:
                deps[s] = max(deps.get(s, 0), v)
            if b.w is not None:
                s, v = b.w
                deps[s] = max(deps.get(s, 0), v)
        waits = []
        seen = self.seen[eng]
        for s, v in deps.items():
            if eng == "pe" and s == "pe":
                continue
            if seen.get(s, 0) >= v:
                continue
            seen[s] = v
            waits.append((s, v))
        return waits

    def _commit(self, tok, reads, writes):
        for b in writes:
            b.w = tok
            b.r = []
        for b in reads:
            b.r.append(tok)

    def op(self, eng, fns, reads=(), writes=()):
        if callable(fns):
            fns = [fns]
        waits = self._deps(eng, reads, writes)
        s = self.sem(eng)
        self.cnt[s] += 1
        tok = (s, self.cnt[s])
        self.ops[eng].append((waits, fns, (s, 1)))
        self._commit(tok, reads, writes)
        return tok

    def dma(self, q, fn, semname, reads=(), writes=()):
        if semname == "ld0":
            self._uniq = getattr(self, "_uniq", 0) + 1
            semname = "ld0_%d" % self._uniq
        waits = self._deps(q, reads, writes)
        s = self.sem(semname)
        self.cnt[s] += 16
        tok = (s, self.cnt[s])
        self.ops[q].append((waits, [fn], (s, 16)))
        self._commit(tok, reads, writes)
        return tok

    def coll(self, fn, semname, reads=(), writes=()):
        waits = self._deps("pool", reads, writes)
        s = self.sem(semname)
        self.cnt[s] += 1
        tok = (s, self.cnt[s])
        self.ops["pool"].append((waits, [fn], (s, 1)))
        self._commit(tok, reads, writes)
        return tok

    def barrier(self, skip=()):
        toks = [(s, v) for s, v in self.cnt.items() if v > 0 and not any(s.startswith(p) for p in skip)]
        for e in self.ENGS:
            waits = []
            for (s, v) in toks:
                if s == e:
                    continue
                if self.seen[e].get(s, 0) < v:
                    self.seen[e][s] = v
                    waits.append((s, v))
            if waits:
                self.ops[e].append((waits, [], None))

    def emit(self):
        nc = self.nc
        with contextlib.ExitStack() as st:
            for name in self.sems:
                self.sems[name] = st.enter_context(nc.semaphore("s_" + name))
            block = st.enter_context(nc.Block())
            sems = self.sems

            def run(e, lst):
                for waits, fns, inc in lst:
                    for (s, v) in waits:
                        e.wait_ge(sems[s], v)
                    last = None
                    for f in fns:
                        last = f(e)
                    if inc is not None:
                        last.then_inc(sems[inc[0]], inc[1])

            block.tensor(lambda e: run(e, self.ops["pe"]))
            block.scalar(lambda e: run(e, self.ops["act"]))
            block.vector(lambda e: run(e, self.ops["dve"]))
            block.gpsimd(lambda e: run(e, self.ops["pool"]))
            block.sync(lambda e: run(e, self.ops["sp"]))


class Arena:
    def __init__(self, nc, nbytes):
        self.t = nc.alloc_sbuf_tensor("arena", [128, nbytes], U8)
        self.n = nbytes
        self.off = 0

    def alloc(self, shape, dt):
        nb = 2 if dt == BF16 else 4
        n = int(np.prod(shape[1:])) * nb
        off = (self.off + 63) // 64 * 64
        assert off + n <= self.n, ("SBUF arena overflow", off, n, self.n)
        self.off = off + n
        ap = self.t[:, off:off + n].bitcast(dt)
        if len(shape) == 3:
            ap = ap.rearrange("p (a b) -> p a b", a=shape[1])
        elif len(shape) == 4:
            ap = ap.rearrange("p (a b c) -> p a b c", a=shape[1], b=shape[2])
        return ap


def _valid(qrow, krow, qcol, kcol, rows_total):
    rs = min(max(qrow - 4, 0), rows_total - 8)
    ws = min(max(qcol - 8, 0), 48)
    return (0 <= krow < rows_total) and (rs <= krow < rs + 8) and (ws <= kcol < ws + 16)


def _mask(qtile_global, chunks, rows_total):
    m = np.zeros((128, len(chunks), 128), np.float32)
    kr2 = np.arange(128) // 64
    kc = np.arange(128) % 64
    for ci, c in enumerate(chunks):
        for q in range(128):
            qrow = 2 * qtile_global + q // 64
            qcol = q % 64
            rs = min(max(qrow - 4, 0), rows_total - 8)
            ws = min(max(qcol - 8, 0), 48)
            krow = 2 * (qtile_global + c) + kr2
            ok = (krow >= 0) & (krow < rows_total) & (krow >= rs) & (krow < rs + 8) & (kc >= ws) & (kc < ws + 16)
            m[:, ci, q] = ok
    return m


NORMAL_CH = [-2, -1, 0, 1, 2]
SPECIAL = {
    ("p", 0): [0, 1, 2, 3], ("p", 1): [-1, 0, 1, 2], ("p", 14): [-2, -1, 0, 1], ("p", 15): [-3, -2, -1, 0],
    ("s", 0): [-2, -1, 0, 1, 2, 3], ("s", 1): [-2, -1, 0, 1, 2],
    ("s", 30): [-2, -1, 0, 1, 2], ("s", 31): [-3, -2, -1, 0, 1, 2],
}
SPECIAL_KEYS = list(SPECIAL.keys())
MSLOT = {}
_o = 5
for _k in SPECIAL_KEYS:
    MSLOT[_k] = _o
    _o += len(SPECIAL[_k])
NMASK = _o


def _host_tables(core):
    j = core % 4
    masks = np.zeros((128, NMASK, 128), np.float32)
    masks[:, 0:5, :] = _mask(8, NORMAL_CH, 32)
    for k in SPECIAL_KEYS:
        ch = SPECIAL[k]
        if k[0] == "p":
            m = _mask(k[1], ch, 32)
        else:
            m = _mask(32 * j + k[1], ch, 256)
        masks[:, MSLOT[k]:MSLOT[k] + len(ch), :] = m
    b = np.arange(128)[:, None, None]
    a = np.arange(128)[:, None]
    k1 = np.arange(128)[None, :]
    ang = 2 * np.pi * ((a * k1) % 128) / 128
    cs = np.stack([np.cos(ang), -np.sin(ang)], axis=1) / np.sqrt(SS_LEN)
    cs_s = cs.reshape(128, 2, 2, 64).transpose(0, 2, 1, 3)
    a16 = np.arange(16)[:, None]
    k16 = np.arange(16)[None, :]
    ang = 2 * np.pi * ((a16 * k16) % 16) / 16
    cs_p = np.zeros((128, 2, 16), np.float64)
    cs_p[0:16] = np.stack([np.cos(ang), -np.sin(ang)], axis=1) / np.sqrt(SP_LEN)
    k1s = np.arange(128)[None, :, None]
    k2l = np.arange(32)[None, None, :]
    k = k1s + 128 * (32 * j + k2l)
    th = 2 * np.pi * ((k * b) % SS_LEN) / SS_LEN
    xs = np.stack([np.sin(th), np.cos(th), -np.sin(th)], axis=2)
    k1p = np.arange(16)[None, :, None]
    k2p = np.arange(128)[None, None, :]
    k = k1p + 16 * k2p
    th = 2 * np.pi * ((k * b) % SP_LEN) / SP_LEN
    xp = np.stack([np.sin(th), np.cos(th), -np.sin(th)], axis=2)
    l = np.arange(64)[:, None]
    c = np.arange(64)[None, :]
    th = 2 * np.pi * ((l * c) % 64) / 64
    bd = np.zeros((128, 2, 128), np.float64)
    for g in range(2):
        bd[64 * g:64 * g + 64, 0, 64 * g:64 * g + 64] = np.cos(th) / 8
        bd[64 * g:64 * g + 64, 1, 64 * g:64 * g + 64] = np.sin(th) / 8
    bf = ml_dtypes.bfloat16
    return dict(
        masks=masks.astype(bf), cs_s=np.ascontiguousarray(cs_s).astype(bf), cs_p=cs_p.astype(bf),
        x_s=np.ascontiguousarray(xs).astype(bf), x_p=np.ascontiguousarray(xp).astype(bf),
        bd=bd.astype(bf), ident=np.eye(128, dtype=np.float32).astype(bf),
    )


def _bias_index():
    key = np.arange(128)[:, None, None]
    c7 = np.arange(7)[None, :, None] - 3
    q = np.arange(128)[None, None, :]
    dr = 2 * c7 + key // 64 - q // 64
    ri = np.clip(dr + 7, 0, 14)
    ci = np.clip(key % 64 - q % 64 + 15, 0, 30)
    return np.broadcast_to(ri, (128, 7, 128)), np.broadcast_to(ci, (128, 7, 128))


def build_program(dbg=None):
    dbg = dbg or {}
    nc = bass.Bass("TRN2", target_bir_lowering=False)
    S = Sched(nc)

    def din(name, shape, dt=F32):
        return nc.dram_tensor(name, list(shape), dt, kind="ExternalInput").ap()

    xp = din("xp", [2 * SP_LEN, D])
    xh = din("xh", [(32 + 2 * HALO) * 128, D])
    w_in = din("w_in", [D, 3072])
    w_out = din("w_out", [D, D])
    w_f = din("w_f", [512, 512])
    bfb = din("bfb", [128, 512])
    gpre = din("gpre", [128, 8])
    gcat = din("gcat", [128, 8])
    gpost = din("gpost", [128, D])
    biasB = din("biasB", [128, 7, 8, 128])
    masks_d = din("masks", [128, NMASK, 128], BF16)
    cs_s_d = din("cs_s", [128, 2, 2, 64], BF16)
    cs_p_d = din("cs_p", [128, 2, 16], BF16)
    x_s_d = din("x_s", [128, 128, 3, 32], BF16)
    x_p_d = din("x_p", [128, 16, 3, 128], BF16)
    bd_d = din("bd", [128, 2, 128], BF16)
    ident_d = din("ident", [128, 128], BF16)
    yp = nc.dram_tensor("yp", [2 * SP_LEN, D], F32, kind="ExternalOutput").ap()
    ys = nc.dram_tensor("ys", [OWN_S, D], F32, kind="ExternalOutput").ap()
    skind = "ExternalOutput" if dbg else "Internal"
    u_p = nc.dram_tensor("u_p", [2, 4, SP_LEN, 128], BF16, kind=skind).ap()
    u_loc2 = [nc.dram_tensor("u_loc%d" % c_, [OWN_S, 128], BF16, kind="Internal").ap() for c_ in range(4)]
    u_s2 = [nc.dram_tensor("u_s%d" % c_, [SS_LEN, 128], BF16, kind="Internal").ap() for c_ in range(4)]
    yf_all = nc.dram_tensor("yf_all", [8192, 512], F32, kind=skind).ap()

    AR = Arena(nc, 188 * 1024)
    ident = AR.alloc([128, 128], BF16)
    gpre_t = AR.alloc([128, 8], F32)
    gcat_t = AR.alloc([128, 8], F32)
    rstd_f = AR.alloc([128, 64], F32)
    ssf = AR.alloc([128, 64], F32)
    sqf = AR.alloc([128, 64], F32)
    b_ident, b_g, b_ssf, b_rstdf = Buf(), Buf(), [Buf() for _ in range(64)], Buf()
    S.dma("sp", lambda e: e.dma_start(out=ident, in_=ident_d), "ld0", writes=[b_ident])
    S.dma("sp", lambda e: e.dma_start(out=gpre_t, in_=gpre), "ld0", writes=[b_g])
    S.dma("sp", lambda e: e.dma_start(out=gcat_t, in_=gcat), "ld0", writes=[b_g])
    base_mark = AR.off

    PS = nc.alloc_psum_tensor("ps", [128, 4096], F32)
    def bank(i, n=1):
        return PS[:, i * 512:(i + n) * 512]
    pT = [bank(0).bitcast(BF16).rearrange("p (a b) -> p a b", a=8), bank(6).bitcast(BF16).rearrange("p (a b) -> p a b", a=8)]
    pA = bank(1, 2)
    pS = bank(3, 4)
    pOv = [bank(6), bank(7)]
    b_pT = [Buf(), Buf()]
    b_pA = [Buf(), Buf()]
    b_pSb = [Buf(), Buf()]
    b_pOv = [b_pT[1], Buf()]
    gst = [0]

    wst = AR.alloc([128, 8, 512], F32)
    W_uf = AR.alloc([128, 8, 512], BF16)
    b_wst, b_Wuf = Buf(), Buf()
    S.dma("sp", lambda e: e.dma_start(out=wst, in_=w_in[:, 2048:2560].rearrange("(c p) n -> p c n", p=128)), "ld0", writes=[b_wst])
    for kc in range(8):
        S.op("dve", lambda e, kc=kc: e.tensor_scalar(out=W_uf[:, kc, :], in0=wst[:, kc, :], scalar1=gpre_t[:, kc:kc + 1], scalar2=None, op0=ALU.mult),
             reads=[b_wst, b_g], writes=[b_Wuf])
    NXQ = 3
    xq = [AR.alloc([128, 4, D], F32) for _ in range(NXQ)]
    b_xq = [Buf() for _ in range(NXQ)]
    stA = [AR.alloc([128, 4], F32) for _ in range(NXQ)]
    sqA = [AR.alloc([128, 4], F32) for _ in range(NXQ)]
    rsA = [AR.alloc([128, 4], F32) for _ in range(NXQ)]
    b_stA = [[Buf() for _ in range(4)] for _ in range(NXQ)]
    b_sqA, b_rsA = [Buf() for _ in range(NXQ)], [Buf() for _ in range(NXQ)]
    junk = AR.alloc([128, D], BF16)
    b_junk = Buf()
    hb = [AR.alloc([128, D], BF16) for _ in range(2)]
    b_hb = [Buf(), Buf()]
    hT = [AR.alloc([128, 8, 128], BF16) for _ in range(2)]
    b_hT = [Buf(), Buf()]
    ub = [AR.alloc([128, 4, 128], BF16) for _ in range(2)]
    b_ub = [Buf(), Buf()]
    b_up = [[[] for _ in range(4)] for _ in range(2)]
    b_us = [[Buf()] for _ in range(4)]
    b_uloc = [[] for _ in range(4)]

    quads = []
    for q in range(8):
        r0 = HALO * 128 + q * 512
        quads.append((xh[r0:r0 + 512, :], None, q * 512, b_uloc))
    for sq_ in range(2):
        for q in range(4):
            quads.append((xp[sq_ * SP_LEN + q * 512: sq_ * SP_LEN + (q + 1) * 512, :], u_p[sq_], q * 512, b_up[sq_]))
    if "quads" in dbg:
        quads = [quads[i] for i in dbg["quads"]]
    ntA = 4 * len(quads)

    def load_quad(qi):
        src = quads[qi][0]
        sl = qi % NXQ
        S.dma("sp", lambda e: e.dma_start(out=xq[sl], in_=src.rearrange("(t p) d -> p t d", p=128)), "ldxq%d" % sl, writes=[b_xq[sl]])

    def stats_quad(qi):
        sl = qi % NXQ
        for i in range(4):
            S.op("act", lambda e, i=i: e.activation(out=junk, in_=xq[sl][:, i, :], func=AF.Square, scale=1.0 / 32, accum_out=stA[sl][:, i:i + 1]),
                 reads=[b_xq[sl]], writes=[b_junk, b_stA[sl][i]])
        S.op("act", lambda e: e.activation(out=sqA[sl], in_=stA[sl], func=AF.Sqrt, bias=EPS, scale=1.0), reads=b_stA[sl], writes=[b_sqA[sl]])
        S.op("dve", lambda e: e.reciprocal(out=rsA[sl], in_=sqA[sl]), reads=[b_sqA[sl]], writes=[b_rsA[sl]])

    def a_s1(it):
        qi, i, p = it // 4, it % 4, it % 2
        sl = qi % NXQ
        S.op("dve", lambda e: e.tensor_scalar(out=hb[p], in0=xq[sl][:, i, :], scalar1=rsA[sl][:, i:i + 1], scalar2=None, op0=ALU.mult),
             reads=[b_xq[sl], b_rsA[sl]], writes=[b_hb[p]])

    def a_s2(it):
        p = it % 2
        S.op("pe", [(lambda e, c=c: e.transpose(out=pT[p][:, c, :], in_=hb[p][:, c * 128:(c + 1) * 128], identity=ident)) for c in range(8)],
             reads=[b_hb[p], b_ident], writes=[b_pT[p]])
        S.op("act", lambda e: e.activation(out=hT[p], in_=pT[p], func=AF.Copy), reads=[b_pT[p]], writes=[b_hT[p]])

    def a_s3(it):
        qi, i, p = it // 4, it % 4, it % 2
        _, udst, tok0, b_ud = quads[qi]
        S.op("pe", [(lambda e, c=c: e.matmul(out=pA[:, p * 512:(p + 1) * 512], lhsT=hT[p][:, c, :], rhs=W_uf[:, c, :], start=(c == 0), stop=(c == 7))) for c in range(8)],
             reads=[b_hT[p], b_Wuf], writes=[b_pA[p]])
        S.op("dve", lambda e: e.tensor_copy(out=ub[p].rearrange("p a b -> p (a b)"), in_=pA[:, p * 512:(p + 1) * 512]), reads=[b_pA[p]], writes=[b_ub[p]])
        t0 = tok0 + i * 128
        if udst is None:
            for c_ in range(4):
                nb_ = Buf()
                b_ud[c_].append(nb_)
                S.dma("sp", lambda e, c_=c_: e.dma_start(out=u_loc2[c_][t0:t0 + 128, :], in_=ub[p][:, c_, :]), "stub%d_%d" % (p, c_),
                      reads=[b_ub[p]], writes=[nb_])
        else:
            nb_ = Buf()
            for c_ in range(4):
                b_ud[c_].append(nb_)
            S.dma("sp", lambda e: e.dma_start(out=udst[:, t0:t0 + 128, :].rearrange("c p k -> p c k"), in_=ub[p]), "stub%d" % p,
                  reads=[b_ub[p]], writes=[nb_])

    for qi in range(min(2, len(quads))):
        load_quad(qi)
    for it in range(-2, ntA):
        i1 = it + 2
        if 0 <= i1 < ntA:
            if i1 % 4 == 0:
                qi = i1 // 4
                if qi + 2 < len(quads):
                    load_quad(qi + 2)
                stats_quad(qi)
            a_s1(i1)
        if 0 <= it + 1 < ntA:
            a_s2(it + 1)
        if 0 <= it:
            a_s3(it)
            if it == 31 and "quads" not in dbg:
                for c_ in range(4):
                    S.coll(lambda e, c_=c_: e.collective_compute("AllGather", ALU.bypass, replica_groups=[[0, 1, 2, 3], [4, 5, 6, 7]], ins=[u_loc2[c_]], outs=[u_s2[c_]]),
                           "ccag%d" % c_, reads=b_uloc[c_], writes=b_us[c_])

    S.barrier()
    AR.off = base_mark
    if dbg.get("stop") == "A":
        S.emit()
        return nc

    bf_t = AR.alloc([128, 512], F32)
    bd_t = AR.alloc([128, 2, 128], BF16)
    wfb = AR.alloc([128, 4, 512], BF16)
    Wcs = AR.alloc([128, 4, 2, 512], BF16)
    cs_s = AR.alloc([128, 2, 2, 64], BF16)
    cs_p = AR.alloc([128, 2, 16], BF16)
    Xt = AR.alloc([128, 12288], BF16)
    x_s = Xt.rearrange("p (a b c) -> p a b c", a=128, b=3)
    x_p = Xt[:, 0:6144].rearrange("p (a b c) -> p a b c", a=16, b=3)
    Ut = AR.alloc([128, 128, 128], BF16)
    Tt = AR.alloc([128, 128, 2, 64], BF16)
    ZT = AR.alloc([128, 4, 2, 4096], BF16)
    wfs = ZT.rearrange("p c r t -> p (c r t)")[:, 0:4096].bitcast(F32).rearrange("p (c n) -> p c n", c=4)
    yft = [AR.alloc([128, 512], F32) for _ in range(2)]
    b_U, b_T, b_ZT = Buf(), Buf(), Buf()
    b_yft = [Buf(), Buf()]
    b_yf = [Buf() for _ in range(64)]
    b_tab, b_X, b_wfb, b_Wcs = Buf(), Buf(), Buf(), Buf()
    b_wfs = b_ZT
    for dst, srcd in ((bf_t, bfb), (bd_t, bd_d), (cs_s, cs_s_d), (cs_p, cs_p_d)):
        S.dma("sp", lambda e, dst=dst, srcd=srcd: e.dma_start(out=dst, in_=srcd), "ld0", writes=[b_tab])
    S.dma("sp", lambda e: e.dma_start(out=x_p, in_=x_p_d), "ldX", writes=[b_X])
    S.dma("sp", lambda e: e.dma_start(out=wfs, in_=w_f.rearrange("(c p) n -> p c n", p=128)), "ld0", writes=[b_wfs])
    S.op("dve", lambda e: e.tensor_copy(out=wfb.rearrange("p a b -> p (a b)"), in_=wfs.rearrange("p a b -> p (a b)")), reads=[b_wfs], writes=[b_wfb])
    for cb in range(4):
        for ri in range(2):
            p = (cb * 2 + ri) % 2
            S.op("pe", lambda e, cb=cb, ri=ri, p=p: e.matmul(out=pA[:, p * 512:(p + 1) * 512], lhsT=bd_t[:, ri, :], rhs=wfb[:, cb, :], start=True, stop=True),
                 reads=[b_tab, b_wfb], writes=[b_pA[p]])
            S.op("act", lambda e, cb=cb, ri=ri, p=p: e.activation(out=Wcs[:, cb, ri, :], in_=pA[:, p * 512:(p + 1) * 512], func=AF.Copy),
                 reads=[b_pA[p]], writes=[b_Wcs])

    evac_rr = [0]

    def evac(out, in_, reads, writes):
        evac_rr[0] ^= 1
        if evac_rr[0]:
            S.op("act", lambda e: e.activation(out=out, in_=in_, func=AF.Copy), reads=reads, writes=writes)
        else:
            S.op("dve", lambda e: e.tensor_copy(out=out, in_=in_), reads=reads, writes=writes)

    def fourier(A, halves, nk1h, NK2, udram, b_udram, CS, X, ntok, yf_row0, ssf_col0):
        Tv = Tt if nk1h == 64 else Tt.rearrange("p c r k -> p (c r k)")[:, 0:2 * nk1h * 128].rearrange("p (c r k) -> p c r k", c=128, r=2)
        S.barrier(skip=("ccag",))
        b_Tg = [Buf() for _ in range(128 // (512 // (2 * nk1h)))]
        b_ZTg = {}
        ZTv = ZT if ntok == 4096 else ZT.rearrange("p c r t -> p (c r t)")[:, 0:8 * ntok].rearrange("p (c r t) -> p c r t", c=4, r=2)
        w1 = 2 * nk1h
        G1 = 512 // w1
        w2 = 2 * NK2
        G2 = 512 // w2
        pb = 0
        for cb in range(4):
            S.dma("sp", lambda e, cb=cb: e.dma_start(out=Ut[0:A], in_=(udram[cb] if udram is not None else u_s2[cb]).rearrange("(a b) k -> a b k", b=128)), "ldU",
                  reads=b_udram[cb], writes=[b_U])
            for hf in range(halves):
                for g in range(128 // G1):
                    pb ^= 1
                    ch0 = g * G1
                    S.op("pe", [(lambda e, i=i, ch0=ch0, hf=hf, pb=pb: e.matmul(out=pA[:, pb * 512 + i * w1: pb * 512 + (i + 1) * w1], lhsT=Ut[0:A, :, ch0 + i],
                                                                            rhs=(CS[0:A, hf].rearrange("p r k -> p (r k)") if halves == 2 else CS[0:A].rearrange("p r k -> p (r k)")),
                                                                            start=True, stop=True)) for i in range(G1)],
                         reads=[b_U, b_tab], writes=[b_pA[pb]])
                    evac(Tv[:, ch0:ch0 + G1, :, :],
                         pA[:, pb * 512:(pb + 1) * 512].rearrange("p (c r k) -> p c r k", c=G1, r=2), [b_pA[pb]], [b_Tg[g]])
                for g in range(nk1h // G2):
                    pb ^= 1
                    fl = []
                    for i in range(G2):
                        k1l = g * G2 + i
                        k1a = hf * nk1h + k1l
                        o = pA[:, pb * 512 + i * w2: pb * 512 + (i + 1) * w2]
                        fl.append(lambda e, o=o, k1l=k1l, k1a=k1a: e.matmul(out=o, lhsT=Tv[:, :, 1, k1l], rhs=X[:, k1a, 0:2, :].rearrange("p r k -> p (r k)"), start=True, stop=False))
                        fl.append(lambda e, o=o, k1l=k1l, k1a=k1a: e.matmul(out=o, lhsT=Tv[:, :, 0, k1l], rhs=X[:, k1a, 1:3, :].rearrange("p r k -> p (r k)"), start=False, stop=True))
                    S.op("pe", fl, reads=b_Tg + [b_X], writes=[b_pA[pb]])
                    k1lo = hf * nk1h + g * G2
                    evac((ZTv[:, cb].rearrange("p r (k1 k2) -> p k1 r k2", k1=A) if A == 128 else ZTv[:, cb].rearrange("p r (k2 k1) -> p k1 r k2", k1=A))[:, k1lo:k1lo + G2],
                         pA[:, pb * 512:(pb + 1) * 512].rearrange("p (k1 r k2) -> p k1 r k2", k1=G2, r=2), [b_pA[pb]], [b_ZTg.setdefault((cb, hf, g), Buf())])
        for tt in range(ntok // 128):
            pb ^= 1
            fl = []
            for cb in range(4):
                for ri in range(2):
                    fl.append(lambda e, cb=cb, ri=ri, pb=pb, tt=tt: e.matmul(out=pA[:, pb * 512:(pb + 1) * 512], lhsT=(ZTv[:, cb, ri].rearrange("p (k1 k2) -> p k1 k2", k1=A)[:, :, tt] if A == 128 else ZTv[:, cb, ri, tt * 128:(tt + 1) * 128]), rhs=Wcs[:, cb, ri, :],
                                                                       start=(cb == 0 and ri == 0), stop=(cb == 3 and ri == 1)))
            S.op("pe", fl, reads=[b_ZT, b_Wcs] + list(b_ZTg.values()), writes=[b_pA[pb]])
            sl = tt % 2
            S.op("dve", lambda e, pb=pb, sl=sl: e.tensor_tensor(out=yft[sl], in0=pA[:, pb * 512:(pb + 1) * 512], in1=bf_t, op=ALU.add),
                 reads=[b_pA[pb], b_tab], writes=[b_yft[sl]])
            col = ssf_col0 + tt
            S.op("act", lambda e, sl=sl, col=col: e.activation(out=junkF, in_=yft[sl], func=AF.Square, scale=float(1.0 / np.sqrt(512.0)), accum_out=ssf[:, col:col + 1]),
                 reads=[b_yft[sl]], writes=[b_junk, b_ssf[col]])
            r0 = yf_row0 + tt * 128
            S.dma("sp", lambda e, sl=sl, r0=r0: e.dma_start(out=yf_all[r0:r0 + 128, :], in_=yft[sl]), "styf%d" % sl, reads=[b_yft[sl]], writes=[b_yf[r0 // 128]])

    junkF = AR.alloc([128, 512], F32)
    fsel = dbg.get("fourier", [0, 1, 2])
    if 0 in fsel:
        fourier(16, 1, 16, 128, u_p[0], b_up[0], cs_p, x_p, 2048, 0, 0)
    if 1 in fsel:
        fourier(16, 1, 16, 128, u_p[1], b_up[1], cs_p, x_p, 2048, 2048, 16)
    if 2 in fsel:
        S.dma("sp", lambda e: e.dma_start(out=x_s, in_=x_s_d), "ldX", writes=[b_X])
        fourier(128, 2, 64, 32, None, b_us, cs_s, x_s, 4096, 4096, 32)
    S.op("act", lambda e: e.activation(out=sqf, in_=ssf, func=AF.Sqrt, bias=EPS, scale=1.0), reads=b_ssf, writes=[b_rstdf])
    S.op("dve", lambda e: e.reciprocal(out=rstd_f, in_=sqf), reads=[b_rstdf], writes=[b_rstdf])

    S.barrier()
    AR.off = base_mark
    if dbg.get("stop") == "F":
        S.emit()
        return nc

    W_r = AR.alloc([128, 8, 2560], BF16)
    W_o = AR.alloc([128, 8, D], BF16)
    E = AR.alloc([128, 7, 8, 128], BF16)
    EBn = AR.alloc([128, 5, 8, 128], BF16)
    mskn = AR.alloc([128, 5, 128], BF16)
    msk = [AR.alloc([128, 6, 128], BF16) for _ in range(2)]
    gpost_t = AR.alloc([128, D], F32)
    prep_mark = AR.off
    wstB = [AR.alloc([128, 3072], F32) for _ in range(2)]
    b_wstB = [Buf(), Buf()]
    b_Wr, b_Wo = Buf(), Buf()
    for kc in range(8):
        sl = kc % 2
        S.dma("sp", lambda e, kc=kc, sl=sl: e.dma_start(out=wstB[sl], in_=w_in[kc * 128:(kc + 1) * 128, :]), "ldw%d" % sl, writes=[b_wstB[sl]])
        S.op("act", lambda e, kc=kc, sl=sl: e.activation(out=W_r[:, kc, 0:2048], in_=wstB[sl][:, 0:2048], func=AF.Copy, scale=gpre_t[:, kc:kc + 1]),
             reads=[b_wstB[sl], b_g], writes=[b_Wr])
        S.op("dve", lambda e, kc=kc, sl=sl: e.tensor_scalar(out=W_r[:, kc, 2048:2560], in0=wstB[sl][:, 2560:3072], scalar1=gpre_t[:, kc:kc + 1], scalar2=None, op0=ALU.mult),
             reads=[b_wstB[sl], b_g], writes=[b_Wr])
    for kc in range(8):
        sl = kc % 2
        S.dma("sp", lambda e, kc=kc, sl=sl: e.dma_start(out=wstB[sl][:, 0:D], in_=w_out[kc * 128:(kc + 1) * 128, :]), "ldw%d" % sl, writes=[b_wstB[sl]])
        S.op("dve", lambda e, kc=kc, sl=sl: e.tensor_scalar(out=W_o[:, kc, :], in0=wstB[sl][:, 0:D], scalar1=gcat_t[:, kc:kc + 1], scalar2=0.5, op0=ALU.mult, op1=ALU.mult),
             reads=[b_wstB[sl], b_g], writes=[b_Wo])
    b_E, b_EBn, b_mskn, b_msk, b_gpost = Buf(), Buf(), Buf(), [Buf(), Buf()], Buf()
    S.dma("sp", lambda e: e.dma_start(out=mskn, in_=masks_d[:, 0:5, :]), "ld0", writes=[b_mskn])
    S.dma("sp", lambda e: e.dma_start(out=gpost_t, in_=gpost), "ld0", writes=[b_gpost])
    for c in range(7):
        sl = c % 2
        st_ = wstB[sl][:, 0:1024].rearrange("p (h q) -> p h q", h=8)
        S.dma("sp", lambda e, c=c, st_=st_: e.dma_start(out=st_, in_=biasB[:, c]), "ldw%d" % sl, writes=[b_wstB[sl]])
        S.op("act", lambda e, c=c, st_=st_: e.activation(out=E[:, c], in_=st_, func=AF.Exp), reads=[b_wstB[sl]], writes=[b_E])
    for c in range(5):
        S.op("dve", lambda e, c=c: e.tensor_tensor(out=EBn[:, c], in0=E[:, c + 1], in1=mskn[:, c:c + 1, :].to_broadcast([128, 8, 128]), op=ALU.mult),
             reads=[b_E, b_mskn], writes=[b_EBn])

    S.barrier()
    AR.off = prep_mark
    NXR = 3
    xr = [AR.alloc([128, D], F32) for _ in range(NXR)]
    b_xr = [Buf() for _ in range(NXR)]
    xres = [AR.alloc([128, D], F32) for _ in range(2)]
    b_xres = [Buf(), Buf()]
    hbB = AR.alloc([128, D], BF16)
    hTB = AR.alloc([128, 8, 128], BF16)
    qk_tm = AR.alloc([128, D], BF16)
    NQ = 6
    qT = [AR.alloc([128, 4, 128], BF16) for _ in range(NQ)]
    NR = 8
    kT = AR.alloc([128, 4, NR, 128], BF16)
    V1 = AR.alloc([128, NR, 8, 65], BF16)
    NG = 7
    gz = [AR.alloc([128, D], BF16) for _ in range(NG)]
    th = AR.alloc([128, 512], F32)
    Pexp = [AR.alloc([128, 6, 128], BF16) for _ in range(2)]
    Pm = [AR.alloc([128, 6, 128], BF16) for _ in range(2)]
    rden = [AR.alloc([128, 8], F32) for _ in range(2)]
    ya = [AR.alloc([128, 8, 64], F32) for _ in range(2)]
    mixed = AR.alloc([128, D], BF16)
    mT = AR.alloc([128, 8, 128], BF16)
    NO = 2
    o_sb = [AR.alloc([128, D], F32) for _ in range(NO)]
    yfr = [AR.alloc([128, 512], F32) for _ in range(2)]
    res1 = AR.alloc([128, D], F32)
    b_res1 = Buf()
    NST = 4
    stB = [AR.alloc([128, 4], F32) for _ in range(NST)]
    sqB = [AR.alloc([128, 4], F32) for _ in range(NST)]
    rsB = [AR.alloc([128, 4], F32) for _ in range(NST)]
    junkB = AR.alloc([128, D], BF16)
    b_hbB, b_hTB, b_qktm, b_th = Buf(), Buf(), Buf(), Buf()
    b_qT = [Buf() for _ in range(NQ)]
    b_kT = [Buf() for _ in range(NR)]
    b_V1 = [Buf() for _ in range(NR)]
    b_gz = [Buf() for _ in range(NG)]
    b_Pexp, b_Pm = [Buf(), Buf()], [Buf(), Buf()]
    b_rden, b_ya = [Buf(), Buf()], [Buf(), Buf()]
    b_mixed, b_mT = Buf(), Buf()
    b_osb = [Buf() for _ in range(NO)]
    b_yfr = [Buf(), Buf()]
    b_stB = [[Buf() for _ in range(4)] for _ in range(NST)]
    b_sqB = [Buf() for _ in range(NST)]
    b_rsB = [Buf() for _ in range(NST)]
    b_junkB = Buf()
    b_out = Buf()
    pTi = pT[0]
    b_pTi = b_pT[0]
    pTo = pTi
    b_pTo = b_pTi
    pO = bank(7)
    b_pO = Buf()
    gctr = [0]

    S.op("dve", lambda e: e.memset(V1.rearrange("p r h d -> p (r h d)"), 1.0), writes=b_V1)
    for i in range(NST):
        S.op("pool", lambda e, i=i: e.memset(stB[i], 1.0), writes=b_stB[i])

    segs = []
    for sq_ in range(2):
        segs.append(dict(kind="p", n=16, lo=0, hi=15, x=xp[sq_ * SP_LEN:(sq_ + 1) * SP_LEN, :], xoff=0,
                         out=yp[sq_ * SP_LEN:(sq_ + 1) * SP_LEN, :], yf0=sq_ * 16))
    segs.append(dict(kind="s", n=32, lo=-HALO, hi=31 + HALO, x=xh, xoff=HALO, out=ys, yf0=32))
    if "segs" in dbg:
        segs = [segs[i] for i in dbg["segs"]]

    def xtile(seg, t):
        r0 = (t + seg["xoff"]) * 128
        return seg["x"][r0:r0 + 128, :]

    def interleave(gens):
        gens = [g for g in gens if g is not None]
        while gens:
            for g in list(gens):
                try:
                    next(g)
                except StopIteration:
                    gens.remove(g)

    def run_segment(seg):
        n, lo, hi, kind = seg["n"], seg["lo"], seg["hi"], seg["kind"]

        def slot(t):
            return t % NR

        def chunks_of(t):
            key = (kind, t)
            if key in SPECIAL:
                return SPECIAL[key], key
            return NORMAL_CH, None

        def load_x(t):
            sl = t % NXR
            S.dma("sp", lambda e, sl=sl, t=t: e.dma_start(out=xr[sl], in_=xtile(seg, t)), "ldx%d" % sl, writes=[b_xr[sl]])

        def sq_pre(t, stslot, col):
            sl = t % NXR
            S.op("act", lambda e: e.activation(out=junkB, in_=xr[sl], func=AF.Square, scale=1.0 / 32, accum_out=stB[stslot][:, col:col + 1]),
                 reads=[b_xr[sl]], writes=[b_junkB, b_stB[stslot][col]])

        def sqrt_batch(stslot):
            S.op("act", lambda e: e.activation(out=sqB[stslot], in_=stB[stslot], func=AF.Sqrt, bias=EPS, scale=1.0), reads=b_stB[stslot], writes=[b_sqB[stslot]])
            S.op("dve", lambda e: e.reciprocal(out=rsB[stslot], in_=sqB[stslot]), reads=[b_sqB[stslot]], writes=[b_rsB[stslot]])

        def stage_in(t, rs_slot):
            kv_only = (t < 0 or t >= n)
            sl = t % NXR
            ks = slot(t)
            S.op("dve", lambda e: e.tensor_scalar(out=hbB, in0=xr[sl], scalar1=rsB[rs_slot][:, 2:3], scalar2=None, op0=ALU.mult),
                 reads=[b_xr[sl], b_rsB[rs_slot]], writes=[b_hbB])
            yield
            S.op("pe", [(lambda e, c=c: e.transpose(out=pTi[:, c, :], in_=hbB[:, c * 128:(c + 1) * 128], identity=ident)) for c in range(8)],
                 reads=[b_hbB, b_ident], writes=[b_pTi])
            S.op("act", lambda e: e.activation(out=hTB, in_=pTi, func=AF.Copy), reads=[b_pTi], writes=[b_hTB])
            yield
            cgs = [1, 2] if kv_only else [1, 0, 2, 3, 4]
            for cgi, cg in enumerate(cgs):
                p = cgi % 2
                pa = pA[:, p * 512:(p + 1) * 512]
                S.op("pe", [(lambda e, c=c, cg=cg, pa=pa: e.matmul(out=pa, lhsT=hTB[:, c, :], rhs=W_r[:, c, cg * 512:(cg + 1) * 512], start=(c == 0), stop=(c == 7))) for c in range(8)],
                     reads=[b_hTB, b_Wr], writes=[b_pA[p]])
                if cg == 0:
                    S.op("act", lambda e, pa=pa: e.activation(out=qk_tm[:, 0:512], in_=pa, func=AF.Copy), reads=[b_pA[p]], writes=[b_qktm])
                elif cg == 1:
                    S.op("dve", lambda e, pa=pa: e.tensor_copy(out=qk_tm[:, 512:1024], in_=pa), reads=[b_pA[p]], writes=[b_qktm])
                elif cg == 2:
                    S.op("dve", lambda e, pa=pa: e.tensor_copy(out=V1[:, ks, :, 0:64], in_=pa.rearrange("p (h d) -> p h d", h=8)), reads=[b_pA[p]], writes=[b_V1[ks]])
                else:
                    go = (cg - 3) * 512
                    g = gz[t % NG]
                    S.op("act", lambda e, pa=pa: e.activation(out=th, in_=pa, func=AF.Tanh, scale=0.5), reads=[b_pA[p]], writes=[b_th])
                    S.op("dve", lambda e, pa=pa, g=g, go=go: e.scalar_tensor_tensor(out=g[:, go:go + 512], in0=th, scalar=1.0, in1=pa, op0=ALU.add, op1=ALU.mult),
                         reads=[b_th, b_pA[p]], writes=[b_gz[t % NG]])
                yield
                if cg == 0 or (kv_only and cg == 1):
                    c0 = 4 if kv_only else 0
                    S.op("pe", [(lambda e, c=c: e.transpose(out=pTi[:, c, :], in_=qk_tm[:, c * 128:(c + 1) * 128], identity=ident)) for c in range(c0, 8)],
                         reads=[b_qktm, b_ident], writes=[b_pTi])
                    if not kv_only:
                        S.op("act", lambda e: e.activation(out=qT[t % NQ], in_=pTi[:, 0:4, :], func=AF.Copy), reads=[b_pTi], writes=[b_qT[t % NQ]])
                    S.op("act", lambda e: e.activation(out=kT[:, :, ks, :], in_=pTi[:, 4:8, :], func=AF.Copy), reads=[b_pTi], writes=[b_kT[ks]])
                    yield

        def stage_att(t, stslot):
            chs, key = chunks_of(t)
            chs = [c for c in chs if lo <= t + c <= hi][:dbg.get("maxch", 6)]
            nch = len(chs)
            ms = 0
            if key is not None:
                full = SPECIAL[key]
                ms = (gctr[0]) % 2
                gctr[0] += 1
                i0 = full.index(chs[0])
                S.dma("sp", lambda e: e.dma_start(out=msk[ms][:, 0:nch, :], in_=masks_d[:, MSLOT[key] + i0: MSLOT[key] + i0 + nch, :]), "ldm%d" % ms, writes=[b_msk[ms]])
            yb = t % 2

            def qk(h):
                hp, po, sb = h // 2, 64 * (h % 2), h % 2
                S.op("pe", [(lambda e, ci=ci, c=c: e.matmul(out=pS[:, (sb * 8 + ci) * 128:(sb * 8 + ci + 1) * 128], lhsT=kT[po:po + 64, hp, slot(t + c), :],
                                                          rhs=qT[t % NQ][po:po + 64, hp, :], start=True, stop=True)) for ci, c in enumerate(chs)],
                     reads=[b_qT[t % NQ]] + [b_kT[slot(t + c)] for c in chs], writes=[b_pSb[sb]])

            def sm(h):
                sb = h % 2
                pe_, pm_ = Pexp[sb], Pm[sb]
                S.op("act", lambda e: e.activation(out=pe_[:, 0:nch, :].rearrange("p a b -> p (a b)"), in_=pS[:, sb * 1024: sb * 1024 + nch * 128], func=AF.Exp, scale=0.125),
                     reads=[b_pSb[sb]], writes=[b_Pexp[sb]])
                yield
                if key is None:
                    S.op("dve", lambda e: e.tensor_tensor(out=pm_[:, 0:5, :], in0=pe_[:, 0:5, :], in1=EBn[:, :, h, :], op=ALU.mult),
                         reads=[b_Pexp[sb], b_EBn], writes=[b_Pm[sb]])
                else:
                    e0 = chs[0] + 3
                    S.op("dve", lambda e: e.tensor_tensor(out=pm_[:, 0:nch, :], in0=pe_[:, 0:nch, :], in1=E[:, e0:e0 + nch, h, :], op=ALU.mult),
                         reads=[b_Pexp[sb], b_E], writes=[b_Pm[sb]])
                    yield
                    S.op("dve", lambda e: e.tensor_tensor(out=pm_[:, 0:nch, :], in0=pm_[:, 0:nch, :], in1=msk[ms][:, 0:nch, :], op=ALU.mult),
                         reads=[b_Pm[sb], b_msk[ms]], writes=[b_Pm[sb]])
                yield

            def pv(h):
                sb = h % 2
                pm_ = Pm[sb]
                ob = pO[:, (h % 4) * 65:(h % 4) * 65 + 65]
                S.op("pe", [(lambda e, ci=ci, c=c: e.matmul(out=ob, lhsT=pm_[:, ci, :], rhs=V1[:, slot(t + c), h, :], start=(ci == 0), stop=(ci == nch - 1)))
                            for ci, c in enumerate(chs)],
                     reads=[b_Pm[sb]] + [b_V1[slot(t + c)] for c in chs], writes=[b_pO])

            def norm(hb2):
                pv_ = pO[:, 0:260].rearrange("p (h d) -> p h d", d=65)
                S.op("dve", lambda e: e.reciprocal(out=rden[yb][:, hb2 * 4:(hb2 + 1) * 4], in_=pv_[:, :, 64]), reads=[b_pO], writes=[b_rden[yb]])
                yield
                S.op("dve", lambda e: e.tensor_tensor(out=ya[yb][:, hb2 * 4:(hb2 + 1) * 4, :], in0=pv_[:, :, 0:64],
                                                      in1=rden[yb][:, hb2 * 4:(hb2 + 1) * 4].unsqueeze(2).to_broadcast([128, 4, 64]), op=ALU.mult),
                     reads=[b_pO, b_rden[yb]], writes=[b_ya[yb]])
                yield

            qk(0)
            yield
            for h in range(8):
                if h + 1 < 8:
                    qk(h + 1)
                    yield
                yield from sm(h)
                pv(h)
                yield
                if h % 4 == 3:
                    yield from norm(h // 4)
            S.op("act", lambda e: e.activation(out=junkB[:, 0:512], in_=ya[yb].rearrange("p h d -> p (h d)"), func=AF.Square, scale=float(1.0 / np.sqrt(512.0)),
                                               accum_out=stB[stslot][:, 0:1]), reads=[b_ya[yb]], writes=[b_junkB, b_stB[stslot][0]])
            yield

        def prefetch_out(t):
            yfs = t % 2
            r0 = (seg["yf0"] + t) * 128
            S.dma("sp", lambda e: e.dma_start(out=yfr[yfs], in_=yf_all[r0:r0 + 128, :]), "ldyf%d" % yfs, reads=[b_yf[seg["yf0"] + t]], writes=[b_yfr[yfs]])

        def prefetch_final(t):
            xs_ = t % 2
            S.dma("sp", lambda e: e.dma_start(out=xres[xs_], in_=xtile(seg, t)), "ldxr%d" % xs_, writes=[b_xres[xs_]])

        def stage_out(t, rs_slot, stslot):
            g = gz[t % NG]
            yfs = t % 2
            yb = t % 2
            S.op("dve", lambda e: e.scalar_tensor_tensor(out=mixed[:, 0:512], in0=ya[yb].rearrange("p h d -> p (h d)"), scalar=rsB[rs_slot][:, 0:1], in1=g[:, 0:512], op0=ALU.mult, op1=ALU.mult),
                 reads=[b_ya[yb], b_rsB[rs_slot], b_gz[t % NG]], writes=[b_mixed])
            yield
            col = seg["yf0"] + t
            S.op("dve", lambda e: e.scalar_tensor_tensor(out=mixed[:, 512:1024], in0=yfr[yfs], scalar=rstd_f[:, col:col + 1], in1=g[:, 512:1024], op0=ALU.mult, op1=ALU.mult),
                 reads=[b_yfr[yfs], b_rstdf, b_gz[t % NG]], writes=[b_mixed])
            yield
            S.op("pe", [(lambda e, c=c: e.transpose(out=pTo[:, c, :], in_=mixed[:, c * 128:(c + 1) * 128], identity=ident)) for c in range(8)],
                 reads=[b_mixed, b_ident], writes=[b_pTo])
            S.op("act", lambda e: e.activation(out=mT, in_=pTo, func=AF.Copy), reads=[b_pTo], writes=[b_mT])
            yield
            for hf in range(2):
                S.op("pe", [(lambda e, c=c, hf=hf: e.matmul(out=pA[:, hf * 512:(hf + 1) * 512], lhsT=mT[:, c, :], rhs=W_o[:, c, hf * 512:(hf + 1) * 512], start=(c == 0), stop=(c == 7))) for c in range(8)],
                     reads=[b_mT, b_Wo], writes=[b_pA[hf]])
            os_ = t % NO
            S.op("act", lambda e: e.activation(out=junkB, in_=pA, func=AF.Square, scale=1.0 / 32, accum_out=stB[stslot][:, 1:2]),
                 reads=[b_pA[0], b_pA[1]], writes=[b_junkB, b_stB[stslot][1]])
            S.op("act", lambda e: e.activation(out=o_sb[os_], in_=pA, func=AF.Copy), reads=[b_pA[0], b_pA[1]], writes=[b_osb[os_]])
            yield

        def stage_final(t, rs_slot):
            os_ = t % NO
            xs_ = t % 2
            fp = dbg.get("fparts", 7)
            if fp & 2:
                S.op("dve", lambda e: e.scalar_tensor_tensor(out=res1, in0=o_sb[os_], scalar=rsB[rs_slot][:, 1:2], in1=gpost_t, op0=ALU.mult, op1=ALU.mult),
                     reads=[b_osb[os_], b_rsB[rs_slot], b_gpost], writes=[b_res1])
                yield
                S.op("dve", lambda e: e.tensor_tensor(out=xres[xs_], in0=res1, in1=xres[xs_], op=ALU.add), reads=[b_res1, b_xres[xs_]], writes=[b_xres[xs_]])
                yield
            r0 = t * 128
            if fp & 4:
                S.dma("pool", lambda e: e.dma_start(out=seg["out"][r0:r0 + 128, :], in_=xres[xs_]), "sto%d" % xs_, reads=[b_xres[xs_]], writes=[Buf()])
            yield

        LAG = 4
        tiles_in = list(range(lo, hi + 1))
        own = lambda t: (t is not None and 0 <= t < n)
        load_x(tiles_in[0])
        load_x(tiles_in[1])
        prev = gst[0] % NST
        gst[0] += 1
        sq_pre(tiles_in[0], prev, 2)
        sqrt_batch(prev)
        nsteps = len(tiles_in) + LAG + 2
        seq = tiles_in + [None] * (LAG + 2)
        for step in range(min(nsteps, dbg.get("max_steps", 10 ** 9))):
            cur = gst[0] % NST
            gst[0] += 1
            tin = seq[step]
            ta = seq[step - LAG] if step - LAG >= 0 else None
            to = seq[step - LAG - 1] if step - LAG - 1 >= 0 else None
            tf = seq[step - LAG - 2] if step - LAG - 2 >= 0 else None
            if tin is not None and tin + 2 <= hi:
                load_x(tin + 2)
            if own(to):
                prefetch_out(to)
            if own(tf) and (dbg.get("fparts", 7) & 1):
                prefetch_final(tf)
            gens = []
            stg = dbg.get("stages", "IAOF")
            if tin is not None and "I" in stg:
                gens.append(stage_in(tin, prev))
            if own(ta) and "A" in stg:
                gens.append(stage_att(ta, cur))
            if own(to) and "O" in stg:
                gens.append(stage_out(to, prev, cur))
            if own(tf) and "F" in stg:
                gens.append(stage_final(tf, prev))
            if dbg.get("serial"):
                for g_ in gens:
                    interleave([g_])
            else:
                interleave(gens)
            if tin is not None and tin + 1 <= hi:
                sq_pre(tin + 1, cur, 2)
            sqrt_batch(cur)
            prev = cur

    for seg_ in segs:
        run_segment(seg_)

    S.barrier()
    S.emit()
    return nc


_CACHE = {}


def kernel(x_prompt, x_sample, w_in, rpb, w_fourier, b_fourier, g_pre, g_na, g_f, w_out, g_post):
    f32 = np.float32
    x_prompt = np.asarray(x_prompt, f32)
    x_sample = np.asarray(x_sample, f32)
    w_in = np.asarray(w_in, f32)[0]
    w_out = np.asarray(w_out, f32)[0]
    w_f = np.asarray(w_fourier, f32)[0]
    rpb = np.asarray(rpb, f32)[0]
    bfb = np.ascontiguousarray(np.broadcast_to(np.asarray(b_fourier, f32)[0][None, :], (128, 512)))
    gpre = np.ascontiguousarray(np.asarray(g_pre, f32)[0].reshape(8, 128).T)
    gcat = np.ascontiguousarray(np.concatenate([np.asarray(g_na, f32)[0], np.asarray(g_f, f32)[0]]).reshape(8, 128).T)
    gpost = np.ascontiguousarray(np.broadcast_to(np.asarray(g_post, f32)[0][None, :], (128, D)))
    ri, ci = _bias_index()
    biasB = np.ascontiguousarray(rpb[:, ri, ci].transpose(1, 2, 0, 3))
    if "nc" not in _CACHE:
        _CACHE["nc"] = build_program(None)
    nc = _CACHE["nc"]
    in_maps = []
    for c in range(NCORES):
        b, j = c // 4, c % 4
        tb = _host_tables(c)
        xh = np.zeros(((32 + 2 * HALO) * 128, D), f32)
        lo_t = 4096 * j - HALO * 128
        hi_t = 4096 * (j + 1) + HALO * 128
        s0, s1 = max(lo_t, 0), min(hi_t, SS_LEN)
        xh[s0 - lo_t:s1 - lo_t] = x_sample[b, s0:s1]
        m = dict(xp=np.ascontiguousarray(x_prompt[2 * c:2 * c + 2].reshape(2 * SP_LEN, D)), xh=xh,
                 w_in=w_in, w_out=w_out, w_f=w_f, bfb=bfb, gpre=gpre, gcat=gcat, gpost=gpost, biasB=biasB)
        m.update(tb)
        in_maps.append(m)
    r = run_bass_kernel_spmd(nc, in_maps, core_ids=list(range(NCORES)))
    y_prompt = np.zeros((16, SP_LEN, D), f32)
    y_sample = np.zeros((2, SS_LEN, D), f32)
    for c in range(NCORES):
        b, j = c // 4, c % 4
        y_prompt[2 * c:2 * c + 2] = np.asarray(r.results[c]["yp"], f32).reshape(2, SP_LEN, D)
        y_sample[b, 4096 * j:4096 * (j + 1)] = np.asarray(r.results[c]["ys"], f32)
    return (y_prompt, y_sample)
```
